# Optimizing a Trainium2 kernel written in Bass

```python
import math
import jax
import jax.numpy as jnp
from jax import lax
import numpy as np

D_MODEL = 1024
BATCH = 16
SEQ = 256
DEPTH = 4
DEC_BATCH = 2
DEC_SEQ = 4096
PAST_LEN = 512

GRID_W = 64
N_AB = (DEPTH + 1) // 2
N_HY = DEPTH // 2
CHUNK = 64
EPS = 1e-6
A_HEADS = 4
A_DK = 128
A_DV = 128
A_QK_W = A_HEADS * A_DK
A_V_W = A_HEADS * A_DV
A_CONV_W = 2 * A_QK_W + A_V_W
SHORT_K = 3
B_HEADS = 4
B_DK = 128
B_DV = 128
B_QK_W = B_HEADS * B_DK
B_V_W = B_HEADS * B_DV
MIX_W = A_V_W + B_V_W
AB_IN = A_CONV_W + A_V_W + 4 * A_HEADS + 2 * B_QK_W + 2 * B_V_W
HY_SHORT_K = 3
HY_EMB = 33
HY_BANDS = (HY_EMB - 1) // 2
HY_FW = 64
HY_TARGET = 1e-2
HY_FAST_PCT = 0.3
HY_SLOW_PCT = 1.5
D_FF = 2816
FFN_K = 3

kernel_name = "hybrid_flow_trunk_prefix_ctx"


def rmsnorm(x, g):
    xf = x.astype(jnp.float32)
    y = xf * lax.rsqrt(jnp.mean(xf * xf, axis=-1, keepdims=True) + EPS)
    return (y * g.astype(jnp.float32)).astype(x.dtype)


def head_groupnorm(x, g):
    xf = x.astype(jnp.float32)
    mu = jnp.mean(xf, axis=-1, keepdims=True)
    var = jnp.mean(jnp.square(xf - mu), axis=-1, keepdims=True)
    return (xf - mu) * lax.rsqrt(var + EPS) * g.astype(jnp.float32)


def l2norm(x):
    xf = x.astype(jnp.float32)
    return xf * lax.rsqrt(jnp.sum(xf * xf, axis=-1, keepdims=True) + EPS)


def modulate(h, shift, scale):
    return h * (1.0 + scale) + shift


def split_cols(x, sizes):
    idx = np.cumsum(sizes)[:-1].tolist()
    return jnp.split(x, idx, axis=-1)


def heads(x, n):
    b, l, _ = x.shape
    return x.reshape(b, l, n, -1).transpose(0, 2, 1, 3)


def flip_if(t, rev):
    return jnp.flip(t, axis=2) if rev else t


def dwconv1d(x, w):
    k = w.shape[0]
    p = k // 2
    l = x.shape[1]
    xp = jnp.pad(x, ((0, 0), (p, p), (0, 0)))
    out = xp[:, 0:l] * w[0]
    for i in range(1, k):
        out = out + xp[:, i:i + l] * w[i]
    return out


def dwconv2d_grid(x, w):
    b, l, ch = x.shape
    rows = l // GRID_W
    xg = jnp.pad(x.reshape(b, rows, GRID_W, ch), ((0, 0), (1, 1), (1, 1), (0, 0)))
    out = xg[:, 0:rows, 0:GRID_W] * w[0, 0]
    for i in range(3):
        for j in range(3):
            if i == 0 and j == 0:
                continue
            out = out + xg[:, i:i + rows, j:j + GRID_W] * w[i, j]
    return out.reshape(b, l, ch)


def gated_delta_chunked(q, k, v, g, beta, s0):
    f32 = jnp.float32
    b, h, l, dk = q.shape
    dv = v.shape[-1]
    n = l // CHUNK
    q = q.astype(f32).reshape(b, h, n, CHUNK, dk)
    k = k.astype(f32).reshape(b, h, n, CHUNK, dk)
    v = v.astype(f32).reshape(b, h, n, CHUNK, dv)
    g = g.astype(f32).reshape(b, h, n, CHUNK)
    beta = beta.astype(f32).reshape(b, h, n, CHUNK)
    gc = jnp.cumsum(g, axis=-1)
    tril = jnp.tril(jnp.ones((CHUNK, CHUNK), bool))
    stril = jnp.tril(jnp.ones((CHUNK, CHUNK), bool), -1)
    diff = gc[..., :, None] - gc[..., None, :]
    dmask = jnp.where(tril, jnp.exp(jnp.where(tril, diff, 0.0)), 0.0)
    kb = k * beta[..., None]
    lmat = jnp.where(stril, jnp.einsum('bhncd,bhnsd->bhncs', kb, k) * dmask, 0.0)
    eye = jnp.eye(CHUNK, dtype=f32)
    tmat = lax.linalg.triangular_solve(eye + lmat, jnp.broadcast_to(eye, lmat.shape),
                                       left_side=True, lower=True)
    u = tmat @ (v * beta[..., None])
    w = tmat @ (kb * jnp.exp(gc)[..., None])
    attn = jnp.where(tril, jnp.einsum('bhncd,bhnsd->bhncs', q, k) * dmask, 0.0)
    qd = q * jnp.exp(gc)[..., None]
    kd = k * jnp.exp(gc[..., -1:] - gc)[..., None]
    gl = jnp.exp(gc[..., -1])

    def step(s, xs):
        u_i, w_i, attn_i, qd_i, kd_i, gl_i = xs
        v_new = u_i - w_i @ s
        o_i = qd_i @ s + attn_i @ v_new
        s = s * gl_i[..., None, None] + jnp.swapaxes(kd_i, -1, -2) @ v_new
        return s, o_i

    xs = tuple(jnp.moveaxis(t, 2, 0) for t in (u, w, attn, qd, kd, gl))
    s_fin, o = lax.scan(step, s0.astype(f32), xs)
    o = jnp.moveaxis(o, 0, 2).reshape(b, h, l, dv)
    return o, s_fin


def retention_chunked(q, k, v, log_gamma, r0):
    f32 = jnp.float32
    b, h, l, dk = q.shape
    dv = v.shape[-1]
    n = l // CHUNK
    q = q.astype(f32).reshape(b, h, n, CHUNK, dk)
    k = k.astype(f32).reshape(b, h, n, CHUNK, dk)
    v = v.astype(f32).reshape(b, h, n, CHUNK, dv)
    lg = log_gamma.astype(f32)[:, None]
    idx = jnp.arange(CHUNK, dtype=f32)
    diff = idx[:, None] - idx[None, :]
    dmat = jnp.where(diff >= 0, jnp.exp(jnp.maximum(diff, 0.0)[None] * lg[..., None]), 0.0)
    cross_decay = jnp.exp((idx + 1.0)[None] * lg)
    state_decay = jnp.exp((CHUNK - 1.0 - idx)[None] * lg)
    chunk_decay = jnp.exp(CHUNK * lg[:, 0])
    inner = jnp.einsum('bhncd,bhnsd->bhncs', q, k) * dmat[None, :, None]
    o_inner = jnp.einsum('bhncs,bhnsv->bhncv', inner, v)
    kv = jnp.einsum('bhncd,bhncv->bhndv', k * state_decay[None, :, None, :, None], v)

    def step(r, kv_i):
        return r * chunk_decay[None, :, None, None] + kv_i, r

    r_fin, r_prev = lax.scan(step, r0.astype(f32), jnp.moveaxis(kv, 2, 0))
    r_prev = jnp.moveaxis(r_prev, 0, 2)
    o = o_inner + jnp.einsum('bhncd,bhndv->bhncv', q * cross_decay[None, :, None, :, None], r_prev)
    return o.reshape(b, h, l, dv), r_fin


def mixer_ab(h, w_in, conv_a, a_log, dt_bias, norm_a, norm_b, ret_decay, w_out, s_delta0, s_ret0):
    f32 = jnp.float32
    bsz, l, _ = h.shape
    proj = h @ w_in
    qkv_a, z_a, a_raw, b_raw, q_b, k_b, v_b, g_b = split_cols(
        proj, [A_CONV_W, A_V_W, 2 * A_HEADS, 2 * A_HEADS, B_QK_W, B_QK_W, B_V_W, B_V_W])
    qkv_a = jax.nn.silu(dwconv1d(qkv_a, conv_a))
    q_a, k_a, v_a = split_cols(qkv_a, [A_QK_W, A_QK_W, A_V_W])
    q_a = l2norm(heads(q_a, A_HEADS)) * (A_DK ** -0.5)
    k_a = l2norm(heads(k_a, A_HEADS))
    v_a = heads(v_a, A_HEADS)
    g_a = -jnp.exp(a_log.astype(f32)) * jax.nn.softplus(
        a_raw.astype(f32).reshape(bsz, l, 2, A_HEADS) + dt_bias.astype(f32))
    beta_a = jax.nn.sigmoid(b_raw.astype(f32).reshape(bsz, l, 2, A_HEADS))
    g_a = g_a.transpose(2, 0, 3, 1)
    beta_a = beta_a.transpose(2, 0, 3, 1)
    q_b = heads(q_b, B_HEADS) * (B_DK ** -0.5)
    k_b = heads(k_b, B_HEADS)
    v_b = heads(v_b, B_HEADS)
    log_gamma = -jnp.exp(ret_decay.astype(f32))
    o_a = 0.0
    o_b = 0.0
    fin_a = []
    fin_b = []
    for d in range(2):
        rev = d == 1
        oa, sa = gated_delta_chunked(flip_if(q_a, rev), flip_if(k_a, rev), flip_if(v_a, rev),
                                     flip_if(g_a[d], rev), flip_if(beta_a[d], rev), s_delta0[:, d])
        ob, sb = retention_chunked(flip_if(q_b, rev), flip_if(k_b, rev), flip_if(v_b, rev),
                                   log_gamma[d], s_ret0[:, d])
        o_a = o_a + flip_if(oa, rev)
        o_b = o_b + flip_if(ob, rev)
        fin_a.append(sa)
        fin_b.append(sb)
    z = z_a.astype(f32).reshape(bsz, l, A_HEADS, A_DV)
    o_a = rmsnorm(o_a.transpose(0, 2, 1, 3), norm_a) * jax.nn.silu(z)
    gb = g_b.astype(f32).reshape(bsz, l, B_HEADS, B_DV)
    o_b = head_groupnorm(o_b.transpose(0, 2, 1, 3), norm_b) * jax.nn.silu(gb)
    o = jnp.concatenate([o_a.reshape(bsz, l, A_V_W), o_b.reshape(bsz, l, B_V_W)], axis=-1).astype(h.dtype)
    return o @ w_out, jnp.stack(fin_a, axis=1), jnp.stack(fin_b, axis=1)


def hyena_filter(l, w1, b1, freq1, w2, b2, freq2, w3):
    f32 = jnp.float32
    t = jnp.linspace(0.0, 1.0, l, dtype=f32)[:, None]
    wpos = 2.0 * math.pi * jnp.arange(l, dtype=f32)[:, None] / l
    bands = jnp.linspace(1e-4, HY_BANDS - 1, HY_BANDS, dtype=f32)[None, :]
    z = jnp.concatenate([t, jnp.cos(bands * wpos), -jnp.sin(bands * wpos)], axis=-1)
    hid = jnp.sin(freq1 * (z @ w1 + b1))
    hid = jnp.sin(freq2 * (hid @ w2 + b2))
    filt = (hid @ w3).astype(f32)
    min_decay = math.log(HY_TARGET) / HY_SLOW_PCT
    max_decay = math.log(HY_TARGET) / HY_FAST_PCT
    deltas = jnp.abs(jnp.linspace(min_decay, max_decay, D_MODEL, dtype=f32))
    window = jnp.exp(-t * deltas[None, :])
    return filt[:, :D_MODEL] * window, filt[:, D_MODEL:] * window


def bidir_long_conv(u, h_fwd, h_bwd, bias):
    f32 = jnp.float32
    l = u.shape[1]
    kern = jnp.concatenate([h_fwd, jnp.zeros((1, h_fwd.shape[1]), f32), jnp.flip(h_bwd[1:], axis=0)], axis=0)
    uf = jnp.fft.rfft(u.astype(f32), n=2 * l, axis=1)
    kf = jnp.fft.rfft(kern, axis=0)
    y = jnp.fft.irfft(uf * kf[None], n=2 * l, axis=1)[:, :l]
    return (y + u.astype(f32) * bias.astype(f32)).astype(u.dtype)


def mixer_hyena(h, w_in, b_in, conv_w, conv_b, f_w1, f_b1, f_freq1, f_w2, f_b2, f_freq2, f_w3, f_bias, w_out, b_out):
    l = h.shape[1]
    u = dwconv1d(h @ w_in + b_in, conv_w) + conv_b
    x0, x1, v = split_cols(u, [D_MODEL, D_MODEL, D_MODEL])
    h_fwd, h_bwd = hyena_filter(l, f_w1, f_b1, f_freq1, f_w2, f_b2, f_freq2, f_w3)
    v = bidir_long_conv(v * x1, h_fwd, h_bwd, f_bias)
    return (v * x0) @ w_out + b_out


def conv_ffn(h, w_gate, w_up, w_conv, b_conv, w_down, on_grid):
    gate = h @ w_gate
    up = h @ w_up
    gate = dwconv2d_grid(gate, w_conv) if on_grid else dwconv1d(gate, w_conv[1])
    return (jax.nn.silu(gate + b_conv) * up) @ w_down


def setup_inputs(seed: int = 0) -> dict:
    key = jax.random.key(seed)
    ks = iter(jax.random.split(key, 48))
    f32 = jnp.float32
    D = D_MODEL

    def nrm(shape, scale=1.0):
        return jax.random.normal(next(ks), shape, f32) * scale

    def gain(shape):
        return 1.0 + nrm(shape, 0.05)

    ret_base = jnp.log(-jnp.log(1.0 - 2.0 ** (-5.0 - jnp.arange(B_HEADS, dtype=f32))))
    dt = jnp.exp(jax.random.uniform(next(ks), (N_AB, 2, A_HEADS), f32, math.log(1e-3), math.log(1e-1)))
    a_log = jnp.log(jax.random.uniform(next(ks), (N_AB, 2, A_HEADS), f32, 1.0, 16.0))
    return {
        "x_prompt": nrm((BATCH, SEQ, D)),
        "x_sample": nrm((DEC_BATCH, DEC_SEQ, D)),
        "state_delta": nrm((DEC_BATCH, N_AB, 2, A_HEADS, A_DK, A_DV), 0.3),
        "state_ret": nrm((DEC_BATCH, N_AB, 2, B_HEADS, B_DK, B_DV), 1.0),
        "c": nrm((DEC_BATCH, D)),
        "c_ctx": nrm((D,)),
        "mod_w": nrm((DEPTH, D, 6 * D), 0.5 * D ** -0.5),
        "mod_b": nrm((DEPTH, 6 * D), 0.02),
        "norm1": gain((DEPTH, D)),
        "norm2": gain((DEPTH, D)),
        "ab_w_in": nrm((N_AB, D, AB_IN), D ** -0.5),
        "ab_conv": nrm((N_AB, SHORT_K, A_CONV_W), SHORT_K ** -0.5),
        "ab_a_log": a_log,
        "ab_dt_bias": dt + jnp.log(-jnp.expm1(-dt)),
        "ab_norm_a": gain((N_AB, A_DV)),
        "ab_norm_b": gain((N_AB, B_DV)),
        "ab_ret_decay": ret_base[None, None, :] + nrm((N_AB, 2, B_HEADS), 0.05),
        "ab_w_out": nrm((N_AB, MIX_W, D), MIX_W ** -0.5),
        "hy_w_in": nrm((N_HY, D, 3 * D), D ** -0.5),
        "hy_b_in": nrm((N_HY, 3 * D), 0.02),
        "hy_conv_w": nrm((N_HY, HY_SHORT_K, 3 * D), HY_SHORT_K ** -0.5),
        "hy_conv_b": nrm((N_HY, 3 * D), 0.02),
        "hy_f_w1": nrm((N_HY, HY_EMB, HY_FW), HY_EMB ** -0.5),
        "hy_f_b1": nrm((N_HY, HY_FW), 0.1),
        "hy_f_freq1": gain((N_HY, HY_FW)),
        "hy_f_w2": nrm((N_HY, HY_FW, HY_FW), HY_FW ** -0.5),
        "hy_f_b2": nrm((N_HY, HY_FW), 0.1),
        "hy_f_freq2": gain((N_HY, HY_FW)),
        "hy_f_w3": nrm((N_HY, HY_FW, 2 * D), 0.02),
        "hy_f_bias": nrm((N_HY, D), 0.5),
        "hy_w_out": nrm((N_HY, D, D), D ** -0.5),
        "hy_b_out": nrm((N_HY, D), 0.02),
        "ffn_w_gate": nrm((DEPTH, D, D_FF), D ** -0.5),
        "ffn_w_up": nrm((DEPTH, D, D_FF), D ** -0.5),
        "ffn_conv": nrm((DEPTH, FFN_K, FFN_K, D_FF), 1.0 / FFN_K),
        "ffn_conv_b": nrm((DEPTH, D_FF), 0.02),
        "ffn_w_down": nrm((DEPTH, D_FF, D), D_FF ** -0.5),
        "final_norm": gain((D,)),
    }


def reference(x_prompt, x_sample, state_delta, state_ret, c, c_ctx, mod_w, mod_b, norm1, norm2,
              ab_w_in, ab_conv, ab_a_log, ab_dt_bias, ab_norm_a, ab_norm_b, ab_ret_decay, ab_w_out,
              hy_w_in, hy_b_in, hy_conv_w, hy_conv_b, hy_f_w1, hy_f_b1, hy_f_freq1, hy_f_w2, hy_f_b2,
              hy_f_freq2, hy_f_w3, hy_f_bias, hy_w_out, hy_b_out,
              ffn_w_gate, ffn_w_up, ffn_conv, ffn_conv_b, ffn_w_down, final_norm):
    f32 = jnp.float32
    n_ctx_b = x_prompt.shape[0]
    xc, xl = x_prompt, x_sample
    sc_ctx = jax.nn.silu(c_ctx)[None, None, :]
    sc_lat = jax.nn.silu(c)[:, None, :]
    new_delta = []
    new_ret = []
    for l in range(DEPTH):
        mc = jnp.split(sc_ctx @ mod_w[l] + mod_b[l], 6, axis=-1)
        ml = jnp.split(sc_lat @ mod_w[l] + mod_b[l], 6, axis=-1)
        hc = modulate(rmsnorm(xc, norm1[l]), mc[0], mc[1])
        hl = modulate(rmsnorm(xl, norm1[l]), ml[0], ml[1])
        j = l // 2
        if l % 2 == 0:
            ab = (ab_w_in[j], ab_conv[j], ab_a_log[j], ab_dt_bias[j], ab_norm_a[j], ab_norm_b[j],
                  ab_ret_decay[j], ab_w_out[j])
            zd = jnp.zeros((n_ctx_b, 2, A_HEADS, A_DK, A_DV), f32)
            zr = jnp.zeros((n_ctx_b, 2, B_HEADS, B_DK, B_DV), f32)
            oc, sd, sr = mixer_ab(hc, *ab, zd, zr)
            ol, _, _ = mixer_ab(hl, *ab, state_delta[:, j], state_ret[:, j])
            new_delta.append(sd)
            new_ret.append(sr)
        else:
            hy = (hy_w_in[j], hy_b_in[j], hy_conv_w[j], hy_conv_b[j], hy_f_w1[j], hy_f_b1[j], hy_f_freq1[j],
                  hy_f_w2[j], hy_f_b2[j], hy_f_freq2[j], hy_f_w3[j], hy_f_bias[j], hy_w_out[j], hy_b_out[j])
            oc = mixer_hyena(hc, *hy)
            ol = mixer_hyena(hl, *hy)
        xc = xc + mc[2] * oc
        xl = xl + ml[2] * ol
        ffn = (ffn_w_gate[l], ffn_w_up[l], ffn_conv[l], ffn_conv_b[l], ffn_w_down[l])
        hc = modulate(rmsnorm(xc, norm2[l]), mc[3], mc[4])
        hl = modulate(rmsnorm(xl, norm2[l]), ml[3], ml[4])
        xc = xc + mc[5] * conv_ffn(hc, *ffn, on_grid=False)
        xl = xl + ml[5] * conv_ffn(hl, *ffn, on_grid=True)
    y_prompt = rmsnorm(xc, final_norm)
    y_sample = rmsnorm(xl, final_norm)
    new_state_delta = jnp.stack(new_delta, axis=1)
    new_state_ret = jnp.stack(new_ret, axis=1)
    return (y_prompt, y_sample, new_state_delta, new_state_ret)
```

```python
import numpy as np
import contextlib
import concourse.bass as bass
import concourse.mybir as mybir
from concourse.bass_utils import run_bass_kernel_spmd

F32 = mybir.dt.float32
F32R = mybir.dt.float32r
BF16 = mybir.dt.bfloat16
AF = mybir.ActivationFunctionType
ALU = mybir.AluOpType


import os as _os
SERIAL = _os.environ.get("SERIAL", "1") == "1"


class Buf:
    def __init__(self, kb, name, t):
        self.kb = kb
        self.name = name
        self.t = t
        self.w = {}
        self.is_dram = False
        self.r = {}
        self.dsem = {}
        self.used = False

    def __getitem__(self, idx):
        return self.t[idx]


class KB:
    def __init__(self):
        self.nc = bass.Bass("TRN2", target_bir_lowering=False)
        nc = self.nc
        self.eng = {"pe": nc.tensor, "act": nc.scalar, "dve": nc.vector, "pool": nc.gpsimd, "sp": nc.sync}
        self.sems = {}
        self.cnt = {}
        for e in self.eng:
            self.sems[e] = nc.alloc_semaphore("sem_" + e)
            self.cnt[e] = 0
        self.sems["cc"] = nc.alloc_semaphore("sem_cc")
        self.cnt["cc"] = 0
        self.seen = {e: {} for e in self.eng}
        self.nbuf = 0
        self.bufs = []
        self.free_dsems = {}
        self.psums = []
        self.pi = 0
        self.dq = 0
        self.group = None

    def sb(self, name, shape, dt=F32):
        self.nbuf += 1
        b = Buf(self, name, self.nc.alloc_sbuf_tensor(f"{name}_{self.nbuf}", list(shape), dt))
        self.bufs.append(b)
        return b

    def sbs(self, es, name, shape, dt=F32):
        self.nbuf += 1
        t = es.enter_context(self.nc.sbuf_tensor(f"{name}_{self.nbuf}", list(shape), dt))
        b = Buf(self, name, t)
        self.bufs.append(b)
        return b

    def dram(self, name, shape, dt=F32, kind=None):
        if kind is None:
            t = self.nc.dram_tensor(name, list(shape), dt)
        else:
            t = self.nc.dram_tensor(name, list(shape), dt, kind=kind)
        b = Buf(self, name, t)
        b.is_dram = True
        b.kind = kind
        self.bufs.append(b)
        return b

    def init_psum(self):
        for i in range(8):
            t = self.nc.alloc_psum_tensor(f"ps{i}", [128, 512], F32)
            b = Buf(self, f"ps{i}", t)
            self.psums.append(b)

    def ps(self):
        b = self.psums[self.pi % 8]
        self.pi += 1
        return b

    def _wait(self, e, key, val):
        if val <= 0:
            return
        if self.seen[e].get(key, 0) >= val:
            return
        self.seen[e][key] = val
        self.eng[e].wait_ge(self.sems[key], val)

    def _deps(self, e, r, w, skipkey=None):
        for b in r:
            for k, v in b.w.items():
                if not (k == e and e == "pe"):
                    self._wait(e, k, v)
        for b in w:
            for k, v in b.w.items():
                if k == skipkey:
                    continue
                if not (k == e and e == "pe"):
                    self._wait(e, k, v)
            for k, v in b.r.items():
                if k == e:
                    continue
                self._wait(e, k, v)

    def op(self, e, fn, w=(), r=()):
        for b in r:
            b.used = True
        self._deps(e, r, w)
        if SERIAL:
            for k_, v_ in list(self.cnt.items()):
                if k_ != e:
                    self._wait(e, k_, v_)
        ins = fn(self.eng[e])
        self.cnt[e] += 1
        ins.then_inc(self.sems[e], 1)
        v = self.cnt[e]
        for b in r:
            b.r[e] = v
        for b in w:
            b.w = {e: v}
            b.r = {}
        return ins

    def _dsem(self, b, q):
        if q not in b.dsem:
            fl = self.free_dsems.setdefault(q, [])
            if fl:
                key = fl.pop()
            else:
                key = f"d{len(self.sems)}"
                self.sems[key] = self.nc.alloc_semaphore(key)
                self.cnt[key] = 0
            b.dsem[q] = key
        return b.dsem[q]

    @contextlib.contextmanager
    def scope(self):
        n0 = len(self.bufs)
        with contextlib.ExitStack() as es:
            yield es
            self.barrier()
            for b in self.bufs[n0:]:
                for q, key in b.dsem.items():
                    self.free_dsems.setdefault(q, []).append(key)
                b.dsem = {}
            del self.bufs[n0:]

    def dma(self, out_ap, in_ap, w, r, q=None, **kw):
        r.used = True
        if q is None:
            q = "pool" if (out_ap.dtype == F32R or in_ap.dtype == F32R) else "sp"
        sb_side = r if (w.is_dram and not r.is_dram) else w
        if self.group is not None and q == "sp":
            if self.group[0] is None:
                gk = f"d{len(self.sems)}"
                self.sems[gk] = self.nc.alloc_semaphore(gk)
                self.cnt[gk] = 0
                self.group[0] = gk
            key = self.group[0]
            self.group[1].append((w, r))
        else:
            key = self._dsem(sb_side, q)
        self._deps(q, [r], [w], skipkey=key)
        self.eng[q].dma_start(out=out_ap, in_=in_ap, **kw).then_inc(self.sems[key], 16)
        self.cnt[key] += 16
        r.r[key] = self.cnt[key]
        if w.is_dram:
            w.w[key] = self.cnt[key]
        else:
            w.w = {key: self.cnt[key]}
        w.r = {}

    def group_begin(self):
        self.group = [None, []]

    def group_end(self):
        key, bl = self.group
        self.group = None
        for (w, r) in bl:
            if w.is_dram:
                w.w[key] = self.cnt[key]
            else:
                w.w = {key: self.cnt[key]}
            r.r[key] = self.cnt[key]

    def allgather(self, ob, ib, groups, out_ap=None, in_ap=None):
        e = "pool"
        self._deps(e, [ib], [ob])
        if in_ap is None:
            in_ap = ib.t.ap()
        if out_ap is None:
            out_ap = ob.t.ap()
        self.eng[e].collective_compute("AllGather", ALU.bypass, replica_groups=groups,
                                       ins=[in_ap.opt()], outs=[out_ap.opt()]).then_inc(self.sems["cc"])
        self.cnt["cc"] += 1
        ib.r["cc"] = self.cnt["cc"]
        ob.w = dict(ob.w)
        ob.w["cc"] = self.cnt["cc"]
        ob.r = {}

    def barrier(self):
        tot = dict(self.cnt)
        for e in self.eng:
            for k, v in tot.items():
                if k != e:
                    self._wait(e, k, v)

    def finish(self):
        self.barrier()


import math

D = 1024
KC = 8
TC = 512
TL = 1024
TT = TC + TL
DFF = 2816
NF = 22
EPS = 1e-6
GROUPS = [[0, 1, 2, 3], [4, 5, 6, 7]]


class Ctx:
    pass


def build(nlayers=4, do_mix=True, raw=False):
    k = KB()
    nc = k.nc
    k.init_psum()
    g = Ctx()
    g.k = k

    def din(name, shape, dt=F32):
        return k.dram(name, shape, dt, kind="ExternalInput")

    g.xT = din("xT", [8, 128, TT])
    g.cond = din("cond", [128, 16])
    g.mod_w = din("mod_w", [4, 1024, 6144], F32R)
    g.mod_b = din("mod_b", [128, 4, 48])
    g.nrm = din("nrm", [128, 9, 8])
    g.ffn_wg = din("ffn_w_gate", [4, 1024, DFF], F32R)
    g.ffn_wu = din("ffn_w_up", [4, 1024, DFF], F32R)
    g.ffn_wd = din("ffn_w_down", [4, DFF, 1024], F32R)
    g.ffn_cw = din("ffn_cw", [128, 4, NF, 9])
    g.ffn_cb = din("ffn_cb", [128, 4, NF])
    g.cmask = din("cmask", [128, 2])
    g.consts = din("consts", [128, 4, 128])
    g.yT = k.dram("yT", [8, 128, TT], F32, kind="ExternalOutput")
    g.halo_in = k.dram("halo_in", [128, 8, 128], F32R)
    g.halo_out = k.dram("halo_out", [4, 128, 8, 128], F32R)
    import os
    if os.environ.get("BIGSCR"):
        g.big = k.dram("bigscr", [int(os.environ["BIGSCR"]), 1024, 256], F32)
        g.bigt = k.sb("bigt", [128, 256])
        k.dma(g.bigt[:], g.big[0, 0:128, :], g.bigt, g.big)

    g.xc = k.sb("xc", [128, KC, TC])
    g.xl = k.sb("xl", [128, KC, TL])
    g.wb = [k.sb(f"wb{i}", [128, 5632], F32R) for i in range(2)]
    g.wi = 0
    g.cst = k.sb("cst", [128, 4, 128], F32R)
    g.cstf = k.sb("cstf", [128, 4, 128])
    g.sT = k.sb("sT", [128, 8, 2], F32R)
    g.cnd = k.sb("cnd", [128, 16])
    g.modT = k.sb("modT", [128, 48, 2])
    g.modb = k.sb("modb", [128, 4, 48])
    g.nrmt = k.sb("nrmt", [128, 9, 8])
    g.mv = k.sb("mv", [128, 2, 6, 8])
    g.cw = k.sb("cw", [128, 4, NF, 9])
    g.cb = k.sb("cb", [128, 4, NF])
    g.cm = k.sb("cm", [128, 2])

    g.post_setup = []
    k.group_begin()
    if do_mix:
        ab_setup(g, din)
        if do_mix > 1:
            hy_setup(g, din)
    for i in range(8):
        k.dma(g.xc[:, i, :], g.xT[i, :, 0:TC], g.xc, g.xT)
        k.dma(g.xl[:, i, :], g.xT[i, :, TC:TT], g.xl, g.xT)
    k.dma(g.cstf[:], g.consts[:], g.cstf, g.consts)
    k.dma(g.cnd[:], g.cond[:], g.cnd, g.cond)
    k.dma(g.modb[:], g.mod_b[:], g.modb, g.mod_b)
    k.dma(g.nrmt[:], g.nrm[:], g.nrmt, g.nrm)
    k.dma(g.cw[:], g.ffn_cw[:], g.cw, g.ffn_cw)
    k.dma(g.cb[:], g.ffn_cb[:], g.cb, g.ffn_cb)
    k.dma(g.cm[:], g.cmask[:], g.cm, g.cmask)
    k.group_end()
    for f in g.post_setup:
        f()
    k.op("dve", lambda e: e.tensor_copy(out=g.cst[:], in_=g.cstf[:]), w=[g.cst], r=[g.cstf])
    k.op("act", lambda e: e.activation(out=g.sT[:].rearrange("p k j -> p j k"),
                                       in_=g.cnd[:].rearrange("p (j k) -> p j k", j=2), func=AF.Silu),
         w=[g.sT], r=[g.cnd])

    for l in range(nlayers):
        mod_layer(g, l)
        if do_mix:
            if l % 2 == 0:
                ab_layer(g, l)
            elif do_mix > 1:
                hy_layer(g, l)
        ffn_layer(g, l)

    final_out(g, raw)
    k.finish()
    return k


def wload(g, Wap, K, cw, src):
    k = g.k
    wb = g.wb[g.wi % 2]
    g.wi += 1
    cwp = ((cw + 127) // 128) * 128
    view = wb[:, 0:K * cwp].rearrange("p (k n) -> p k n", k=K)
    k.dma(view[:, :, 0:cw], Wap.rearrange("(k p) n -> p k n", p=128), wb, src)
    return wb, view


def proj(g, Wsrc, Wap2d, K, ncols, xin, tblocks, evac, cbw=512):
    k = g.k
    for c0 in range(0, ncols, cbw):
        cw = min(cbw, ncols - c0)
        wb, view = wload(g, Wap2d[:, c0:c0 + cw], K, cw, Wsrc)
        nmt = (cw + 127) // 128
        for mt in range(nmt):
            m = min(128, cw - mt * 128)
            for ti, (t0, tl) in enumerate(tblocks):
                ps = k.ps()
                for kc in range(K):
                    ap, buf = xin(kc, ti)
                    k.op("pe", lambda e, kc=kc, ap=ap: e.matmul(ps[:, 0:tl], lhsT=view[:, kc, mt * 128:(mt + 1) * 128],
                                                                 rhs=ap, start=(kc == 0), stop=(kc == K - 1)),
                         w=[ps], r=[wb, buf])
                evac(ps, c0 // 128 + mt, ti, m)


def mod_layer(g, l):
    k = g.k
    ps = k.ps()
    for cb in range(12):
        wb, view = wload(g, g.mod_w[l, :, cb * 512:(cb + 1) * 512], 8, 512, g.mod_w)
        for mt in range(4):
            j = cb * 4 + mt
            for kc in range(8):
                k.op("pe", lambda e, kc=kc, j=j, mt=mt: e.matmul(ps[:, 2 * j:2 * j + 2], lhsT=view[:, kc, mt * 128:(mt + 1) * 128],
                                                                 rhs=g.sT[:, kc, :], start=(kc == 0), stop=(kc == 7)),
                     w=[ps], r=[wb, g.sT])
    for w_ in range(2):
        k.op("dve", lambda e, w_=w_: e.tensor_tensor(out=g.modT[:, :, w_], in0=ps[:, 0:96].rearrange("p (j w) -> p j w", w=2)[:, :, w_],
                                                     in1=g.modb[:, l, :], op=ALU.add), w=[g.modT], r=[ps, g.modb])
    for w_ in range(2):
        for half in range(2):
            nidx = l if half == 0 else 4 + l
            sh = g.modT[:, (3 * half + 0) * 8:(3 * half + 1) * 8, w_]
            sc = g.modT[:, (3 * half + 1) * 8:(3 * half + 2) * 8, w_]
            gt = g.modT[:, (3 * half + 2) * 8:(3 * half + 3) * 8, w_]
            k.op("dve", lambda e, sc=sc, nidx=nidx, half=half, w_=w_: e.scalar_tensor_tensor(
                out=g.mv[:, w_, 3 * half + 0, :], in0=sc, scalar=1.0, in1=g.nrmt[:, nidx, :], op0=ALU.add, op1=ALU.mult),
                 w=[g.mv], r=[g.modT, g.nrmt])
            k.op("dve", lambda e, sh=sh, half=half, w_=w_: e.tensor_copy(out=g.mv[:, w_, 3 * half + 1, :], in_=sh), w=[g.mv], r=[g.modT])
            k.op("dve", lambda e, gt=gt, half=half, w_=w_: e.tensor_copy(out=g.mv[:, w_, 3 * half + 2, :], in_=gt), w=[g.mv], r=[g.modT])


def normmod(g, x, T, which, half, out, tsl=None, ooff=0):
    k = g.k
    with k.scope() as es:
        sq = k.sbs(es, "sq", [128, 8, 512], F32R)
        rstd = k.sbs(es, "rstd", [128, 512])
        tmp = k.sbs(es, "tmp", [128, 512])
        for t0 in range(0, T, 512):
            tl = min(512, T - t0)
            for kc in range(8):
                k.op("act", lambda e, kc=kc: e.activation(out=sq[:, kc, 0:tl], in_=x[:, kc, t0:t0 + tl], func=AF.Square),
                     w=[sq], r=[x])
            ps = k.ps()
            for kc in range(8):
                k.op("pe", lambda e, kc=kc: e.matmul(ps[:, 0:tl], lhsT=g.cst[:, 1, :], rhs=sq[:, kc, 0:tl],
                                                     start=(kc == 0), stop=(kc == 7)), w=[ps], r=[g.cst, sq])
            k.op("act", lambda e: e.activation(out=rstd[:, 0:tl], in_=ps[:, 0:tl], func=AF.Sqrt, bias=EPS), w=[rstd], r=[ps])
            k.op("dve", lambda e: e.reciprocal(out=rstd[:, 0:tl], in_=rstd[:, 0:tl]), w=[rstd], r=[rstd])
            for kc in range(8):
                k.op("dve", lambda e, kc=kc: e.tensor_tensor(out=tmp[:, 0:tl], in0=x[:, kc, t0:t0 + tl], in1=rstd[:, 0:tl], op=ALU.mult),
                     w=[tmp], r=[x, rstd])
                k.op("act", lambda e, kc=kc: e.activation(out=out[:, kc, ooff + t0:ooff + t0 + tl], in_=tmp[:, 0:tl], func=AF.Identity,
                                                          scale=g.mv[:, which, 3 * half + 0, kc:kc + 1],
                                                          bias=g.mv[:, which, 3 * half + 1, kc:kc + 1]),
                     w=[out], r=[tmp, g.mv])


def ffn_layer(g, l):
    k = g.k
    with k.scope() as es:
        h2 = k.sbs(es, "h2", [128, 8, TC], F32R)
        aT = k.sbs(es, "aT", [128, NF, TC], F32R)
        gb = k.sbs(es, "gb", [128, TC])
        acc = k.sbs(es, "acc", [128, TC])
        normmod(g, g.xc, TC, 0, 1, h2)
        ffn_core(g, l, h2, aT, gb, acc, [(0, TC)], grid=False, which=0, x=g.xc, xoff=0)
    with k.scope() as es:
        h2 = k.sbs(es, "h2l", [128, 8, 64 + TL + 64], F32R)
        ed = k.sbs(es, "ed", [128, 8, 128], F32R)
        hp = k.sbs(es, "hp", [128, 8, 128], F32R)
        normmod(g, g.xl, TL, 1, 1, h2, ooff=64)
        k.op("dve", lambda e: e.tensor_copy(out=ed[:, :, 0:64], in_=h2[:, :, 64:128].bitcast(F32)), w=[ed], r=[h2])
        k.op("dve", lambda e: e.tensor_copy(out=ed[:, :, 64:128], in_=h2[:, :, 64 + TL - 64:64 + TL].bitcast(F32)), w=[ed], r=[h2])
        k.dma(g.halo_in[:], ed[:], g.halo_in, ed)
        k.allgather(g.halo_out, g.halo_in, GROUPS)
        pid = nc_pid(g)
        rp = (pid + 3) % 4
        rn = (pid + 1) % 4
        k.dma(hp[:, :, 0:64], g.halo_out[bass.ds(rp, 1), :, :, 64:128].rearrange("o p k t -> p (o k) t"), hp, g.halo_out)
        k.dma(hp[:, :, 64:128], g.halo_out[bass.ds(rn, 1), :, :, 0:64].rearrange("o p k t -> p (o k) t"), hp, g.halo_out)
        k.op("dve", lambda e: e.tensor_scalar(out=h2[:, :, 0:64], in0=hp[:, :, 0:64].bitcast(F32), scalar1=g.cm[:, 0:1], scalar2=None, op0=ALU.mult),
             w=[h2], r=[hp, g.cm])
        k.op("dve", lambda e: e.tensor_scalar(out=h2[:, :, 64 + TL:128 + TL], in0=hp[:, :, 64:128].bitcast(F32), scalar1=g.cm[:, 1:2], scalar2=None, op0=ALU.mult),
             w=[h2], r=[hp, g.cm])
        aT = k.sbs(es, "aTl", [128, NF, 512], F32R)
        gb = k.sbs(es, "gbl", [128, 640])
        acc = k.sbs(es, "accl", [128, 512])
        for hf in range(2):
            ffn_core(g, l, h2, aT, gb, acc, [(hf * 512, 640)], grid=True, which=1, x=g.xl, xoff=hf * 512)


def nc_pid(g):
    if not hasattr(g, "pid"):
        g.pid = g.k.nc.gpsimd.partition_id()
    return g.pid


def ffn_core(g, l, h2, aT, gb, acc, tblk, grid, which, x, xoff):
    k = g.k
    t0, tl = tblk[0]
    nout = 512

    def xin(kc, ti):
        return h2[:, kc, t0:t0 + tl], h2

    blocks = [(0, 512)] if tl == 512 else [(0, 512), (512, tl - 512)]

    def xin2(kc, ti):
        b0, bl = blocks[ti]
        return h2[:, kc, t0 + b0:t0 + b0 + bl], h2

    def evac_gate(ps, mt, ti, m):
        b0, bl = blocks[ti]
        k.op("act", lambda e: e.activation(out=gb[:, b0:b0 + bl], in_=ps[:, 0:bl], func=AF.Copy), w=[gb], r=[ps])
        if ti != len(blocks) - 1:
            return
        cwt = g.cw[:, l, mt, :]
        if not grid:
            gv = gb[:, 0:512].rearrange("p (s t) -> p s t", s=2)
            av = acc[:, 0:512].rearrange("p (s t) -> p s t", s=2)
            k.op("dve", lambda e: e.tensor_scalar(out=acc[:, 0:512], in0=gb[:, 0:512], scalar1=cwt[:, 4:5], scalar2=None, op0=ALU.mult), w=[acc], r=[gb, g.cw])
            k.op("dve", lambda e: e.scalar_tensor_tensor(out=av[:, :, 1:256], in0=gv[:, :, 0:255], scalar=cwt[:, 3:4], in1=av[:, :, 1:256],
                                                         op0=ALU.mult, op1=ALU.add), w=[acc], r=[gb, g.cw, acc])
            k.op("dve", lambda e: e.scalar_tensor_tensor(out=av[:, :, 0:255], in0=gv[:, :, 1:256], scalar=cwt[:, 5:6], in1=av[:, :, 0:255],
                                                         op0=ALU.mult, op1=ALU.add), w=[acc], r=[gb, g.cw, acc])
        else:
            gv = gb[:, 0:640].rearrange("p (r c) -> p r c", c=64)
            av = acc[:, 0:512].rearrange("p (r c) -> p r c", c=64)
            first = True
            eng = "dve"
            for i in range(3):
                for j in (1, 0, 2):
                    tap = cwt[:, 3 * i + j:3 * i + j + 1]
                    if j == 1:
                        o_, i_ = av[:, :, :], gv[:, i:i + 8, :]
                    elif j == 0:
                        o_, i_ = av[:, :, 1:64], gv[:, i:i + 8, 0:63]
                    else:
                        o_, i_ = av[:, :, 0:63], gv[:, i:i + 8, 1:64]
                    if first:
                        k.op(eng, lambda e, o_=o_, i_=i_, tap=tap: e.tensor_scalar(out=o_, in0=i_, scalar1=tap, scalar2=None, op0=ALU.mult),
                             w=[acc], r=[gb, g.cw])
                        first = False
                    else:
                        k.op(eng, lambda e, o_=o_, i_=i_, tap=tap: e.scalar_tensor_tensor(out=o_, in0=i_, scalar=tap, in1=o_, op0=ALU.mult, op1=ALU.add),
                             w=[acc], r=[gb, g.cw, acc])
        k.op("act", lambda e: e.activation(out=aT[:, mt, :], in_=acc[:, 0:512], func=AF.Silu, bias=g.cb[:, l, mt:mt + 1]),
             w=[aT], r=[acc, g.cb])

    proj(g, g.ffn_wg, g.ffn_wg[l], 8, DFF, xin2, blocks, evac_gate)

    uoff = t0 + (64 if grid else 0)

    def xin_up(kc, ti):
        return h2[:, kc, uoff:uoff + 512], h2

    def evac_up(ps, mt, ti, m):
        k.op("dve", lambda e: e.tensor_tensor(out=aT[:, mt, :], in0=aT[:, mt, :].bitcast(F32), in1=ps[:, 0:512], op=ALU.mult), w=[aT], r=[aT, ps])

    proj(g, g.ffn_wu, g.ffn_wu[l], 8, DFF, xin_up, [(0, 512)], evac_up)

    def xin_dn(kc, ti):
        return aT[:, kc, :], aT

    def evac_dn(ps, mt, ti, m):
        k.op("dve", lambda e: e.scalar_tensor_tensor(out=x[:, mt, xoff:xoff + 512], in0=ps[:, 0:512], scalar=g.mv[:, which, 5, mt:mt + 1],
                                                     in1=x[:, mt, xoff:xoff + 512], op0=ALU.mult, op1=ALU.add), w=[x], r=[ps, g.mv, x])

    proj(g, g.ffn_wd, g.ffn_wd[l], NF, 1024, xin_dn, [(0, 512)], evac_dn, cbw=256)


def final_out(g, raw):
    k = g.k
    with k.scope() as es:
        g.mvf = None
        for (x, T, off) in ((g.xc, TC, 0), (g.xl, TL, TC)):
            o = k.sbs(es, "fo", [128, 8, T])
            if raw:
                k.op("dve", lambda e, o=o, x=x: e.tensor_copy(out=o[:], in_=x[:]), w=[o], r=[x])
            else:
                fin_norm(g, x, T, o, es)
            for kc in range(8):
                k.dma(g.yT[kc, :, off:off + T], o[:, kc, :], g.yT, o)


def fin_norm(g, x, T, out, es):
    k = g.k
    sq = k.sbs(es, "sqf", [128, 8, 512], F32R)
    rstd = k.sbs(es, "rstdf", [128, 512])
    for t0 in range(0, T, 512):
        tl = 512
        for kc in range(8):
            k.op("act", lambda e, kc=kc: e.activation(out=sq[:, kc, 0:tl], in_=x[:, kc, t0:t0 + tl], func=AF.Square), w=[sq], r=[x])
        ps = k.ps()
        for kc in range(8):
            k.op("pe", lambda e, kc=kc: e.matmul(ps[:, 0:tl], lhsT=g.cst[:, 1, :], rhs=sq[:, kc, 0:tl], start=(kc == 0), stop=(kc == 7)),
                 w=[ps], r=[g.cst, sq])
        k.op("act", lambda e: e.activation(out=rstd[:, 0:tl], in_=ps[:, 0:tl], func=AF.Sqrt, bias=EPS), w=[rstd], r=[ps])
        k.op("dve", lambda e: e.reciprocal(out=rstd[:, 0:tl], in_=rstd[:, 0:tl]), w=[rstd], r=[rstd])
        for kc in range(8):
            k.op("dve", lambda e, kc=kc: e.scalar_tensor_tensor(out=out[:, kc, t0:t0 + tl], in0=x[:, kc, t0:t0 + tl], scalar=g.nrmt[:, 8, kc:kc + 1],
                                                                in1=rstd[:, 0:tl], op0=ALU.mult, op1=ALU.mult), w=[out], r=[x, g.nrmt, rstd])


def fm(v):
    v = np.asarray(v, np.float32)
    sh = v.shape
    n = sh[-1] // 128
    v = v.reshape(sh[:-1] + (n, 128))
    return np.ascontiguousarray(np.moveaxis(v, -1, 0))


def prep(inp):
    shared = {}
    shared["mod_w"] = np.ascontiguousarray(inp["mod_w"], np.float32)
    shared["mod_b"] = np.ascontiguousarray(fm(inp["mod_b"]))
    nrm = np.concatenate([inp["norm1"], inp["norm2"], inp["final_norm"][None]], 0)
    shared["nrm"] = fm(nrm)
    shared["ffn_w_gate"] = np.ascontiguousarray(inp["ffn_w_gate"], np.float32)
    shared["ffn_w_up"] = np.ascontiguousarray(inp["ffn_w_up"], np.float32)
    shared["ffn_w_down"] = np.ascontiguousarray(inp["ffn_w_down"], np.float32)
    cw = np.asarray(inp["ffn_conv"], np.float32).reshape(4, 9, DFF)
    shared["ffn_cw"] = np.ascontiguousarray(np.moveaxis(fm(cw), 2, 3))
    shared["ffn_cb"] = fm(inp["ffn_conv_b"])
    consts = np.zeros((128, 4, 128), np.float32)
    consts[:, 0, :] = np.eye(128)
    consts[:, 1, :] = 1.0 / 1024
    consts[:, 2, :] = 1.0 / 128
    consts[:, 3, :] = 1.0
    shared["consts"] = consts
    shared["ab_w_in"] = np.ascontiguousarray(inp["ab_w_in"], np.float32)
    shared["ab_w_out"] = np.ascontiguousarray(inp["ab_w_out"], np.float32)
    abc = np.asarray(inp["ab_conv"], np.float32)
    cwc = np.moveaxis(fm(abc), 2, 3)
    shared["ab_cw_c"] = np.ascontiguousarray(cwc)
    shared["ab_nab"] = np.ascontiguousarray(np.stack([inp["ab_norm_a"].T, inp["ab_norm_b"].T], -1), np.float32)
    tab = np.zeros((128, 12, 128), np.float32)
    tab[:, 0, :] = np.eye(128)
    ii = np.arange(64)
    s_, c_ = ii[:, None], ii[None, :]
    tab[:64, 1, :64] = (s_ <= c_); tab[:64, 2, :64] = (s_ >= c_)
    tab[:64, 3, :64] = (s_ > c_); tab[:64, 4, :64] = (s_ < c_)
    tab[:64, 5, :64] = -1.0 * (c_ < s_)
    tab[:64, 6, :64] = -1.0 * (c_ > s_)
    tab[:64, 7, :64] = np.abs(s_ - c_)
    tab[:, 8, :64] = (ii + 1)[None, :]
    tab[:, 9, :64] = (64 - ii)[None, :]
    tab[:64, 10, 0] = 63 - ii
    tab[:64, 10, 1] = ii
    tab[:, 11, :] = 1.0
    shared["abtab"] = tab
    if "hy_w_in" in inp and HY_HOST is not None:
        HY_HOST(inp, shared, None)
    maps = []
    for c in range(8):
        gi, r = c // 4, c % 4
        m = dict(shared)
        wi = np.asarray(inp["ab_w_in"], np.float32)
        cols = []
        for base in (0, 512, 1024, 1536, 2064, 2576, 3088, 3600):
            cols += list(range(base + 128 * r, base + 128 * r + 128))
        cols += [2048 + r, 2052 + r, 2056 + r, 2060 + r]
        m["ab_w_in_l"] = np.ascontiguousarray(wi[:, :, cols])
        m["ab_cw_l"] = np.ascontiguousarray(cwc[:, :, [r, 4 + r, 8 + r], :])
        t5 = np.zeros((128, 2, 5, 2), np.float32)
        rt = np.zeros((128, 2, 5, 2), np.float32)
        for hd in range(5):
            hh = hd if hd < 4 else r
            for d in range(2):
                t5[d, :, hd, 0] = inp["ab_a_log"][:, d, hh]
                t5[d, :, hd, 1] = inp["ab_dt_bias"][:, d, hh]
                rt[:, :, hd, d] = inp["ab_ret_decay"][:, d, hh][None, :]
        m["ab_abc"] = t5
        m["ab_ret"] = rt
        m["sdl_in"] = np.ascontiguousarray(inp["state_delta"][gi, :, :, r], np.float32)
        m["srl_in"] = np.ascontiguousarray(inp["state_ret"][gi, :, :, r], np.float32)
        if "hy_w_in" in inp and HY_HOST is not None:
            HY_HOST(inp, m, (c, gi, r))
            m.pop("_deltas", None)
        xt = np.concatenate([inp["x_prompt"][2 * c].T, inp["x_prompt"][2 * c + 1].T,
                             inp["x_sample"][gi, 1024 * r:1024 * (r + 1)].T], axis=1)
        m["xT"] = np.ascontiguousarray(xt.reshape(8, 128, TT), np.float32)
        cond = np.stack([inp["c_ctx"], inp["c"][gi]], 0)
        m["cond"] = np.ascontiguousarray(fm(cond).reshape(128, 16))
        cm = np.ones((128, 2), np.float32)
        if r == 0:
            cm[:, 0] = 0
        if r == 3:
            cm[:, 1] = 0
        m["cmask"] = cm
        maps.append(m)
    return maps


HY_HOST = None


def assemble_states(res):
    sd = np.zeros((16, 2, 2, 4, 128, 128), np.float32)
    sr = np.zeros((16, 2, 2, 4, 128, 128), np.float32)
    for c in range(8):
        sd[2 * c:2 * c + 2] = res[c]["sd_out"].reshape(2, 2, 2, 4, 128, 128)
        sr[2 * c:2 * c + 2] = res[c]["sr_out"].reshape(2, 2, 2, 4, 128, 128)
    return sd, sr


def assemble(res):
    yp = np.zeros((16, 256, 1024), np.float32)
    ys = np.zeros((2, 4096, 1024), np.float32)
    for c in range(8):
        gi, r = c // 4, c % 4
        yT = res[c]["yT"].reshape(1024, TT)
        yp[2 * c] = yT[:, 0:256].T
        yp[2 * c + 1] = yT[:, 256:512].T
        ys[gi, 1024 * r:1024 * (r + 1)] = yT[:, 512:].T
    return yp, ys


C = 64


def ab_setup(g, din):
    k = g.k
    g.ab_w_in = din("ab_w_in", [2, 1024, 4112], F32R)
    import os
    if os.environ.get("SKIPL") == "1":
        din2 = lambda n, sh, dt=F32: k.dram(n, sh, dt)
    else:
        din2 = din
    g.ab_w_in_l = din2("ab_w_in_l", [2, 1024, 1028], F32R)
    g.ab_w_out = din("ab_w_out", [2, 1024, 1024], F32R)
    g.ab_cw_c = din("ab_cw_c", [128, 2, 12, 3])
    g.ab_cw_l = din("ab_cw_l", [128, 2, 3, 3])
    g.ab_abc = din("ab_abc", [128, 2, 5, 2])
    g.ab_ret = din("ab_ret", [128, 2, 5, 2])
    g.ab_nab = din("ab_nab", [128, 2, 2])
    g.sdl_in = din2("sdl_in", [2, 2, 128, 128])
    g.srl_in = din2("srl_in", [2, 2, 128, 128])
    g.abtab = din("abtab", [128, 12, 128])
    g.sd_out = k.dram("sd_out", [32, 128, 128], F32, kind="ExternalOutput")
    g.sr_out = k.dram("sr_out", [32, 128, 128], F32, kind="ExternalOutput")
    import os
    g.dbg = True if os.environ.get("DBG") == "1" else None
    g.Pc = k.dram("Pc", [4224, TC], F32)
    g.Pl = k.dram("Pl", [1152, 4096], F32)
    g.h1_in = k.dram("h1_in", [8, 128, TL], F32R)
    g.h1_out = k.dram("h1_out", [8, 4, 128, TL], F32R)
    g.o_in = k.dram("o_in", [2, 4, 128, 1024], F32R)
    g.o_out = k.dram("o_out", [2, 4, 4, 128, 1024], F32R)
    g.cwc = k.sb("cwc", [128, 2, 12, 3])
    g.cwl = k.sb("cwl", [128, 2, 3, 3])
    g.abc = k.sb("abc", [128, 2, 5, 2])
    g.nexp = k.sb("nexp", [128, 2, 5, 1])
    g.ret = k.sb("ret", [128, 2, 5, 2])
    g.lgam = k.sb("lgam", [128, 2, 5, 2])
    g.nab = k.sb("nab", [128, 2, 2])
    g.tab = k.sb("tab", [128, 12, 128])
    for (t, s) in ((g.cwc, g.ab_cw_c), (g.cwl, g.ab_cw_l), (g.abc, g.ab_abc), (g.ret, g.ab_ret), (g.nab, g.ab_nab), (g.tab, g.abtab)):
        k.dma(t[:], s[:], t, s)
    g.post_setup.append(lambda: ab_setup2(g))


def ab_setup2(g):
    k = g.k
    k.op("act", lambda e: e.activation(out=g.nexp[:], in_=g.abc[:, :, :, 0:1], func=AF.Exp), w=[g.nexp], r=[g.abc])
    k.op("dve", lambda e: e.tensor_scalar(out=g.nexp[:], in0=g.nexp[:], scalar1=-1.0, scalar2=None, op0=ALU.mult), w=[g.nexp], r=[g.nexp])
    k.op("act", lambda e: e.activation(out=g.lgam[:], in_=g.ret[:], func=AF.Exp), w=[g.lgam], r=[g.ret])
    k.op("dve", lambda e: e.tensor_scalar(out=g.lgam[:], in0=g.lgam[:], scalar1=-1.0, scalar2=None, op0=ALU.mult), w=[g.lgam], r=[g.lgam])


def T_(g, i, p=64, f=64):
    return g.tab[0:p, i, 0:f]


def ab_layer(g, l):
    k = g.k
    j = l // 2
    with k.scope() as es:
        h1 = k.sbs(es, "h1c", [128, 8, TC], F32R)
        stg = [k.sbs(es, f"stg{i}", [128, 512]) for i in range(2)]
        normmod(g, g.xc, TC, 0, 0, h1)
        h1l = k.sbs(es, "h1l", [128, 8, TL], F32R)
        normmod(g, g.xl, TL, 1, 0, h1l)
        h1_exchange(g, h1l)
        cnt = [0]

        def mk_evac(P, tcol):
            def evac(ps, mt, ti, m):
                s = stg[cnt[0] % 2]
                cnt[0] += 1
                k.op("act", lambda e: e.activation(out=s[:], in_=ps[:], func=AF.Copy), w=[s], r=[ps])
                k.dma(P[mt * 128:(mt + 1) * 128, tcol:tcol + 512], s[:], P, s)
            return evac

        proj(g, g.ab_w_in, g.ab_w_in[j], 8, 4112, lambda kc, ti: (h1[:, kc, :], h1), [(0, 512)], mk_evac(g.Pc, 0))
        hb = h1
        import os
        SKIPL = os.environ.get("SKIPL") == "1"
        for tb in (range(0) if SKIPL else range(8)):
            k.dma(hb[:], g.h1_out[:, tb // 2, :, (tb % 2) * 512:(tb % 2) * 512 + 512].rearrange("k p t -> p k t"), hb, g.h1_out)
            proj(g, g.ab_w_in_l, g.ab_w_in_l[j], 8, 1028, lambda kc, ti: (hb[:, kc, :], hb), [(0, 512)], mk_evac(g.Pl, tb * 512), cbw=640)
    import os
    if os.environ.get("ABV") == "1":
        return
    with k.scope() as es:
        omix = k.sbs(es, "omix", [128, 8, TC], F32R)
        for s in range(2):
            for h in range(4):
                rows = dict(qa=128 * h, ka=512 + 128 * h, va=1024 + 128 * h, z=1536 + 128 * h,
                            a=[2048 + h, 2052 + h], b=[2056 + h, 2060 + h],
                            qb=2064 + 128 * h, kb=2576 + 128 * h, vb=3088 + 128 * h, gb=3600 + 128 * h)
                head_pass(g, j, g.Pc, rows, s * 256, 256, 256, h, g.cwc[:, j, :, :], [h, 4 + h, 8 + h], None,
                          (s, h), omix, (h, 4 + h), s * 256)

        def evac_o(ps, mt, ti, m):
            k.op("dve", lambda e: e.scalar_tensor_tensor(out=g.xc[:, mt, :], in0=ps[:, 0:512], scalar=g.mv[:, 0, 2, mt:mt + 1],
                                                         in1=g.xc[:, mt, :], op0=ALU.mult, op1=ALU.add), w=[g.xc], r=[ps, g.mv, g.xc])
        proj(g, g.ab_w_out, g.ab_w_out[j], 8, 1024, lambda kc, ti: (omix[:, kc, :], omix), [(0, 512)], evac_o)
    import os
    if os.environ.get("SKIPL") == "1":
        return
    with k.scope() as es:
        omixl = k.sbs(es, "omixl", [128, 2, 4096], F32R)
        rows = dict(qa=0, ka=128, va=256, z=384, qb=512, kb=640, vb=768, gb=896, a=[1024, 1025], b=[1026, 1027])
        head_pass(g, j, g.Pl, rows, 0, 4096, 256, 4, g.cwl[:, j, :, :], [0, 1, 2], (g.sdl_in, g.srl_in), None, omixl, (0, 1), 0)
        o_exchange(g, omixl)
    mix_out_lat(g, g.ab_w_out, g.ab_w_out[j], None)


def h1_exchange(g, h1l):
    k = g.k
    k.dma(g.h1_in[:].rearrange("k p t -> p k t"), h1l[:], g.h1_in, h1l)
    for kc in range(8):
        k.allgather(g.h1_out, g.h1_in, GROUPS, out_ap=g.h1_out[kc].rearrange("r p t -> (r p) t"), in_ap=g.h1_in[kc])


def o_exchange(g, omixl):
    k = g.k
    k.dma(g.o_in[:].rearrange("a q p t -> p a q t"), omixl[:].rearrange("p a (q t) -> p a q t", q=4), g.o_in, omixl)
    for a in range(2):
        for q in range(4):
            k.allgather(g.o_out, g.o_in, GROUPS, out_ap=g.o_out[a, q].rearrange("r p t -> (r p) t"), in_ap=g.o_in[a, q])


def mix_out_lat(g, Wsrc, Wap, bias, kcmap=lambda a, i: a * 4 + i):
    k = g.k
    pid = nc_pid(g)
    r4 = pid % 4
    with k.scope() as es:
        om = k.sbs(es, "om", [128, 8, 512], F32R)
        for hf in range(2):
            for i in range(4):
                for a in range(2):
                    k.dma(om[:, kcmap(a, i), :], g.o_out[a, bass.ds(r4, 1), i, :, hf * 512:(hf + 1) * 512].rearrange("o p t -> p (o t)"), om, g.o_out)

            def evac_o(ps, mt, ti, m):
                xs = g.xl[:, mt, hf * 512:(hf + 1) * 512]
                if bias is not None:
                    k.op("act", lambda e: e.activation(out=ps[:, 0:512], in_=ps[:, 0:512], func=AF.Identity, bias=bias[:, mt:mt + 1]),
                         w=[ps], r=[ps, g.hyb])
                k.op("dve", lambda e: e.scalar_tensor_tensor(out=xs, in0=ps[:, 0:512], scalar=g.mv[:, 1, 2, mt:mt + 1],
                                                             in1=xs, op0=ALU.mult, op1=ALU.add), w=[g.xl], r=[ps, g.mv, g.xl])
            proj(g, Wsrc, Wap, 8, 1024, lambda kc, ti: (om[:, kc, :], om), [(0, 512)], evac_o)


def dbg_dump(g, idx, tile):
    k = g.k
    if getattr(g, "dbg", None) is None:
        return
    slots = {0: 8, 1: 9, 4: 10, 5: 11, 6: 12, 8: 13, 9: 14, 10: 15, 11: 24, 12: 25, 13: 26, 14: 27, 15: 28, 16: 29, 17: 30, 18: 31}
    if idx not in slots:
        return
    k.dma(g.sd_out[slots[idx]], tile[:, 0:128], g.sd_out, tile)


def head_pass(g, j, P, rows, tok0, L, B, hd, cwt, cwi, init, sout, omix, och, ooff):
    k = g.k
    nb = L // B
    ncb = B // C
    with k.scope() as es:
        oa = k.sbs(es, "oa", [128, L])
        ob = k.sbs(es, "ob", [128, L])
        raw = [k.sbs(es, f"raw{i}", [128, B + 2]) for i in range(3)]
        cv = [k.sbs(es, f"cv{i}", [128, B]) for i in range(3)]
        rb = [k.sbs(es, f"rb{i}", [128, B]) for i in range(3)]
        AB = k.sbs(es, "AB", [128, B])
        tmp = k.sbs(es, "hp_tmp", [128, B])
        S = [k.sbs(es, f"S{i}", [128, 128]) for i in range(2)]
        R = [k.sbs(es, f"R{i}", [128, 128]) for i in range(2)]
        sm = {n: k.sbs(es, n, [128, 128]) for n in ("tm", "cols", "Gbc", "Grow", "DT", "Dm", "t1", "AT", "XA", "XB", "XTA", "XTB",
                                                   "TT", "kd", "bv", "qd", "rv", "vn", "ATb", "vtb", "ksd", "qcd", "DmTb", "cdrow", "rc")}
        k.op("dve", lambda e: e.memset(AB[:], 0.0), w=[AB])
        if getattr(g, "dbg", None):
            for nm_, t_ in sm.items():
                k.op("dve", lambda e, t_=t_: e.memset(t_[:], 0.0), w=[t_])
        for d in range(2):
            lg = g.lgam[:, j, hd, d:d + 1]
            k.op("act", lambda e: e.activation(out=sm["DmTb"][0:64, 0:64], in_=T_(g, 7), func=AF.Exp, scale=lg[0:64, :]), w=[sm["DmTb"]], r=[g.tab, g.lgam])
            k.op("dve", lambda e: e.scalar_tensor_tensor(out=sm["DmTb"][0:64, 0:64], in0=sm["DmTb"][0:64, 0:64], scalar=128.0 ** -0.5,
                                                         in1=g.tab[0:64, 1 + d, 0:64], op0=ALU.mult, op1=ALU.mult), w=[sm["DmTb"]], r=[g.tab, sm["DmTb"]])
            k.op("act", lambda e: e.activation(out=sm["cdrow"][:, 0:64], in_=g.tab[:, 8 + d, 0:64], func=AF.Exp, scale=lg), w=[sm["cdrow"]], r=[g.tab, g.lgam])
            k.op("dve", lambda e: e.tensor_scalar(out=sm["cdrow"][:, 0:64], in0=sm["cdrow"][:, 0:64], scalar1=128.0 ** -0.5, scalar2=None, op0=ALU.mult),
                 w=[sm["cdrow"]], r=[sm["cdrow"]])
            k.op("act", lambda e: e.activation(out=sm["rc"][0:64, 0:1], in_=g.tab[0:64, 10, d:d + 1], func=AF.Exp, scale=lg[0:64, :]), w=[sm["rc"]], r=[g.tab, g.lgam])
            k.op("act", lambda e: e.activation(out=sm["rc"][:, 1:2], in_=g.tab[:, 11, 0:1], func=AF.Exp, scale=lg, bias=0.0), w=[sm["rc"]], r=[g.tab, g.lgam])
            k.op("dve", lambda e: e.tensor_tensor(out=sm["rc"][:, 1:2], in0=sm["rc"][:, 1:2], in1=sm["rc"][:, 1:2], op=ALU.mult), w=[sm["rc"]], r=[sm["rc"]])
            for _ in range(5):
                k.op("dve", lambda e: e.tensor_tensor(out=sm["rc"][:, 1:2], in0=sm["rc"][:, 1:2], in1=sm["rc"][:, 1:2], op=ALU.mult), w=[sm["rc"]], r=[sm["rc"]])
            si = 0
            if init is None:
                k.op("dve", lambda e: e.memset(S[0][:], 0.0), w=[S[0]])
                k.op("dve", lambda e: e.memset(R[0][:], 0.0), w=[R[0]])
            else:
                k.dma(S[0][:], init[0][j, d], S[0], init[0])
                k.dma(R[0][:], init[1][j, d], R[0], init[1])
            for bi in (range(nb) if d == 0 else reversed(range(nb))):
                t0 = bi * B
                for i, nm in enumerate(("qa", "ka", "va")):
                    lo = max(t0 - 1, 0)
                    hi = min(t0 + B + 1, L)
                    if lo > t0 - 1:
                        k.op("dve", lambda e, i=i: e.memset(raw[i][:, 0:1], 0.0), w=[raw[i]])
                    if hi < t0 + B + 1:
                        k.op("dve", lambda e, i=i: e.memset(raw[i][:, B + 1:B + 2], 0.0), w=[raw[i]])
                    k.dma(raw[i][:, lo - (t0 - 1):hi - (t0 - 1)], P[rows[nm]:rows[nm] + 128, tok0 + lo:tok0 + hi], raw[i], P)
                    w3 = cwt[:, cwi[i], :]
                    k.op("dve", lambda e, i=i, w3=w3: e.tensor_scalar(out=tmp[:], in0=raw[i][:, 0:B], scalar1=w3[:, 0:1], scalar2=None, op0=ALU.mult), w=[tmp], r=[raw[i]])
                    k.op("dve", lambda e, i=i, w3=w3: e.scalar_tensor_tensor(out=tmp[:], in0=raw[i][:, 1:B + 1], scalar=w3[:, 1:2], in1=tmp[:], op0=ALU.mult, op1=ALU.add), w=[tmp], r=[raw[i], tmp])
                    k.op("dve", lambda e, i=i, w3=w3: e.scalar_tensor_tensor(out=tmp[:], in0=raw[i][:, 2:B + 2], scalar=w3[:, 2:3], in1=tmp[:], op0=ALU.mult, op1=ALU.add), w=[tmp], r=[raw[i], tmp])
                    k.op("act", lambda e, i=i: e.activation(out=cv[i][:], in_=tmp[:], func=AF.Silu), w=[cv[i]], r=[tmp])
                    if i < 2:
                        k.op("act", lambda e, i=i: e.activation(out=tmp[:], in_=cv[i][:], func=AF.Square), w=[tmp], r=[cv[i]])
                        ps = k.ps()
                        k.op("pe", lambda e: e.matmul(ps[:, 0:B], lhsT=g.tab[:, 11, :], rhs=tmp[:], start=True, stop=True), w=[ps], r=[g.tab, tmp])
                        k.op("act", lambda e: e.activation(out=tmp[:], in_=ps[:, 0:B], func=AF.Sqrt, bias=EPS), w=[tmp], r=[ps])
                        k.op("dve", lambda e: e.reciprocal(out=tmp[:], in_=tmp[:]), w=[tmp], r=[tmp])
                        if i == 0:
                            k.op("dve", lambda e, i=i: e.scalar_tensor_tensor(out=cv[i][:], in0=cv[i][:], scalar=128.0 ** -0.5, in1=tmp[:], op0=ALU.mult, op1=ALU.mult), w=[cv[i]], r=[cv[i], tmp])
                        else:
                            k.op("dve", lambda e, i=i: e.tensor_tensor(out=cv[i][:], in0=cv[i][:], in1=tmp[:], op=ALU.mult), w=[cv[i]], r=[cv[i], tmp])
                for i, nm in enumerate(("qb", "kb", "vb")):
                    k.dma(rb[i][:], P[rows[nm]:rows[nm] + 128, tok0 + t0:tok0 + t0 + B], rb[i], P)
                import os
                for dd in (range(0) if os.environ.get("ABCUT2") == "1" else range(2)):
                    k.dma(AB[dd:dd + 1, :], P[rows["a"][dd]:rows["a"][dd] + 1, tok0 + t0:tok0 + t0 + B], AB, P)
                    k.dma(AB[32 + dd:33 + dd, :], P[rows["b"][dd]:rows["b"][dd] + 1, tok0 + t0:tok0 + t0 + B], AB, P)
                k.op("act", lambda e: e.activation(out=AB[0:2, :], in_=AB[0:2, :], func=AF.Exp, bias=g.abc[0:2, j, hd, 1:2]), w=[AB], r=[AB, g.abc])
                k.op("act", lambda e: e.activation(out=AB[0:2, :], in_=AB[0:2, :], func=AF.Ln, bias=1.0), w=[AB], r=[AB])
                k.op("dve", lambda e: e.tensor_scalar(out=AB[0:2, :], in0=AB[0:2, :], scalar1=g.nexp[0:2, j, hd, 0:1], scalar2=None, op0=ALU.mult), w=[AB], r=[AB, g.nexp])
                k.op("act", lambda e: e.activation(out=AB[32:34, :], in_=AB[32:34, :], func=AF.Sigmoid), w=[AB], r=[AB])
                import os
                ABCUT = int(os.environ.get("ABCUT", "0"))
                for ci in (range(ncb) if d == 0 else reversed(range(ncb))):
                    if ABCUT == 1:
                        continue
                    c0 = ci * C
                    sl = slice(c0, c0 + C)
                    tcol = t0 + c0
                    last = 63 if d == 0 else 0
                    S0, S1 = S[si % 2], S[(si + 1) % 2]
                    R0, R1 = R[si % 2], R[(si + 1) % 2]
                    si += 1
                    tm, cols, Gbc, Grow, DT, Dm, t1, AT = (sm[n] for n in ("tm", "cols", "Gbc", "Grow", "DT", "Dm", "t1", "AT"))
                    ps = k.ps()
                    k.op("pe", lambda e: e.matmul(ps[0:64, 0:34], lhsT=AB[0:34, sl], rhs=g.tab[0:34, 0, 0:34], start=True, stop=True), w=[ps], r=[AB, g.tab])
                    k.op("act", lambda e: e.activation(out=tm[0:64, 0:34], in_=ps[0:64, 0:34], func=AF.Copy), w=[tm], r=[ps])
                    ps = k.ps()
                    k.op("pe", lambda e: e.matmul(ps[0:64, 0:2], lhsT=T_(g, 1 + d), rhs=tm[0:64, 0:2], start=True, stop=True), w=[ps], r=[g.tab, tm])
                    k.op("pe", lambda e: e.matmul(ps[0:64, 2:4], lhsT=T_(g, 3 + d), rhs=tm[0:64, 0:2], start=True, stop=True), w=[ps], r=[g.tab, tm])
                    k.op("dve", lambda e: e.tensor_copy(out=cols[0:64, 0:1], in_=ps[0:64, d:d + 1]), w=[cols], r=[ps])
                    k.op("act", lambda e: e.activation(out=cols[0:64, 1:2], in_=ps[0:64, d:d + 1], func=AF.Exp), w=[cols], r=[ps])
                    k.op("act", lambda e: e.activation(out=cols[0:64, 3:4], in_=ps[0:64, 2 + d:3 + d], func=AF.Exp), w=[cols], r=[ps])
                    k.op("dve", lambda e: e.scalar_tensor_tensor(out=cols[0:64, 2:3], in0=cols[0:64, 1:2], scalar=-1.0, in1=tm[0:64, 32 + d:33 + d], op0=ALU.mult, op1=ALU.mult), w=[cols], r=[cols, tm])
                    k.op("dve", lambda e: e.tensor_tensor(out=Gbc[0:64, :], in0=g.tab[0:64, 11, :], in1=tm[0:64, d:d + 1].to_broadcast([64, 128]), op=ALU.mult), w=[Gbc], r=[g.tab, tm])
                    ps = k.ps()
                    k.op("pe", lambda e: e.matmul(ps[:, 0:64], lhsT=Gbc[0:64, :], rhs=T_(g, 1 + d), start=True, stop=True), w=[ps], r=[Gbc, g.tab])
                    k.op("act", lambda e: e.activation(out=Grow[:, 0:64], in_=ps[:, 0:64], func=AF.Exp), w=[Grow], r=[ps])
                    gcb = cols[0:64, 0:1].to_broadcast([64, 64])
                    k.op("dve", lambda e: e.tensor_tensor(out=DT[0:64, 0:64], in0=ps[0:64, 0:64], in1=gcb, op=ALU.subtract), w=[DT], r=[ps, cols])
                    k.op("dve", lambda e: e.tensor_scalar(out=DT[0:64, 0:64], in0=DT[0:64, 0:64], scalar1=0.0, scalar2=None, op0=ALU.min), w=[DT], r=[DT])
                    k.op("act", lambda e: e.activation(out=DT[0:64, 0:64], in_=DT[0:64, 0:64], func=AF.Exp), w=[DT], r=[DT])
                    k.op("dve", lambda e: e.tensor_tensor(out=Dm[0:64, 0:64], in0=ps[0:64, 0:64], in1=gcb, op=ALU.subtract), w=[Dm], r=[ps, cols])
                    k.op("dve", lambda e: e.tensor_scalar(out=Dm[0:64, 0:64], in0=Dm[0:64, 0:64], scalar1=0.0, scalar2=None, op0=ALU.max), w=[Dm], r=[Dm])
                    k.op("act", lambda e: e.activation(out=Dm[0:64, 0:64], in_=Dm[0:64, 0:64], func=AF.Exp, scale=-1.0), w=[Dm], r=[Dm])
                    qaT, kaT, vaT = cv[0][:, sl], cv[1][:, sl], cv[2][:, sl]
                    psk = k.ps()
                    k.op("pe", lambda e: e.matmul(psk[0:64, 0:64], lhsT=kaT, rhs=kaT, start=True, stop=True), w=[psk], r=[cv[1]])
                    k.op("pe", lambda e: e.matmul(psk[0:64, 64:128], lhsT=kaT, rhs=qaT, start=True, stop=True), w=[psk], r=[cv[1], cv[0]])
                    XA, XB, XTA, XTB, TTt = sm["XA"], sm["XB"], sm["XTA"], sm["XTB"], sm["TT"]
                    k.op("dve", lambda e: e.tensor_tensor(out=t1[0:64, 0:64], in0=psk[0:64, 0:64], in1=Dm[0:64, 0:64], op=ALU.mult), w=[t1], r=[psk, Dm])
                    k.op("dve", lambda e: e.tensor_tensor(out=t1[0:64, 0:64], in0=t1[0:64, 0:64], in1=tm[0:64, 32 + d:33 + d].to_broadcast([64, 64]), op=ALU.mult), w=[t1], r=[t1, tm])
                    k.op("dve", lambda e: e.tensor_tensor(out=XA[0:64, 0:64], in0=t1[0:64, 0:64], in1=T_(g, 5 + d), op=ALU.mult), w=[XA], r=[t1, g.tab])
                    k.op("dve", lambda e: e.tensor_tensor(out=t1[0:64, 64:128], in0=psk[0:64, 64:128], in1=DT[0:64, 0:64], op=ALU.mult), w=[t1], r=[psk, DT])
                    k.op("dve", lambda e: e.tensor_tensor(out=AT[0:64, 0:64], in0=t1[0:64, 64:128], in1=T_(g, 1 + d), op=ALU.mult), w=[AT], r=[t1, g.tab])
                    ps = k.ps()
                    k.op("pe", lambda e: e.matmul(ps[0:64, 0:64], lhsT=XA[0:64, 0:64], rhs=T_(g, 0), start=True, stop=True), w=[ps], r=[XA, g.tab])
                    k.op("act", lambda e: e.activation(out=XTA[0:64, 0:64], in_=ps[0:64, 0:64], func=AF.Copy), w=[XTA], r=[ps])
                    k.op("dve", lambda e: e.tensor_tensor(out=TTt[0:64, 0:64], in0=ps[0:64, 0:64], in1=T_(g, 0), op=ALU.add), w=[TTt], r=[ps, g.tab])
                    DBG = (sout == (0, 0) and d == 0 and ci == 0 and j == 0 and getattr(g, "dbg", None) is not None)
                    if DBG:
                        for ii, nm in enumerate(("tm", "cols", "Grow", "DT", "Dm", "XA", "AT", "XTA")):
                            dbg_dump(g, ii, sm[nm])
                        dbg_dump(g, 14, cv[0]); dbg_dump(g, 15, cv[1]); dbg_dump(g, 16, cv[2]); dbg_dump(g, 17, AB)
                    X, XT, Xn, XTn = XA, XTA, XB, XTB
                    for jj in range(1, 6):
                        ps = k.ps()
                        k.op("pe", lambda e, XT=XT, X=X: e.matmul(ps[0:64, 0:64], lhsT=XT[0:64, 0:64], rhs=X[0:64, 0:64], start=True, stop=True), w=[ps], r=[XT, X])
                        k.op("act", lambda e, Xn=Xn: e.activation(out=Xn[0:64, 0:64], in_=ps[0:64, 0:64], func=AF.Copy), w=[Xn], r=[ps])
                        if jj < 5:
                            ps2 = k.ps()
                            k.op("pe", lambda e, XT=XT, X=X: e.matmul(ps2[0:64, 0:64], lhsT=X[0:64, 0:64], rhs=XT[0:64, 0:64], start=True, stop=True), w=[ps2], r=[XT, X])
                            k.op("act", lambda e, XTn=XTn: e.activation(out=XTn[0:64, 0:64], in_=ps2[0:64, 0:64], func=AF.Copy), w=[XTn], r=[ps2])
                        ps3 = k.ps()
                        k.op("pe", lambda e, Xn=Xn: e.matmul(ps3[0:64, 0:64], lhsT=Xn[0:64, 0:64], rhs=TTt[0:64, 0:64], start=True, stop=True), w=[ps3], r=[Xn, TTt])
                        k.op("dve", lambda e: e.tensor_tensor(out=TTt[0:64, 0:64], in0=TTt[0:64, 0:64], in1=ps3[0:64, 0:64], op=ALU.add), w=[TTt], r=[TTt, ps3])
                        X, XT, Xn, XTn = Xn, XTn, X, XT
                    kd, bv, qd, rv, vn = sm["kd"], sm["bv"], sm["qd"], sm["rv"], sm["vn"]
                    ps = k.ps()
                    k.op("pe", lambda e: e.matmul(ps[0:64, 0:128], lhsT=kaT, rhs=g.tab[:, 0, :], start=True, stop=True), w=[ps], r=[cv[1], g.tab])
                    k.op("dve", lambda e: e.tensor_tensor(out=kd[0:64, :], in0=ps[0:64, 0:128], in1=cols[0:64, 3:4].to_broadcast([64, 128]), op=ALU.mult), w=[kd], r=[ps, cols])
                    ps = k.ps()
                    k.op("pe", lambda e: e.matmul(ps[0:64, 0:128], lhsT=vaT, rhs=g.tab[:, 0, :], start=True, stop=True), w=[ps], r=[cv[2], g.tab])
                    k.op("dve", lambda e: e.tensor_tensor(out=bv[0:64, :], in0=ps[0:64, 0:128], in1=tm[0:64, 32 + d:33 + d].to_broadcast([64, 128]), op=ALU.mult), w=[bv], r=[ps, tm])
                    k.op("dve", lambda e: e.tensor_tensor(out=qd[:, 0:64], in0=qaT, in1=Grow[:, 0:64], op=ALU.mult), w=[qd], r=[cv[0], Grow])
                    ps = k.ps()
                    k.op("pe", lambda e: e.matmul(ps[0:64, 0:128], lhsT=kaT, rhs=S0[:], start=True, stop=True), w=[ps], r=[cv[1], S0])
                    k.op("dve", lambda e: e.tensor_tensor(out=rv[0:64, :], in0=ps[0:64, 0:128], in1=cols[0:64, 2:3].to_broadcast([64, 128]), op=ALU.mult), w=[rv], r=[ps, cols])
                    k.op("dve", lambda e: e.tensor_tensor(out=rv[0:64, :], in0=rv[0:64, :], in1=bv[0:64, :], op=ALU.add), w=[rv], r=[rv, bv])
                    ps = k.ps()
                    k.op("pe", lambda e: e.matmul(ps[0:64, 0:128], lhsT=TTt[0:64, 0:64], rhs=rv[0:64, :], start=True, stop=True), w=[ps], r=[TTt, rv])
                    k.op("act", lambda e: e.activation(out=vn[0:64, :], in_=ps[0:64, 0:128], func=AF.Copy), w=[vn], r=[ps])
                    ps = k.ps()
                    k.op("pe", lambda e: e.matmul(ps[:, 0:64], lhsT=S0[:], rhs=qd[:, 0:64], start=True, stop=False), w=[ps], r=[S0, qd])
                    k.op("pe", lambda e: e.matmul(ps[:, 0:64], lhsT=vn[0:64, :], rhs=AT[0:64, 0:64], start=False, stop=True), w=[ps], r=[vn, AT])
                    if d == 0:
                        k.op("act", lambda e: e.activation(out=oa[:, tcol:tcol + C], in_=ps[:, 0:64], func=AF.Copy), w=[oa], r=[ps])
                    else:
                        k.op("dve", lambda e: e.tensor_tensor(out=oa[:, tcol:tcol + C], in0=oa[:, tcol:tcol + C], in1=ps[:, 0:64], op=ALU.add), w=[oa], r=[oa, ps])
                    ps = k.ps()
                    k.op("pe", lambda e: e.matmul(ps[:, 0:128], lhsT=kd[0:64, :], rhs=vn[0:64, :], start=True, stop=True), w=[ps], r=[kd, vn])
                    k.op("dve", lambda e: e.tensor_tensor(out=S1[:], in0=S0[:], in1=Grow[:, last:last + 1].to_broadcast([128, 128]), op=ALU.mult), w=[S1], r=[S0, Grow])
                    k.op("dve", lambda e: e.tensor_tensor(out=S1[:], in0=S1[:], in1=ps[:, 0:128], op=ALU.add), w=[S1], r=[S1, ps])
                    if DBG:
                        for ii, nm in enumerate(("TT", "kd", "bv", "qd", "rv", "vn")):
                            dbg_dump(g, 8 + ii, sm[nm])
                        dbg_dump(g, 18, S1)
                    qbT, kbT, vbT = rb[0][:, sl], rb[1][:, sl], rb[2][:, sl]
                    ATb, vtb, ksd, qcd = sm["ATb"], sm["vtb"], sm["ksd"], sm["qcd"]
                    ps = k.ps()
                    k.op("pe", lambda e: e.matmul(ps[0:64, 0:64], lhsT=kbT, rhs=qbT, start=True, stop=True), w=[ps], r=[rb[1], rb[0]])
                    k.op("dve", lambda e: e.tensor_tensor(out=ATb[0:64, 0:64], in0=ps[0:64, 0:64], in1=sm["DmTb"][0:64, 0:64], op=ALU.mult), w=[ATb], r=[ps, sm["DmTb"]])
                    ps = k.ps()
                    k.op("pe", lambda e: e.matmul(ps[0:64, 0:128], lhsT=vbT, rhs=g.tab[:, 0, :], start=True, stop=True), w=[ps], r=[rb[2], g.tab])
                    k.op("act", lambda e: e.activation(out=vtb[0:64, :], in_=ps[0:64, 0:128], func=AF.Copy), w=[vtb], r=[ps])
                    ps = k.ps()
                    k.op("pe", lambda e: e.matmul(ps[0:64, 0:128], lhsT=kbT, rhs=g.tab[:, 0, :], start=True, stop=True), w=[ps], r=[rb[1], g.tab])
                    k.op("dve", lambda e: e.tensor_scalar(out=ksd[0:64, :], in0=ps[0:64, 0:128], scalar1=sm["rc"][0:64, 0:1], scalar2=None, op0=ALU.mult), w=[ksd], r=[ps, sm["rc"]])
                    k.op("dve", lambda e: e.tensor_tensor(out=qcd[:, 0:64], in0=qbT, in1=sm["cdrow"][:, 0:64], op=ALU.mult), w=[qcd], r=[rb[0], sm["cdrow"]])
                    ps = k.ps()
                    k.op("pe", lambda e: e.matmul(ps[:, 0:64], lhsT=R0[:], rhs=qcd[:, 0:64], start=True, stop=False), w=[ps], r=[R0, qcd])
                    k.op("pe", lambda e: e.matmul(ps[:, 0:64], lhsT=vtb[0:64, :], rhs=ATb[0:64, 0:64], start=False, stop=True), w=[ps], r=[vtb, ATb])
                    if d == 0:
                        k.op("act", lambda e: e.activation(out=ob[:, tcol:tcol + C], in_=ps[:, 0:64], func=AF.Copy), w=[ob], r=[ps])
                    else:
                        k.op("dve", lambda e: e.tensor_tensor(out=ob[:, tcol:tcol + C], in0=ob[:, tcol:tcol + C], in1=ps[:, 0:64], op=ALU.add), w=[ob], r=[ob, ps])
                    ps = k.ps()
                    k.op("pe", lambda e: e.matmul(ps[:, 0:128], lhsT=ksd[0:64, :], rhs=vtb[0:64, :], start=True, stop=True), w=[ps], r=[ksd, vtb])
                    k.op("dve", lambda e: e.scalar_tensor_tensor(out=R1[:], in0=R0[:], scalar=sm["rc"][:, 1:2], in1=ps[:, 0:128], op0=ALU.mult, op1=ALU.add), w=[R1], r=[R0, sm["rc"], ps])
            import os
            if sout is not None and os.environ.get("ABCUT3") != "1":
                s_, h_ = sout
                k.dma(g.sd_out[((s_ * 2 + j) * 2 + d) * 4 + h_], S[si % 2][:], g.sd_out, S[si % 2])
                k.dma(g.sr_out[((s_ * 2 + j) * 2 + d) * 4 + h_], R[si % 2][:], g.sr_out, R[si % 2])
        zt = raw[0]
        for t0 in range(0, L, B):
            tl = B
            k.op("act", lambda e: e.activation(out=tmp[:], in_=oa[:, t0:t0 + tl], func=AF.Square), w=[tmp], r=[oa])
            ps = k.ps()
            k.op("pe", lambda e: e.matmul(ps[:, 0:tl], lhsT=g.cstf[:, 2, :], rhs=tmp[:], start=True, stop=True), w=[ps], r=[g.cstf, tmp])
            k.op("act", lambda e: e.activation(out=tmp[:], in_=ps[:, 0:tl], func=AF.Sqrt, bias=EPS), w=[tmp], r=[ps])
            k.op("dve", lambda e: e.reciprocal(out=tmp[:], in_=tmp[:]), w=[tmp], r=[tmp])
            k.op("dve", lambda e: e.tensor_tensor(out=tmp[:], in0=tmp[:], in1=oa[:, t0:t0 + tl], op=ALU.mult), w=[tmp], r=[tmp, oa])
            k.dma(zt[:, 0:tl], P[rows["z"]:rows["z"] + 128, tok0 + t0:tok0 + t0 + tl], zt, P)
            k.op("act", lambda e: e.activation(out=zt[:, 0:tl], in_=zt[:, 0:tl], func=AF.Silu), w=[zt], r=[zt])
            k.op("dve", lambda e: e.scalar_tensor_tensor(out=omix[:, och[0], ooff + t0:ooff + t0 + tl], in0=tmp[:], scalar=g.nab[:, j, 0:1], in1=zt[:, 0:tl],
                                                         op0=ALU.mult, op1=ALU.mult), w=[omix], r=[tmp, g.nab, zt])
            ps = k.ps()
            k.op("pe", lambda e: e.matmul(ps[:, 0:tl], lhsT=g.cstf[:, 2, :], rhs=ob[:, t0:t0 + tl], start=True, stop=True), w=[ps], r=[g.cstf, ob])
            cen = cv[0]
            k.op("dve", lambda e: e.tensor_tensor(out=cen[:, 0:tl], in0=ob[:, t0:t0 + tl], in1=ps[:, 0:tl], op=ALU.subtract), w=[cen], r=[ob, ps])
            k.op("act", lambda e: e.activation(out=tmp[:], in_=cen[:, 0:tl], func=AF.Square), w=[tmp], r=[cen])
            ps = k.ps()
            k.op("pe", lambda e: e.matmul(ps[:, 0:tl], lhsT=g.cstf[:, 2, :], rhs=tmp[:], start=True, stop=True), w=[ps], r=[g.cstf, tmp])
            k.op("act", lambda e: e.activation(out=tmp[:], in_=ps[:, 0:tl], func=AF.Sqrt, bias=EPS), w=[tmp], r=[ps])
            k.op("dve", lambda e: e.reciprocal(out=tmp[:], in_=tmp[:]), w=[tmp], r=[tmp])
            k.op("dve", lambda e: e.tensor_tensor(out=tmp[:], in0=tmp[:], in1=cen[:, 0:tl], op=ALU.mult), w=[tmp], r=[tmp, cen])
            k.dma(zt[:, 0:tl], P[rows["gb"]:rows["gb"] + 128, tok0 + t0:tok0 + t0 + tl], zt, P)
            k.op("act", lambda e: e.activation(out=zt[:, 0:tl], in_=zt[:, 0:tl], func=AF.Silu), w=[zt], r=[zt])
            k.op("dve", lambda e: e.scalar_tensor_tensor(out=omix[:, och[1], ooff + t0:ooff + t0 + tl], in0=tmp[:], scalar=g.nab[:, j, 1:2], in1=zt[:, 0:tl],
                                                         op0=ALU.mult, op1=ALU.mult), w=[omix], r=[tmp, g.nab, zt])


def hy_setup(g, din):
    k = g.k
    g.hy_w_in = din("hy_w_in", [2, 1024, 3072], F32R)
    g.hy_w_in_l = din("hy_w_in_l", [2, 1024, 768], F32R)
    g.hy_w_out = din("hy_w_out", [2, 1024, 1024], F32R)
    g.hy_pc_d = din("hy_pc", [128, 2, 24, 5])
    g.hy_pl_d = din("hy_pl", [128, 2, 6, 5])
    g.hy_fb_d = din("hy_fb", [128, 2, 10])
    g.hy_bo_d = din("hy_bo", [128, 2, 8])
    g.hy_w1_d = din("hy_w1", [33, 2, 64])
    g.hy_w2_d = din("hy_w2", [64, 2, 64])
    g.hy_fp_d = din("hy_fp", [64, 2, 4])
    g.hy_w3c = din("hy_w3c", [64, 2, 2048])
    g.hy_w3l = din("hy_w3l", [64, 2, 512])
    g.hy_zc = din("hy_zc", [33, 256])
    g.hy_zl = din("hy_zl", [33, 4096])
    g.hy_tnc_d = din("hy_tnc", [128, 2, 2])
    g.hy_tnl_d = din("hy_tnl", [128, 32, 2])
    g.hy_dlc = din("hy_dlc", [128, 1024])
    g.hy_dll = din("hy_dll", [128, 256])
    g.hy_wfc_d = din("hy_wfc", [128, 3])
    g.hy_wfl_d = din("hy_wfl", [128, 33])
    g.TCc = din("hy_TCc", [384, 384], F32R)
    g.TSc = din("hy_TSc", [384, 384], F32R)
    g.TCl = din("hy_TCl", [4224, 4224], F32R)
    g.TSl = din("hy_TSl", [4224, 4224], F32R)
    g.DDc = k.dram("DDc", [256, 4096], F32R)
    g.DDl = k.dram("DDl", [4096, 768], F32R)
    g.SPc = k.dram("SPc", [2, 384, 4096], F32)
    g.SPl = k.dram("SPl", [2, 4224, 768], F32)
    g.YSc = k.dram("YSc", [2, 384, 2048], F32R)
    g.YSl = k.dram("YSl", [2, 4224, 256], F32R)
    g.hy_pc = k.sb("hy_pc", [128, 2, 24, 5])
    g.hy_pl = k.sb("hy_pl", [128, 2, 6, 5])
    g.hy_fb = k.sb("hy_fb", [128, 2, 10])
    g.hyb = k.sb("hy_bo", [128, 2, 8])
    g.hy_w1 = k.sb("hy_w1", [33, 2, 64])
    g.hy_w2 = k.sb("hy_w2", [64, 2, 64])
    g.hy_fp = k.sb("hy_fp", [64, 2, 4])
    g.hy_fbb = k.sb("hy_fbb", [64, 2, 2])
    g.hy_tnc = k.sb("hy_tnc", [128, 2, 2])
    g.hy_tnl = k.sb("hy_tnl", [128, 32, 2])
    g.hy_wfc = k.sb("hy_wfc", [128, 3])
    g.hy_wfl = k.sb("hy_wfl", [128, 33])
    for (t, s) in ((g.hy_pc, g.hy_pc_d), (g.hy_pl, g.hy_pl_d), (g.hy_fb, g.hy_fb_d), (g.hyb, g.hy_bo_d), (g.hy_w1, g.hy_w1_d),
                   (g.hy_w2, g.hy_w2_d), (g.hy_fp, g.hy_fp_d), (g.hy_tnc, g.hy_tnc_d), (g.hy_tnl, g.hy_tnl_d),
                   (g.hy_wfc, g.hy_wfc_d), (g.hy_wfl, g.hy_wfl_d)):
        k.dma(t[:], s[:], t, s)

    def post():
        k.op("dve", lambda e: e.tensor_tensor(out=g.hy_fbb[:, :, 0:1], in0=g.hy_fp[:, :, 0:1], in1=g.hy_fp[:, :, 1:2], op=ALU.mult), w=[g.hy_fbb], r=[g.hy_fp])
        k.op("dve", lambda e: e.tensor_tensor(out=g.hy_fbb[:, :, 1:2], in0=g.hy_fp[:, :, 2:3], in1=g.hy_fp[:, :, 3:4], op=ALU.mult), w=[g.hy_fbb], r=[g.hy_fp])
    g.post_setup.append(post)


def hy_conv(g, P, row, tok0, L, t0, B, prm, raw, tmp, out):
    k = g.k
    lo = max(t0 - 1, 0)
    hi = min(t0 + B + 1, L)
    if lo > t0 - 1:
        k.op("dve", lambda e: e.memset(raw[:, 0:1], 0.0), w=[raw])
    if hi < t0 + B + 1:
        k.op("dve", lambda e: e.memset(raw[:, B + 1:B + 2], 0.0), w=[raw])
    k.dma(raw[:, lo - (t0 - 1):hi - (t0 - 1)], P[row:row + 128, tok0 + lo:tok0 + hi], raw, P)
    k.op("dve", lambda e: e.tensor_scalar(out=tmp[:, 0:B], in0=raw[:, 0:B], scalar1=prm[:, 1:2], scalar2=None, op0=ALU.mult), w=[tmp], r=[raw])
    k.op("dve", lambda e: e.scalar_tensor_tensor(out=tmp[:, 0:B], in0=raw[:, 1:B + 1], scalar=prm[:, 2:3], in1=tmp[:, 0:B], op0=ALU.mult, op1=ALU.add), w=[tmp], r=[raw, tmp])
    k.op("dve", lambda e: e.scalar_tensor_tensor(out=tmp[:, 0:B], in0=raw[:, 2:B + 2], scalar=prm[:, 3:4], in1=tmp[:, 0:B], op0=ALU.mult, op1=ALU.add), w=[tmp], r=[raw, tmp])
    k.op("dve", lambda e: e.tensor_scalar(out=out[:, 0:B], in0=tmp[:, 0:B], scalar1=prm[:, 4:5], scalar2=None, op0=ALU.add), w=[out], r=[tmp])


def hy_layer(g, l):
    k = g.k
    j = l // 2
    with k.scope() as es:
        h1 = k.sbs(es, "h1c", [128, 8, TC], F32R)
        stg = [k.sbs(es, f"stg{i}", [128, 512]) for i in range(2)]
        normmod(g, g.xc, TC, 0, 0, h1)
        h1l = k.sbs(es, "h1l", [128, 8, TL], F32R)
        normmod(g, g.xl, TL, 1, 0, h1l)
        h1_exchange(g, h1l)
        cnt = [0]

        def mk_evac(P, tcol, prm):
            def evac(ps, mt, ti, m):
                s = stg[cnt[0] % 2]
                cnt[0] += 1
                k.op("act", lambda e: e.activation(out=s[:], in_=ps[:], func=AF.Identity, bias=prm[:, j, mt, 0:1]), w=[s], r=[ps, g.hy_pc, g.hy_pl])
                k.dma(P[mt * 128:(mt + 1) * 128, tcol:tcol + 512], s[:], P, s)
            return evac

        proj(g, g.hy_w_in, g.hy_w_in[j], 8, 3072, lambda kc, ti: (h1[:, kc, :], h1), [(0, 512)], mk_evac(g.Pc, 0, g.hy_pc))
        hb = h1
        for tb in range(8):
            k.dma(hb[:], g.h1_out[:, tb // 2, :, (tb % 2) * 512:(tb % 2) * 512 + 512].rearrange("k p t -> p k t"), hb, g.h1_out)
            proj(g, g.hy_w_in_l, g.hy_w_in_l[j], 8, 768, lambda kc, ti: (hb[:, kc, :], hb), [(0, 512)], mk_evac(g.Pl, tb * 512, g.hy_pl), cbw=384)
    with k.scope() as es:
        omix = k.sbs(es, "omix", [128, 8, TC], F32R)
        cfg = dict(L=256, nseq=2, nch=8, P=g.Pc, prm=g.hy_pc, fbo=0, zT=g.hy_zc, tn=g.hy_tnc, dl=g.hy_dlc, wf=g.hy_wfc,
                   TC=g.TCc, TS=g.TSc, DD=g.DDc, SP=g.SPc, YS=g.YSc, w3=g.hy_w3c, omix=omix, oseq=256)
        hy_core(g, j, cfg)

        def evac_o(ps, mt, ti, m):
            k.op("act", lambda e: e.activation(out=ps[:, 0:512], in_=ps[:, 0:512], func=AF.Identity, bias=g.hyb[:, j, mt:mt + 1]), w=[ps], r=[ps, g.hyb])
            k.op("dve", lambda e: e.scalar_tensor_tensor(out=g.xc[:, mt, :], in0=ps[:, 0:512], scalar=g.mv[:, 0, 2, mt:mt + 1],
                                                         in1=g.xc[:, mt, :], op0=ALU.mult, op1=ALU.add), w=[g.xc], r=[ps, g.mv, g.xc])
        proj(g, g.hy_w_out, g.hy_w_out[j], 8, 1024, lambda kc, ti: (omix[:, kc, :], omix), [(0, 512)], evac_o)
    with k.scope() as es:
        omixl = k.sbs(es, "omixl", [128, 2, 4096], F32R)
        cfg = dict(L=4096, nseq=1, nch=2, P=g.Pl, prm=g.hy_pl, fbo=8, zT=g.hy_zl, tn=g.hy_tnl, dl=g.hy_dll, wf=g.hy_wfl,
                   TC=g.TCl, TS=g.TSl, DD=g.DDl, SP=g.SPl, YS=g.YSl, w3=g.hy_w3l, omix=omixl, oseq=4096)
        hy_core(g, j, cfg)
        o_exchange(g, omixl)
    mix_out_lat(g, g.hy_w_out, g.hy_w_out[j], g.hyb[:, j, :], kcmap=lambda a, i: 2 * i + a)


def hy_core(g, j, c):
    k = g.k
    L, nseq, nch, P, DD, SP, YS = c["L"], c["nseq"], c["nch"], c["P"], c["DD"], c["SP"], c["YS"]
    ntc = L // 128
    nft = ntc + 1
    W = nch * 128
    ndata = nseq * W
    fcol = ndata
    ncols = ndata + 2 * W
    prm = c["prm"]
    with k.scope() as es:
        zt = k.sbs(es, "zt", [33, L])
        hid1 = k.sbs(es, "hid1", [64, L])
        hid2 = k.sbs(es, "hid2", [64, L])
        msk = k.sbs(es, "msk", [64, 512])
        w3t = k.sbs(es, "w3t", [64, 2 * W])
        dlt = k.sbs(es, "dlt", [128, W])
        win = k.sbs(es, "win", [128, W])
        flt = k.sbs(es, "flt", [128, 2 * W])
        cmb = k.sbs(es, "cmb", [128, 2 * W], F32R)
        k.dma(zt[:], c["zT"][:], zt, c["zT"])
        k.dma(w3t[:], c["w3"][:, j, :], w3t, c["w3"])
        k.dma(dlt[:], c["dl"][:], dlt, c["dl"])
        for (src, srcb, dst, wt, kk, fi) in ((zt, zt, hid1, g.hy_w1, 33, 0), (hid1, hid1, hid2, g.hy_w2, 64, 1)):
            for b0 in range(0, L, 512):
                bl = min(512, L - b0)
                ps = k.ps()
                k.op("pe", lambda e: e.matmul(ps[0:64, 0:bl], lhsT=wt[0:kk, j, :], rhs=src[0:kk, b0:b0 + bl], start=True, stop=True), w=[ps], r=[wt, srcb])
                d_ = dst[:, b0:b0 + bl]
                k.op("act", lambda e: e.activation(out=d_, in_=ps[0:64, 0:bl], func=AF.Identity, scale=g.hy_fp[:, j, 2 * fi:2 * fi + 1],
                                                   bias=g.hy_fbb[:, j, fi:fi + 1]), w=[dst], r=[ps, g.hy_fp, g.hy_fbb])
                for _ in range(2):
                    k.op("dve", lambda e: e.tensor_scalar(out=msk[:, 0:bl], in0=d_, scalar1=math.pi, scalar2=None, op0=ALU.is_gt), w=[msk], r=[dst])
                    k.op("dve", lambda e: e.scalar_tensor_tensor(out=d_, in0=msk[:, 0:bl], scalar=-2.0 * math.pi, in1=d_, op0=ALU.mult, op1=ALU.add), w=[dst], r=[msk, dst])
                    k.op("dve", lambda e: e.tensor_scalar(out=msk[:, 0:bl], in0=d_, scalar1=-math.pi, scalar2=None, op0=ALU.is_lt), w=[msk], r=[dst])
                    k.op("dve", lambda e: e.scalar_tensor_tensor(out=d_, in0=msk[:, 0:bl], scalar=2.0 * math.pi, in1=d_, op0=ALU.mult, op1=ALU.add), w=[dst], r=[msk, dst])
                k.op("act", lambda e: e.activation(out=d_, in_=d_, func=AF.Sin), w=[dst], r=[dst])
        for tc in range(ntc):
            for c0 in range(0, 2 * W, 512):
                ps = k.ps()
                k.op("pe", lambda e: e.matmul(ps[:, 0:512], lhsT=hid2[:, tc * 128:(tc + 1) * 128], rhs=w3t[:, c0:c0 + 512], start=True, stop=True), w=[ps], r=[hid2, w3t])
                k.op("act", lambda e: e.activation(out=flt[:, c0:c0 + 512], in_=ps[:, 0:512], func=AF.Copy), w=[flt], r=[ps])
            k.op("act", lambda e: e.activation(out=win[:], in_=dlt[:], func=AF.Exp, scale=c["tn"][:, tc, 0:1]), w=[win], r=[dlt, c["tn"]])
            k.op("dve", lambda e: e.tensor_tensor(out=flt[:, 0:W], in0=flt[:, 0:W], in1=win[:], op=ALU.mult), w=[flt], r=[flt, win])
            k.op("dve", lambda e: e.scalar_tensor_tensor(out=flt[:, W:2 * W], in0=flt[:, W:2 * W], scalar=c["tn"][:, tc, 1:2], in1=win[:], op0=ALU.mult, op1=ALU.mult),
                 w=[flt], r=[flt, win, c["tn"]])
            k.op("dve", lambda e: e.tensor_tensor(out=cmb[:, 0:W], in0=flt[:, 0:W], in1=flt[:, W:2 * W], op=ALU.add), w=[cmb], r=[flt])
            k.op("dve", lambda e: e.tensor_tensor(out=cmb[:, W:2 * W], in0=flt[:, W:2 * W], in1=flt[:, 0:W], op=ALU.subtract), w=[cmb], r=[flt])
            k.dma(DD[tc * 128:(tc + 1) * 128, fcol:fcol + 2 * W], cmb[:], DD, cmb)
    B = 256
    with k.scope() as es:
        raw = k.sbs(es, "hraw", [128, B + 2])
        tmp = k.sbs(es, "htmp", [128, B])
        x1c = k.sbs(es, "x1c", [128, B])
        vc = k.sbs(es, "vc", [128, B])
        tmr = [k.sbs(es, f"tmr{i}", [128, W], F32R) for i in range(2)]
        for s in range(nseq):
            for t0 in range(0, L, B):
                for ct in range(nch):
                    hy_conv(g, P, (nch + ct) * 128, s * L, L, t0, B, prm[:, j, nch + ct, :], raw, tmp, x1c)
                    hy_conv(g, P, (2 * nch + ct) * 128, s * L, L, t0, B, prm[:, j, 2 * nch + ct, :], raw, tmp, vc)
                    k.op("dve", lambda e: e.tensor_tensor(out=vc[:], in0=vc[:], in1=x1c[:], op=ALU.mult), w=[vc], r=[vc, x1c])
                    for sb in range(2):
                        ps = k.ps()
                        k.op("pe", lambda e: e.matmul(ps[:, 0:128], lhsT=vc[:, sb * 128:(sb + 1) * 128], rhs=g.tab[:, 0, :], start=True, stop=True), w=[ps], r=[vc, g.tab])
                        k.op("act", lambda e: e.activation(out=tmr[sb][:, ct * 128:(ct + 1) * 128], in_=ps[:, 0:128], func=AF.Copy), w=[tmr[sb]], r=[ps])
                for sb in range(2):
                    k.dma(DD[t0 + sb * 128:t0 + (sb + 1) * 128, s * W:(s + 1) * W], tmr[sb][:], DD, tmr[sb])
    with k.scope() as es:
        CG = 256
        ddg = k.sbs(es, "ddg", [128, ntc, CG], F32R)
        tt = k.sbs(es, "tt", [128, ntc, 128], F32R)
        st = [k.sbs(es, f"fst{i}", [128, 512]) for i in range(2)]
        si = 0
        for c0 in range(0, ncols, CG):
            cw = min(CG, ncols - c0)
            k.dma(ddg[:, :, 0:cw], DD[:, c0:c0 + cw].rearrange("(t p) c -> p t c", p=128), ddg, DD)
            for ft in range(nft):
                for Ti, T in enumerate((c["TC"], c["TS"])):
                    k.dma(tt[:], T[0:L, ft * 128:(ft + 1) * 128].rearrange("(t p) f -> p t f", p=128), tt, T)
                    ps = k.ps()
                    for tc in range(ntc):
                        k.op("pe", lambda e, tc=tc: e.matmul(ps[:, 0:cw], lhsT=tt[:, tc, :], rhs=ddg[:, tc, 0:cw], start=(tc == 0), stop=(tc == ntc - 1)), w=[ps], r=[tt, ddg])
                    s_ = st[si % 2]
                    si += 1
                    k.op("act", lambda e: e.activation(out=s_[:, 0:cw], in_=ps[:, 0:cw], func=AF.Copy), w=[s_], r=[ps])
                    k.dma(SP[Ti, ft * 128:(ft + 1) * 128, c0:c0 + cw], s_[:, 0:cw], SP, s_)
    with k.scope() as es:
        uc = k.sbs(es, "uc", [128, W])
        us = k.sbs(es, "us", [128, W])
        kr = k.sbs(es, "kr", [128, W])
        ki = k.sbs(es, "ki", [128, W])
        t1 = k.sbs(es, "yt1", [128, W])
        ya = k.sbs(es, "ya", [128, W], F32R)
        yb = k.sbs(es, "yb", [128, W], F32R)
        for ft in range(nft):
            rs = slice(ft * 128, (ft + 1) * 128)
            k.dma(kr[:], SP[0, rs, fcol:fcol + W], kr, SP)
            k.dma(ki[:], SP[1, rs, fcol + W:fcol + 2 * W], ki, SP)
            for s in range(nseq):
                k.dma(uc[:], SP[0, rs, s * W:(s + 1) * W], uc, SP)
                k.dma(us[:], SP[1, rs, s * W:(s + 1) * W], us, SP)
                wfc = c["wf"][:, ft:ft + 1]
                k.op("dve", lambda e: e.tensor_tensor(out=t1[:], in0=uc[:], in1=kr[:], op=ALU.mult), w=[t1], r=[uc, kr])
                k.op("dve", lambda e: e.tensor_tensor(out=ya[:], in0=us[:], in1=ki[:], op=ALU.mult), w=[ya], r=[us, ki])
                k.op("dve", lambda e: e.tensor_tensor(out=t1[:], in0=t1[:], in1=ya[:].bitcast(F32), op=ALU.add), w=[t1], r=[t1, ya])
                k.op("dve", lambda e: e.tensor_scalar(out=ya[:], in0=t1[:], scalar1=wfc, scalar2=None, op0=ALU.mult), w=[ya], r=[t1, c["wf"]])
                k.op("dve", lambda e: e.tensor_tensor(out=t1[:], in0=us[:], in1=kr[:], op=ALU.mult), w=[t1], r=[us, kr])
                k.op("dve", lambda e: e.tensor_tensor(out=yb[:], in0=uc[:], in1=ki[:], op=ALU.mult), w=[yb], r=[uc, ki])
                k.op("dve", lambda e: e.tensor_tensor(out=t1[:], in0=t1[:], in1=yb[:].bitcast(F32), op=ALU.subtract), w=[t1], r=[t1, yb])
                k.op("dve", lambda e: e.tensor_scalar(out=yb[:], in0=t1[:], scalar1=wfc, scalar2=None, op0=ALU.mult), w=[yb], r=[t1, c["wf"]])
                k.dma(YS[0, rs, s * W:(s + 1) * W], ya[:], YS, ya)
                k.dma(YS[1, rs, s * W:(s + 1) * W], yb[:], YS, yb)
    TB = min(512, L)
    with k.scope() as es:
        YA = k.sbs(es, "YA", [128, ndata], F32R)
        YB = k.sbs(es, "YB", [128, ndata], F32R)
        tcC = k.sbs(es, "tcC", [128, TB], F32R)
        tcS = k.sbs(es, "tcS", [128, TB], F32R)
        raw = k.sbs(es, "iraw", [128, TB + 2])
        tmp = k.sbs(es, "itmp", [128, TB])
        x0c = k.sbs(es, "ix0", [128, TB])
        x1c = k.sbs(es, "ix1", [128, TB])
        vc = k.sbs(es, "ivc", [128, TB])
        units = [(s, ct) for s in range(nseq) for ct in range(nch)]
        for t0 in range(0, L, TB):
            for u0 in range(0, len(units), 4):
                grp = units[u0:u0 + 4]
                pss = [k.ps() for _ in grp]
                for fc in range(nft):
                    k.dma(tcC[:], c["TC"][fc * 128:(fc + 1) * 128, t0:t0 + TB], tcC, c["TC"])
                    k.dma(tcS[:], c["TS"][fc * 128:(fc + 1) * 128, t0:t0 + TB], tcS, c["TS"])
                    k.dma(YA[:], YS[0, fc * 128:(fc + 1) * 128, :], YA, YS)
                    k.dma(YB[:], YS[1, fc * 128:(fc + 1) * 128, :], YB, YS)
                    for (s, ct), ps in zip(grp, pss):
                        col = s * W + ct * 128
                        k.op("pe", lambda e, ps=ps, col=col: e.matmul(ps[:, 0:TB], lhsT=YA[:, col:col + 128], rhs=tcC[:], start=(fc == 0), stop=False), w=[ps], r=[YA, tcC])
                        k.op("pe", lambda e, ps=ps, col=col: e.matmul(ps[:, 0:TB], lhsT=YB[:, col:col + 128], rhs=tcS[:], start=False, stop=(fc == nft - 1)), w=[ps], r=[YB, tcS])
                for (s, ct), ps in zip(grp, pss):
                    hy_conv(g, P, ct * 128, s * L, L, t0, TB, prm[:, j, ct, :], raw, tmp, x0c)
                    hy_conv(g, P, (nch + ct) * 128, s * L, L, t0, TB, prm[:, j, nch + ct, :], raw, tmp, x1c)
                    hy_conv(g, P, (2 * nch + ct) * 128, s * L, L, t0, TB, prm[:, j, 2 * nch + ct, :], raw, tmp, vc)
                    k.op("dve", lambda e: e.tensor_tensor(out=vc[:], in0=vc[:], in1=x1c[:], op=ALU.mult), w=[vc], r=[vc, x1c])
                    k.op("dve", lambda e, ps=ps, ct=ct: e.scalar_tensor_tensor(out=vc[:], in0=vc[:], scalar=g.hy_fb[:, j, c["fbo"] + ct:c["fbo"] + ct + 1], in1=ps[:, 0:TB],
                                                                               op0=ALU.mult, op1=ALU.add), w=[vc], r=[vc, g.hy_fb, ps])
                    oo = c["omix"][:, ct, s * c["oseq"] + t0:s * c["oseq"] + t0 + TB]
                    k.op("dve", lambda e, oo=oo: e.tensor_tensor(out=oo, in0=vc[:], in1=x0c[:], op=ALU.mult), w=[c["omix"]], r=[vc, x0c])


def _hy_tables(L):
    Lp = L + 128
    a = np.arange(Lp, dtype=np.float64)
    ang = 2.0 * np.pi * np.outer(a, a) / (2.0 * L)
    return np.cos(ang).astype(np.float32), np.sin(ang).astype(np.float32)


def _hy_z(L):
    t = np.linspace(0.0, 1.0, L, dtype=np.float32)[:, None]
    wpos = (2.0 * math.pi * np.arange(L, dtype=np.float32)[:, None] / L).astype(np.float32)
    bands = np.linspace(1e-4, 16 - 1, 16, dtype=np.float32)[None, :]
    z = np.concatenate([t, np.cos(bands * wpos), -np.sin(bands * wpos)], axis=-1).astype(np.float32)
    return np.ascontiguousarray(z.T), t[:, 0]


def _hy_host(inp, m, core):
    f32 = np.float32
    if core is None:
        m["hy_w_in"] = np.ascontiguousarray(inp["hy_w_in"], f32)
        m["hy_w_out"] = np.ascontiguousarray(inp["hy_w_out"], f32)
        par = np.stack([inp["hy_b_in"], inp["hy_conv_w"][:, 0], inp["hy_conv_w"][:, 1], inp["hy_conv_w"][:, 2], inp["hy_conv_b"]], 1)
        m["hy_pc"] = np.ascontiguousarray(np.moveaxis(fm(par), 2, 3))
        m["hy_bo"] = fm(inp["hy_b_out"])
        m["hy_w1"] = np.ascontiguousarray(np.moveaxis(np.asarray(inp["hy_f_w1"], f32), 0, 1))
        m["hy_w2"] = np.ascontiguousarray(np.moveaxis(np.asarray(inp["hy_f_w2"], f32), 0, 1))
        fp = np.stack([inp["hy_f_freq1"], inp["hy_f_b1"], inp["hy_f_freq2"], inp["hy_f_b2"]], -1)
        m["hy_fp"] = np.ascontiguousarray(np.moveaxis(np.asarray(fp, f32), 0, 1))
        m["hy_w3c"] = np.ascontiguousarray(np.moveaxis(np.asarray(inp["hy_f_w3"], f32), 0, 1))
        min_decay = math.log(1e-2) / 1.5
        max_decay = math.log(1e-2) / 0.3
        deltas = np.abs(np.linspace(min_decay, max_decay, 1024, dtype=f32))
        m["_deltas"] = deltas
        m["hy_dlc"] = np.ascontiguousarray(np.broadcast_to(deltas[None, :], (128, 1024)), f32)
        for nm, L in (("c", 256), ("l", 4096)):
            zT, t = _hy_z(L)
            m["hy_z" + nm] = zT
            ntc = L // 128
            tn = np.zeros((128, ntc, 2), f32)
            tn[:, :, 0] = -t.reshape(ntc, 128).T
            tn[:, :, 1] = 1.0
            tn[0, 0, 1] = 0.0
            m["hy_tn" + nm] = tn
            nft = ntc + 1
            f = np.arange(nft * 128)
            wf = np.where(f <= L, 2.0, 0.0)
            wf[0] = 1.0
            wf[L] = 1.0
            wf = (wf / (2.0 * L)).astype(f32)
            m["hy_wf" + nm] = np.ascontiguousarray(wf.reshape(nft, 128).T)
            Ct, St = _hy_tables(L)
            m["hy_TC" + nm] = Ct
            m["hy_TS" + nm] = St
        return
    c, gi, r = core
    sl = slice(256 * r, 256 * (r + 1))
    cols = np.r_[np.arange(256 * r, 256 * r + 256), 1024 + np.arange(256 * r, 256 * r + 256), 2048 + np.arange(256 * r, 256 * r + 256)]
    m["hy_w_in_l"] = np.ascontiguousarray(np.asarray(inp["hy_w_in"], f32)[:, :, cols])
    tiles = [2 * r, 2 * r + 1, 8 + 2 * r, 8 + 2 * r + 1, 16 + 2 * r, 16 + 2 * r + 1]
    m["hy_pl"] = np.ascontiguousarray(m["hy_pc"][:, :, tiles, :])
    fb = fm(inp["hy_f_bias"])
    m["hy_fb"] = np.ascontiguousarray(np.concatenate([fb, fb[:, :, 2 * r:2 * r + 2]], -1))
    w3 = np.asarray(inp["hy_f_w3"], f32)
    w3l = np.concatenate([w3[:, :, sl], w3[:, :, 1024 + 256 * r:1024 + 256 * (r + 1)]], -1)
    m["hy_w3l"] = np.ascontiguousarray(np.moveaxis(w3l, 0, 1))
    m["hy_dll"] = np.ascontiguousarray(np.broadcast_to(m["_deltas"][None, sl], (128, 256)), f32)


HY_HOST = _hy_host


MIXERS = 2


def kernel(**inputs):
    inp = {n: np.asarray(v) for n, v in inputs.items()}
    kbld = build(nlayers=4, do_mix=MIXERS, raw=False)
    maps = prep(inp)
    used = {b.name for b in kbld.bufs if getattr(b, "kind", None) == "ExternalInput"}
    maps = [{n: v for n, v in m.items() if n in used} for m in maps]
    res = run_bass_kernel_spmd(kbld.nc, maps, core_ids=list(range(8)))
    yp, ys = assemble(res.results)
    sd, sr = assemble_states(res.results)
    return yp, ys, sd, sr
```

```python
import numpy as np
import contextlib
import concourse.bass as bass
import concourse.mybir as mybir
from concourse.bass_utils import run_bass_kernel_spmd

F32 = mybir.dt.float32
F32R = mybir.dt.float32r
BF16 = mybir.dt.bfloat16
AF = mybir.ActivationFunctionType
ALU = mybir.AluOpType


import os as _os
SERIAL = _os.environ.get("SERIAL", "1") == "1"
FENCE = _os.environ.get("FENCE", "0") == "1"


class Buf:
    def __init__(self, kb, name, t):
        self.kb = kb
        self.name = name
        self.t = t
        self.w = {}
        self.is_dram = False
        self.r = {}
        self.dsem = {}
        self.used = False

    def __getitem__(self, idx):
        return self.t[idx]


class KB:
    def __init__(self):
        self.nc = bass.Bass("TRN2", target_bir_lowering=False)
        nc = self.nc
        self.eng = {"pe": nc.tensor, "act": nc.scalar, "dve": nc.vector, "pool": nc.gpsimd, "sp": nc.sync}
        self.sems = {}
        self.cnt = {}
        for e in self.eng:
            self.sems[e] = nc.alloc_semaphore("sem_" + e)
            self.cnt[e] = 0
        self.sems["cc"] = nc.alloc_semaphore("sem_cc")
        self.cnt["cc"] = 0
        self.seen = {e: {} for e in self.eng}
        self.nbuf = 0
        self.bufs = []
        self.free_dsems = {}
        self.psums = []
        self.pi = 0
        self.dq = 0
        self.group = None
        self.serial = SERIAL
        self.fence = False
        self.ftile = None

    def sb(self, name, shape, dt=F32):
        self.nbuf += 1
        b = Buf(self, name, self.nc.alloc_sbuf_tensor(f"{name}_{self.nbuf}", list(shape), dt))
        self.bufs.append(b)
        return b

    def sbs(self, es, name, shape, dt=F32):
        self.nbuf += 1
        t = es.enter_context(self.nc.sbuf_tensor(f"{name}_{self.nbuf}", list(shape), dt))
        b = Buf(self, name, t)
        self.bufs.append(b)
        return b

    def dram(self, name, shape, dt=F32, kind=None):
        if kind is None:
            t = self.nc.dram_tensor(name, list(shape), dt)
        else:
            t = self.nc.dram_tensor(name, list(shape), dt, kind=kind)
        b = Buf(self, name, t)
        b.is_dram = True
        b.kind = kind
        self.bufs.append(b)
        return b

    def init_psum(self):
        self.ftile = self.nc.alloc_sbuf_tensor("fence_t", [128, 8], F32)
        for i in range(8):
            t = self.nc.alloc_psum_tensor(f"ps{i}", [128, 512], F32)
            b = Buf(self, f"ps{i}", t)
            self.psums.append(b)

    def ps(self):
        b = self.psums[self.pi % 8]
        self.pi += 1
        return b

    def _wait(self, e, key, val):
        if val <= 0:
            return
        if self.seen[e].get(key, 0) >= val:
            return
        self.seen[e][key] = val
        self.eng[e].wait_ge(self.sems[key], val)

    def _deps(self, e, r, w, skipkey=None):
        for b in r:
            for k, v in b.w.items():
                if not (k == e and e == "pe"):
                    self._wait(e, k, v)
        for b in w:
            for k, v in b.w.items():
                if k == skipkey:
                    continue
                if not (k == e and e == "pe"):
                    self._wait(e, k, v)
            for k, v in b.r.items():
                if k == e:
                    continue
                self._wait(e, k, v)

    def op(self, e, fn, w=(), r=()):
        for b in r:
            b.used = True
        self._deps(e, r, w)
        if self.serial:
            for k_, v_ in list(self.cnt.items()):
                if k_ != e:
                    self._wait(e, k_, v_)
        ins = fn(self.eng[e])
        self.cnt[e] += 1
        ins.then_inc(self.sems[e], 1)
        if self.fence and FENCE and e in ("act", "dve"):
            if e == "dve":
                ins2 = self.eng[e].memset(self.ftile[0:1, 0:1], 0.0)
            else:
                ins2 = self.eng[e].activation(out=self.ftile[0:1, 2:3], in_=self.ftile[0:1, 4:5], func=AF.Copy)
            self.cnt[e] += 1
            ins2.then_inc(self.sems[e], 1)
        v = self.cnt[e]
        for b in r:
            b.r[e] = v
        for b in w:
            b.w = {e: v}
            b.r = {}
        return ins

    def _dsem(self, b, q):
        if q not in b.dsem:
            fl = self.free_dsems.setdefault(q, [])
            if fl:
                key = fl.pop()
            else:
                key = f"d{len(self.sems)}"
                self.sems[key] = self.nc.alloc_semaphore(key)
                self.cnt[key] = 0
            b.dsem[q] = key
        return b.dsem[q]

    @contextlib.contextmanager
    def scope(self):
        n0 = len(self.bufs)
        with contextlib.ExitStack() as es:
            yield es
            self.barrier()
            for b in self.bufs[n0:]:
                for q, key in b.dsem.items():
                    self.free_dsems.setdefault(q, []).append(key)
                b.dsem = {}
            del self.bufs[n0:]

    def dma(self, out_ap, in_ap, w, r, q=None, **kw):
        r.used = True
        if q is None:
            q = "pool" if (out_ap.dtype == F32R or in_ap.dtype == F32R) else "sp"
        sb_side = r if (w.is_dram and not r.is_dram) else w
        if self.group is not None and q == "sp":
            if self.group[0] is None:
                gk = f"d{len(self.sems)}"
                self.sems[gk] = self.nc.alloc_semaphore(gk)
                self.cnt[gk] = 0
                self.group[0] = gk
            key = self.group[0]
            self.group[1].append((w, r))
        else:
            key = self._dsem(sb_side, q)
        self._deps(q, [r], [w], skipkey=key)
        self.eng[q].dma_start(out=out_ap, in_=in_ap, **kw).then_inc(self.sems[key], 16)
        self.cnt[key] += 16
        r.r[key] = self.cnt[key]
        if w.is_dram:
            w.w[key] = self.cnt[key]
        else:
            w.w = {key: self.cnt[key]}
        w.r = {}

    def group_begin(self):
        self.group = [None, []]

    def group_end(self):
        key, bl = self.group
        self.group = None
        for (w, r) in bl:
            if w.is_dram:
                w.w[key] = self.cnt[key]
            else:
                w.w = {key: self.cnt[key]}
            r.r[key] = self.cnt[key]

    def allgather(self, ob, ib, groups, out_ap=None, in_ap=None):
        e = "pool"
        self._deps(e, [ib], [ob])
        if in_ap is None:
            in_ap = ib.t.ap()
        if out_ap is None:
            out_ap = ob.t.ap()
        self.eng[e].collective_compute("AllGather", ALU.bypass, replica_groups=groups,
                                       ins=[in_ap.opt()], outs=[out_ap.opt()]).then_inc(self.sems["cc"])
        self.cnt["cc"] += 1
        ib.r["cc"] = self.cnt["cc"]
        ob.w = dict(ob.w)
        ob.w["cc"] = self.cnt["cc"]
        ob.r = {}

    def barrier(self):
        tot = dict(self.cnt)
        for e in self.eng:
            for k, v in tot.items():
                if k != e:
                    self._wait(e, k, v)

    def finish(self):
        self.barrier()


import math
import os

D = 1024
KC = 8
TC = 512
TL = 1024
TT = TC + TL
DFF = 2816
NF = 22
EPS = 1e-6
GROUPS = [[0, 1, 2, 3], [4, 5, 6, 7]]


class Ctx:
    pass


def build(nlayers=4, do_mix=True, raw=False):
    k = KB()
    nc = k.nc
    k.init_psum()
    g = Ctx()
    g.k = k

    def din(name, shape, dt=F32):
        return k.dram(name, shape, dt, kind="ExternalInput")

    g.xT = din("xT", [8, 128, TT])
    g.cond = din("cond", [128, 16])
    g.mod_w = din("mod_w", [4, 1024, 6144], F32R)
    g.mod_b = din("mod_b", [128, 4, 48])
    g.nrm = din("nrm", [128, 9, 8])
    g.ffn_wg = din("ffn_w_gate", [4, 1024, DFF], F32R)
    g.ffn_wu = din("ffn_w_up", [4, 1024, DFF], F32R)
    g.ffn_wd = din("ffn_w_down", [4, DFF, 1024], F32R)
    g.ffn_cw = din("ffn_cw", [128, 4, NF, 9])
    g.ffn_cb = din("ffn_cb", [128, 4, NF])
    g.cmask = din("cmask", [128, 2])
    g.consts = din("consts", [128, 4, 128])
    g.yT = k.dram("yT", [8, 128, TT], F32, kind="ExternalOutput")
    g.halo_in = k.dram("halo_in", [128, 8, 128], F32R)
    g.halo_out = k.dram("halo_out", [4, 128, 8, 128], F32R)
    import os
    if os.environ.get("BIGSCR"):
        g.big = k.dram("bigscr", [int(os.environ["BIGSCR"]), 1024, 256], F32)
        g.bigt = k.sb("bigt", [128, 256])
        k.dma(g.bigt[:], g.big[0, 0:128, :], g.bigt, g.big)

    g.xc = k.sb("xc", [128, KC, TC])
    g.xl = k.sb("xl", [128, KC, TL])
    g.wb = [k.sb(f"wb{i}", [128, 5632], F32R) for i in range(2)]
    g.wi = 0
    g.cst = k.sb("cst", [128, 4, 128], F32R)
    g.cstf = k.sb("cstf", [128, 4, 128])
    g.sT = k.sb("sT", [128, 8, 2], F32R)
    g.cnd = k.sb("cnd", [128, 16])
    g.modT = k.sb("modT", [128, 48, 2])
    g.modb = k.sb("modb", [128, 4, 48])
    g.nrmt = k.sb("nrmt", [128, 9, 8])
    g.mv = k.sb("mv", [128, 2, 6, 8])
    g.cw = k.sb("cw", [128, 4, NF, 9])
    g.cb = k.sb("cb", [128, 4, NF])
    g.cm = k.sb("cm", [128, 2])

    g.post_setup = []
    k.group_begin()
    if do_mix:
        ab_setup(g, din)
        if do_mix > 1:
            hy_setup(g, din)
    for i in range(8):
        k.dma(g.xc[:, i, :], g.xT[i, :, 0:TC], g.xc, g.xT)
        k.dma(g.xl[:, i, :], g.xT[i, :, TC:TT], g.xl, g.xT)
    k.dma(g.cstf[:], g.consts[:], g.cstf, g.consts)
    k.dma(g.cnd[:], g.cond[:], g.cnd, g.cond)
    k.dma(g.modb[:], g.mod_b[:], g.modb, g.mod_b)
    k.dma(g.nrmt[:], g.nrm[:], g.nrmt, g.nrm)
    k.dma(g.cw[:], g.ffn_cw[:], g.cw, g.ffn_cw)
    k.dma(g.cb[:], g.ffn_cb[:], g.cb, g.ffn_cb)
    k.dma(g.cm[:], g.cmask[:], g.cm, g.cmask)
    k.group_end()
    for f in g.post_setup:
        f()
    k.op("dve", lambda e: e.tensor_copy(out=g.cst[:], in_=g.cstf[:]), w=[g.cst], r=[g.cstf])
    k.op("act", lambda e: e.activation(out=g.sT[:].rearrange("p k j -> p j k"),
                                       in_=g.cnd[:].rearrange("p (j k) -> p j k", j=2), func=AF.Silu),
         w=[g.sT], r=[g.cnd])

    ser = k.serial
    relax = os.environ.get("RELAX", "1") == "1"
    for l in range(nlayers):
        k.serial = ser and not relax
        mod_layer(g, l)
        k.serial = ser
        if do_mix:
            if l % 2 == 0:
                ab_layer(g, l)
            elif do_mix > 1:
                hy_layer(g, l)
        k.serial = ser and not relax
        ffn_layer(g, l)
    final_out(g, raw)
    k.serial = ser
    k.finish()
    return k


def wload(g, Wap, K, cw, src):
    k = g.k
    wb = g.wb[g.wi % 2]
    g.wi += 1
    cwp = ((cw + 127) // 128) * 128
    view = wb[:, 0:K * cwp].rearrange("p (k n) -> p k n", k=K)
    k.dma(view[:, :, 0:cw], Wap.rearrange("(k p) n -> p k n", p=128), wb, src)
    return wb, view


def proj(g, Wsrc, Wap2d, K, ncols, xin, tblocks, evac, cbw=512):
    k = g.k
    for c0 in range(0, ncols, cbw):
        cw = min(cbw, ncols - c0)
        wb, view = wload(g, Wap2d[:, c0:c0 + cw], K, cw, Wsrc)
        nmt = (cw + 127) // 128
        for mt in range(nmt):
            m = min(128, cw - mt * 128)
            for ti, (t0, tl) in enumerate(tblocks):
                ps = k.ps()
                for kc in range(K):
                    ap, buf = xin(kc, ti)
                    k.op("pe", lambda e, kc=kc, ap=ap: e.matmul(ps[:, 0:tl], lhsT=view[:, kc, mt * 128:(mt + 1) * 128],
                                                                 rhs=ap, start=(kc == 0), stop=(kc == K - 1)),
                         w=[ps], r=[wb, buf])
                evac(ps, c0 // 128 + mt, ti, m)


def mod_layer(g, l):
    k = g.k
    ps = k.ps()
    for cb in range(12):
        wb, view = wload(g, g.mod_w[l, :, cb * 512:(cb + 1) * 512], 8, 512, g.mod_w)
        for mt in range(4):
            j = cb * 4 + mt
            for kc in range(8):
                k.op("pe", lambda e, kc=kc, j=j, mt=mt: e.matmul(ps[:, 2 * j:2 * j + 2], lhsT=view[:, kc, mt * 128:(mt + 1) * 128],
                                                                 rhs=g.sT[:, kc, :], start=(kc == 0), stop=(kc == 7)),
                     w=[ps], r=[wb, g.sT])
    for w_ in range(2):
        k.op("dve", lambda e, w_=w_: e.tensor_tensor(out=g.modT[:, :, w_], in0=ps[:, 0:96].rearrange("p (j w) -> p j w", w=2)[:, :, w_],
                                                     in1=g.modb[:, l, :], op=ALU.add), w=[g.modT], r=[ps, g.modb])
    for w_ in range(2):
        for half in range(2):
            nidx = l if half == 0 else 4 + l
            sh = g.modT[:, (3 * half + 0) * 8:(3 * half + 1) * 8, w_]
            sc = g.modT[:, (3 * half + 1) * 8:(3 * half + 2) * 8, w_]
            gt = g.modT[:, (3 * half + 2) * 8:(3 * half + 3) * 8, w_]
            k.op("dve", lambda e, sc=sc, nidx=nidx, half=half, w_=w_: e.scalar_tensor_tensor(
                out=g.mv[:, w_, 3 * half + 0, :], in0=sc, scalar=1.0, in1=g.nrmt[:, nidx, :], op0=ALU.add, op1=ALU.mult),
                 w=[g.mv], r=[g.modT, g.nrmt])
            k.op("dve", lambda e, sh=sh, half=half, w_=w_: e.tensor_copy(out=g.mv[:, w_, 3 * half + 1, :], in_=sh), w=[g.mv], r=[g.modT])
            k.op("dve", lambda e, gt=gt, half=half, w_=w_: e.tensor_copy(out=g.mv[:, w_, 3 * half + 2, :], in_=gt), w=[g.mv], r=[g.modT])


def normmod(g, x, T, which, half, out, tsl=None, ooff=0):
    k = g.k
    with k.scope() as es:
        sq = k.sbs(es, "sq", [128, 8, 512], F32R)
        rstd = k.sbs(es, "rstd", [128, 512])
        tmp = k.sbs(es, "tmp", [128, 512])
        for t0 in range(0, T, 512):
            tl = min(512, T - t0)
            for kc in range(8):
                k.op("act", lambda e, kc=kc: e.activation(out=sq[:, kc, 0:tl], in_=x[:, kc, t0:t0 + tl], func=AF.Square),
                     w=[sq], r=[x])
            ps = k.ps()
            for kc in range(8):
                k.op("pe", lambda e, kc=kc: e.matmul(ps[:, 0:tl], lhsT=g.cst[:, 1, :], rhs=sq[:, kc, 0:tl],
                                                     start=(kc == 0), stop=(kc == 7)), w=[ps], r=[g.cst, sq])
            k.op("act", lambda e: e.activation(out=rstd[:, 0:tl], in_=ps[:, 0:tl], func=AF.Sqrt, bias=EPS), w=[rstd], r=[ps])
            k.op("dve", lambda e: e.reciprocal(out=rstd[:, 0:tl], in_=rstd[:, 0:tl]), w=[rstd], r=[rstd])
            for kc in range(8):
                k.op("dve", lambda e, kc=kc: e.tensor_tensor(out=tmp[:, 0:tl], in0=x[:, kc, t0:t0 + tl], in1=rstd[:, 0:tl], op=ALU.mult),
                     w=[tmp], r=[x, rstd])
                k.op("act", lambda e, kc=kc: e.activation(out=out[:, kc, ooff + t0:ooff + t0 + tl], in_=tmp[:, 0:tl], func=AF.Identity,
                                                          scale=g.mv[:, which, 3 * half + 0, kc:kc + 1],
                                                          bias=g.mv[:, which, 3 * half + 1, kc:kc + 1]),
                     w=[out], r=[tmp, g.mv])


def ffn_layer(g, l):
    k = g.k
    with k.scope() as es:
        h2 = k.sbs(es, "h2", [128, 8, TC], F32R)
        aT = k.sbs(es, "aT", [128, NF, TC], F32R)
        gb = k.sbs(es, "gb", [128, TC])
        acc = k.sbs(es, "acc", [128, TC])
        normmod(g, g.xc, TC, 0, 1, h2)
        ffn_core(g, l, h2, aT, gb, acc, [(0, TC)], grid=False, which=0, x=g.xc, xoff=0)
    with k.scope() as es:
        h2 = k.sbs(es, "h2l", [128, 8, 64 + TL + 64], F32R)
        ed = k.sbs(es, "ed", [128, 8, 128], F32R)
        hp = k.sbs(es, "hp", [128, 8, 128], F32R)
        normmod(g, g.xl, TL, 1, 1, h2, ooff=64)
        k.op("dve", lambda e: e.tensor_copy(out=ed[:, :, 0:64], in_=h2[:, :, 64:128].bitcast(F32)), w=[ed], r=[h2])
        k.op("dve", lambda e: e.tensor_copy(out=ed[:, :, 64:128], in_=h2[:, :, 64 + TL - 64:64 + TL].bitcast(F32)), w=[ed], r=[h2])
        k.dma(g.halo_in[:], ed[:], g.halo_in, ed)
        k.allgather(g.halo_out, g.halo_in, GROUPS)
        pid = nc_pid(g)
        rp = (pid + 3) % 4
        rn = (pid + 1) % 4
        k.dma(hp[:, :, 0:64], g.halo_out[bass.ds(rp, 1), :, :, 64:128].rearrange("o p k t -> p (o k) t"), hp, g.halo_out)
        k.dma(hp[:, :, 64:128], g.halo_out[bass.ds(rn, 1), :, :, 0:64].rearrange("o p k t -> p (o k) t"), hp, g.halo_out)
        k.op("dve", lambda e: e.tensor_scalar(out=h2[:, :, 0:64], in0=hp[:, :, 0:64].bitcast(F32), scalar1=g.cm[:, 0:1], scalar2=None, op0=ALU.mult),
             w=[h2], r=[hp, g.cm])
        k.op("dve", lambda e: e.tensor_scalar(out=h2[:, :, 64 + TL:128 + TL], in0=hp[:, :, 64:128].bitcast(F32), scalar1=g.cm[:, 1:2], scalar2=None, op0=ALU.mult),
             w=[h2], r=[hp, g.cm])
        aT = k.sbs(es, "aTl", [128, NF, 512], F32R)
        gb = k.sbs(es, "gbl", [128, 640])
        acc = k.sbs(es, "accl", [128, 512])
        for hf in range(2):
            ffn_core(g, l, h2, aT, gb, acc, [(hf * 512, 640)], grid=True, which=1, x=g.xl, xoff=hf * 512)


def nc_pid(g):
    if not hasattr(g, "pid"):
        g.pid = g.k.nc.gpsimd.partition_id()
    return g.pid


def ffn_core(g, l, h2, aT, gb, acc, tblk, grid, which, x, xoff):
    k = g.k
    t0, tl = tblk[0]
    nout = 512

    def xin(kc, ti):
        return h2[:, kc, t0:t0 + tl], h2

    blocks = [(0, 512)] if tl == 512 else [(0, 512), (512, tl - 512)]

    def xin2(kc, ti):
        b0, bl = blocks[ti]
        return h2[:, kc, t0 + b0:t0 + b0 + bl], h2

    def evac_gate(ps, mt, ti, m):
        b0, bl = blocks[ti]
        k.op("act", lambda e: e.activation(out=gb[:, b0:b0 + bl], in_=ps[:, 0:bl], func=AF.Copy), w=[gb], r=[ps])
        if ti != len(blocks) - 1:
            return
        cwt = g.cw[:, l, mt, :]
        if not grid:
            gv = gb[:, 0:512].rearrange("p (s t) -> p s t", s=2)
            av = acc[:, 0:512].rearrange("p (s t) -> p s t", s=2)
            k.op("dve", lambda e: e.tensor_scalar(out=acc[:, 0:512], in0=gb[:, 0:512], scalar1=cwt[:, 4:5], scalar2=None, op0=ALU.mult), w=[acc], r=[gb, g.cw])
            k.op("dve", lambda e: e.scalar_tensor_tensor(out=av[:, :, 1:256], in0=gv[:, :, 0:255], scalar=cwt[:, 3:4], in1=av[:, :, 1:256],
                                                         op0=ALU.mult, op1=ALU.add), w=[acc], r=[gb, g.cw, acc])
            k.op("dve", lambda e: e.scalar_tensor_tensor(out=av[:, :, 0:255], in0=gv[:, :, 1:256], scalar=cwt[:, 5:6], in1=av[:, :, 0:255],
                                                         op0=ALU.mult, op1=ALU.add), w=[acc], r=[gb, g.cw, acc])
        else:
            gv = gb[:, 0:640].rearrange("p (r c) -> p r c", c=64)
            av = acc[:, 0:512].rearrange("p (r c) -> p r c", c=64)
            first = True
            eng = "dve"
            for i in range(3):
                for j in (1, 0, 2):
                    tap = cwt[:, 3 * i + j:3 * i + j + 1]
                    if j == 1:
                        o_, i_ = av[:, :, :], gv[:, i:i + 8, :]
                    elif j == 0:
                        o_, i_ = av[:, :, 1:64], gv[:, i:i + 8, 0:63]
                    else:
                        o_, i_ = av[:, :, 0:63], gv[:, i:i + 8, 1:64]
                    if first:
                        k.op(eng, lambda e, o_=o_, i_=i_, tap=tap: e.tensor_scalar(out=o_, in0=i_, scalar1=tap, scalar2=None, op0=ALU.mult),
                             w=[acc], r=[gb, g.cw])
                        first = False
                    else:
                        k.op(eng, lambda e, o_=o_, i_=i_, tap=tap: e.scalar_tensor_tensor(out=o_, in0=i_, scalar=tap, in1=o_, op0=ALU.mult, op1=ALU.add),
                             w=[acc], r=[gb, g.cw, acc])
        k.op("act", lambda e: e.activation(out=aT[:, mt, :], in_=acc[:, 0:512], func=AF.Silu, bias=g.cb[:, l, mt:mt + 1]),
             w=[aT], r=[acc, g.cb])

    proj(g, g.ffn_wg, g.ffn_wg[l], 8, DFF, xin2, blocks, evac_gate)

    uoff = t0 + (64 if grid else 0)

    def xin_up(kc, ti):
        return h2[:, kc, uoff:uoff + 512], h2

    def evac_up(ps, mt, ti, m):
        k.op("dve", lambda e: e.tensor_tensor(out=aT[:, mt, :], in0=aT[:, mt, :].bitcast(F32), in1=ps[:, 0:512], op=ALU.mult), w=[aT], r=[aT, ps])

    proj(g, g.ffn_wu, g.ffn_wu[l], 8, DFF, xin_up, [(0, 512)], evac_up)

    def xin_dn(kc, ti):
        return aT[:, kc, :], aT

    def evac_dn(ps, mt, ti, m):
        k.op("dve", lambda e: e.scalar_tensor_tensor(out=x[:, mt, xoff:xoff + 512], in0=ps[:, 0:512], scalar=g.mv[:, which, 5, mt:mt + 1],
                                                     in1=x[:, mt, xoff:xoff + 512], op0=ALU.mult, op1=ALU.add), w=[x], r=[ps, g.mv, x])

    proj(g, g.ffn_wd, g.ffn_wd[l], NF, 1024, xin_dn, [(0, 512)], evac_dn, cbw=256)


def final_out(g, raw):
    k = g.k
    with k.scope() as es:
        g.mvf = None
        for (x, T, off) in ((g.xc, TC, 0), (g.xl, TL, TC)):
            o = k.sbs(es, "fo", [128, 8, T])
            if raw:
                k.op("dve", lambda e, o=o, x=x: e.tensor_copy(out=o[:], in_=x[:]), w=[o], r=[x])
            else:
                fin_norm(g, x, T, o, es)
            for kc in range(8):
                k.dma(g.yT[kc, :, off:off + T], o[:, kc, :], g.yT, o)


def fin_norm(g, x, T, out, es):
    k = g.k
    sq = k.sbs(es, "sqf", [128, 8, 512], F32R)
    rstd = k.sbs(es, "rstdf", [128, 512])
    for t0 in range(0, T, 512):
        tl = 512
        for kc in range(8):
            k.op("act", lambda e, kc=kc: e.activation(out=sq[:, kc, 0:tl], in_=x[:, kc, t0:t0 + tl], func=AF.Square), w=[sq], r=[x])
        ps = k.ps()
        for kc in range(8):
            k.op("pe", lambda e, kc=kc: e.matmul(ps[:, 0:tl], lhsT=g.cst[:, 1, :], rhs=sq[:, kc, 0:tl], start=(kc == 0), stop=(kc == 7)),
                 w=[ps], r=[g.cst, sq])
        k.op("act", lambda e: e.activation(out=rstd[:, 0:tl], in_=ps[:, 0:tl], func=AF.Sqrt, bias=EPS), w=[rstd], r=[ps])
        k.op("dve", lambda e: e.reciprocal(out=rstd[:, 0:tl], in_=rstd[:, 0:tl]), w=[rstd], r=[rstd])
        for kc in range(8):
            k.op("dve", lambda e, kc=kc: e.scalar_tensor_tensor(out=out[:, kc, t0:t0 + tl], in0=x[:, kc, t0:t0 + tl], scalar=g.nrmt[:, 8, kc:kc + 1],
                                                                in1=rstd[:, 0:tl], op0=ALU.mult, op1=ALU.mult), w=[out], r=[x, g.nrmt, rstd])


def fm(v):
    v = np.asarray(v, np.float32)
    sh = v.shape
    n = sh[-1] // 128
    v = v.reshape(sh[:-1] + (n, 128))
    return np.ascontiguousarray(np.moveaxis(v, -1, 0))


def prep(inp):
    shared = {}
    shared["mod_w"] = np.ascontiguousarray(inp["mod_w"], np.float32)
    shared["mod_b"] = np.ascontiguousarray(fm(inp["mod_b"]))
    nrm = np.concatenate([inp["norm1"], inp["norm2"], inp["final_norm"][None]], 0)
    shared["nrm"] = fm(nrm)
    shared["ffn_w_gate"] = np.ascontiguousarray(inp["ffn_w_gate"], np.float32)
    shared["ffn_w_up"] = np.ascontiguousarray(inp["ffn_w_up"], np.float32)
    shared["ffn_w_down"] = np.ascontiguousarray(inp["ffn_w_down"], np.float32)
    cw = np.asarray(inp["ffn_conv"], np.float32).reshape(4, 9, DFF)
    shared["ffn_cw"] = np.ascontiguousarray(np.moveaxis(fm(cw), 2, 3))
    shared["ffn_cb"] = fm(inp["ffn_conv_b"])
    consts = np.zeros((128, 4, 128), np.float32)
    consts[:, 0, :] = np.eye(128)
    consts[:, 1, :] = 1.0 / 1024
    consts[:, 2, :] = 1.0 / 128
    consts[:, 3, :] = 1.0
    shared["consts"] = consts
    shared["ab_w_in"] = np.ascontiguousarray(inp["ab_w_in"], np.float32)
    shared["ab_w_out"] = np.ascontiguousarray(inp["ab_w_out"], np.float32)
    abc = np.asarray(inp["ab_conv"], np.float32)
    cwc = np.moveaxis(fm(abc), 2, 3)
    shared["ab_cw_c"] = np.ascontiguousarray(cwc)
    shared["ab_nab"] = np.ascontiguousarray(np.stack([inp["ab_norm_a"].T, inp["ab_norm_b"].T], -1), np.float32)
    tab = np.zeros((128, 12, 128), np.float32)
    tab[:, 0, :] = np.eye(128)
    ii = np.arange(64)
    s_, c_ = ii[:, None], ii[None, :]
    tab[:64, 1, :64] = (s_ <= c_); tab[:64, 2, :64] = (s_ >= c_)
    tab[:64, 3, :64] = (s_ > c_); tab[:64, 4, :64] = (s_ < c_)
    tab[:64, 5, :64] = -1.0 * (c_ < s_)
    tab[:64, 6, :64] = -1.0 * (c_ > s_)
    tab[:64, 7, :64] = np.abs(s_ - c_)
    tab[:, 8, :64] = (ii + 1)[None, :]
    tab[:, 9, :64] = (64 - ii)[None, :]
    tab[:64, 10, 0] = 63 - ii
    tab[:64, 10, 1] = ii
    tab[:, 11, :] = 1.0
    shared["abtab"] = tab
    if "hy_w_in" in inp and HY_HOST is not None:
        HY_HOST(inp, shared, None)
    maps = []
    for c in range(8):
        gi, r = c // 4, c % 4
        m = dict(shared)
        wi = np.asarray(inp["ab_w_in"], np.float32)
        cols = []
        for base in (0, 512, 1024, 1536, 2064, 2576, 3088, 3600):
            cols += list(range(base + 128 * r, base + 128 * r + 128))
        cols += [2048 + r, 2052 + r, 2056 + r, 2060 + r]
        m["ab_w_in_l"] = np.ascontiguousarray(wi[:, :, cols])
        m["ab_cw_l"] = np.ascontiguousarray(cwc[:, :, [r, 4 + r, 8 + r], :])
        t5 = np.zeros((128, 2, 5, 2), np.float32)
        rt = np.zeros((128, 2, 5, 2), np.float32)
        for hd in range(5):
            hh = hd if hd < 4 else r
            for d in range(2):
                t5[d, :, hd, 0] = inp["ab_a_log"][:, d, hh]
                t5[d, :, hd, 1] = inp["ab_dt_bias"][:, d, hh]
                rt[:, :, hd, d] = inp["ab_ret_decay"][:, d, hh][None, :]
        m["ab_abc"] = t5
        m["ab_ret"] = rt
        m["sdl_in"] = np.ascontiguousarray(inp["state_delta"][gi, :, :, r], np.float32)
        m["srl_in"] = np.ascontiguousarray(inp["state_ret"][gi, :, :, r], np.float32)
        if "hy_w_in" in inp and HY_HOST is not None:
            HY_HOST(inp, m, (c, gi, r))
            m.pop("_deltas", None)
        xt = np.concatenate([inp["x_prompt"][2 * c].T, inp["x_prompt"][2 * c + 1].T,
                             inp["x_sample"][gi, 1024 * r:1024 * (r + 1)].T], axis=1)
        m["xT"] = np.ascontiguousarray(xt.reshape(8, 128, TT), np.float32)
        cond = np.stack([inp["c_ctx"], inp["c"][gi]], 0)
        m["cond"] = np.ascontiguousarray(fm(cond).reshape(128, 16))
        cm = np.ones((128, 2), np.float32)
        if r == 0:
            cm[:, 0] = 0
        if r == 3:
            cm[:, 1] = 0
        m["cmask"] = cm
        maps.append(m)
    return maps


HY_HOST = None


def assemble_states(res):
    sd = np.zeros((16, 2, 2, 4, 128, 128), np.float32)
    sr = np.zeros((16, 2, 2, 4, 128, 128), np.float32)
    for c in range(8):
        sd[2 * c:2 * c + 2] = res[c]["sd_out"].reshape(2, 2, 2, 4, 128, 128)
        sr[2 * c:2 * c + 2] = res[c]["sr_out"].reshape(2, 2, 2, 4, 128, 128)
    return sd, sr


def assemble(res):
    yp = np.zeros((16, 256, 1024), np.float32)
    ys = np.zeros((2, 4096, 1024), np.float32)
    for c in range(8):
        gi, r = c // 4, c % 4
        yT = res[c]["yT"].reshape(1024, TT)
        yp[2 * c] = yT[:, 0:256].T
        yp[2 * c + 1] = yT[:, 256:512].T
        ys[gi, 1024 * r:1024 * (r + 1)] = yT[:, 512:].T
    return yp, ys


C = 64


def ab_setup(g, din):
    k = g.k
    g.ab_w_in = din("ab_w_in", [2, 1024, 4112], F32R)
    import os
    if os.environ.get("SKIPL") == "1":
        din2 = lambda n, sh, dt=F32: k.dram(n, sh, dt)
    else:
        din2 = din
    g.ab_w_in_l = din2("ab_w_in_l", [2, 1024, 1028], F32R)
    g.ab_w_out = din("ab_w_out", [2, 1024, 1024], F32R)
    g.ab_cw_c = din("ab_cw_c", [128, 2, 12, 3])
    g.ab_cw_l = din("ab_cw_l", [128, 2, 3, 3])
    g.ab_abc = din("ab_abc", [128, 2, 5, 2])
    g.ab_ret = din("ab_ret", [128, 2, 5, 2])
    g.ab_nab = din("ab_nab", [128, 2, 2])
    g.sdl_in = din2("sdl_in", [2, 2, 128, 128])
    g.srl_in = din2("srl_in", [2, 2, 128, 128])
    g.abtab = din("abtab", [128, 12, 128])
    g.sd_out = k.dram("sd_out", [32, 128, 128], F32, kind="ExternalOutput")
    g.sr_out = k.dram("sr_out", [32, 128, 128], F32, kind="ExternalOutput")
    import os
    g.dbg = True if os.environ.get("DBG") == "1" else None
    g.Pc = k.dram("Pc", [4224, TC], F32)
    g.Pl = k.dram("Pl", [1152, 4096], F32)
    g.h1_in = k.dram("h1_in", [8, 128, TL], F32R)
    g.h1_out = k.dram("h1_out", [8, 4, 128, TL], F32R)
    g.o_in = k.dram("o_in", [2, 4, 128, 1024], F32R)
    g.o_out = k.dram("o_out", [2, 4, 4, 128, 1024], F32R)
    g.cwc = k.sb("cwc", [128, 2, 12, 3])
    g.cwl = k.sb("cwl", [128, 2, 3, 3])
    g.abc = k.sb("abc", [128, 2, 5, 2])
    g.nexp = k.sb("nexp", [128, 2, 5, 1])
    g.ret = k.sb("ret", [128, 2, 5, 2])
    g.lgam = k.sb("lgam", [128, 2, 5, 2])
    g.nab = k.sb("nab", [128, 2, 2])
    g.tab = k.sb("tab", [128, 12, 128])
    for (t, s) in ((g.cwc, g.ab_cw_c), (g.cwl, g.ab_cw_l), (g.abc, g.ab_abc), (g.ret, g.ab_ret), (g.nab, g.ab_nab), (g.tab, g.abtab)):
        k.dma(t[:], s[:], t, s)
    g.post_setup.append(lambda: ab_setup2(g))


def ab_setup2(g):
    k = g.k
    k.op("act", lambda e: e.activation(out=g.nexp[:], in_=g.abc[:, :, :, 0:1], func=AF.Exp), w=[g.nexp], r=[g.abc])
    k.op("dve", lambda e: e.tensor_scalar(out=g.nexp[:], in0=g.nexp[:], scalar1=-1.0, scalar2=None, op0=ALU.mult), w=[g.nexp], r=[g.nexp])
    k.op("act", lambda e: e.activation(out=g.lgam[:], in_=g.ret[:], func=AF.Exp), w=[g.lgam], r=[g.ret])
    k.op("dve", lambda e: e.tensor_scalar(out=g.lgam[:], in0=g.lgam[:], scalar1=-1.0, scalar2=None, op0=ALU.mult), w=[g.lgam], r=[g.lgam])


def T_(g, i, p=64, f=64):
    return g.tab[0:p, i, 0:f]


def ab_layer(g, l):
    k = g.k
    j = l // 2
    with k.scope() as es:
        h1 = k.sbs(es, "h1c", [128, 8, TC], F32R)
        stg = [k.sbs(es, f"stg{i}", [128, 512]) for i in range(2)]
        normmod(g, g.xc, TC, 0, 0, h1)
        h1l = k.sbs(es, "h1l", [128, 8, TL], F32R)
        normmod(g, g.xl, TL, 1, 0, h1l)
        h1_exchange(g, h1l)
        cnt = [0]

        def mk_evac(P, tcol):
            def evac(ps, mt, ti, m):
                s = stg[cnt[0] % 2]
                cnt[0] += 1
                k.op("act", lambda e: e.activation(out=s[:], in_=ps[:], func=AF.Copy), w=[s], r=[ps])
                k.dma(P[mt * 128:(mt + 1) * 128, tcol:tcol + 512], s[:], P, s)
            return evac

        proj(g, g.ab_w_in, g.ab_w_in[j], 8, 4112, lambda kc, ti: (h1[:, kc, :], h1), [(0, 512)], mk_evac(g.Pc, 0))
        hb = h1
        import os
        SKIPL = os.environ.get("SKIPL") == "1"
        for tb in (range(0) if SKIPL else range(8)):
            k.dma(hb[:], g.h1_out[:, tb // 2, :, (tb % 2) * 512:(tb % 2) * 512 + 512].rearrange("k p t -> p k t"), hb, g.h1_out)
            proj(g, g.ab_w_in_l, g.ab_w_in_l[j], 8, 1028, lambda kc, ti: (hb[:, kc, :], hb), [(0, 512)], mk_evac(g.Pl, tb * 512), cbw=640)
    import os
    if os.environ.get("ABV") == "1":
        return
    with k.scope() as es:
        omix = k.sbs(es, "omix", [128, 8, TC], F32R)
        for s in range(2):
            for h in range(4):
                rows = dict(qa=128 * h, ka=512 + 128 * h, va=1024 + 128 * h, z=1536 + 128 * h,
                            a=[2048 + h, 2052 + h], b=[2056 + h, 2060 + h],
                            qb=2064 + 128 * h, kb=2576 + 128 * h, vb=3088 + 128 * h, gb=3600 + 128 * h)
                head_pass(g, j, g.Pc, rows, s * 256, 256, 256, h, g.cwc[:, j, :, :], [h, 4 + h, 8 + h], None,
                          (s, h), omix, (h, 4 + h), s * 256)

        def evac_o(ps, mt, ti, m):
            k.op("dve", lambda e: e.scalar_tensor_tensor(out=g.xc[:, mt, :], in0=ps[:, 0:512], scalar=g.mv[:, 0, 2, mt:mt + 1],
                                                         in1=g.xc[:, mt, :], op0=ALU.mult, op1=ALU.add), w=[g.xc], r=[ps, g.mv, g.xc])
        proj(g, g.ab_w_out, g.ab_w_out[j], 8, 1024, lambda kc, ti: (omix[:, kc, :], omix), [(0, 512)], evac_o)
    import os
    if os.environ.get("SKIPL") == "1":
        return
    with k.scope() as es:
        omixl = k.sbs(es, "omixl", [128, 2, 4096], F32R)
        rows = dict(qa=0, ka=128, va=256, z=384, qb=512, kb=640, vb=768, gb=896, a=[1024, 1025], b=[1026, 1027])
        head_pass(g, j, g.Pl, rows, 0, 4096, 256, 4, g.cwl[:, j, :, :], [0, 1, 2], (g.sdl_in, g.srl_in), None, omixl, (0, 1), 0)
        o_exchange(g, omixl)
    mix_out_lat(g, g.ab_w_out, g.ab_w_out[j], None)


def h1_exchange(g, h1l):
    k = g.k
    k.dma(g.h1_in[:].rearrange("k p t -> p k t"), h1l[:], g.h1_in, h1l)
    for kc in range(8):
        k.allgather(g.h1_out, g.h1_in, GROUPS, out_ap=g.h1_out[kc].rearrange("r p t -> (r p) t"), in_ap=g.h1_in[kc])


def o_exchange(g, omixl):
    k = g.k
    k.dma(g.o_in[:].rearrange("a q p t -> p a q t"), omixl[:].rearrange("p a (q t) -> p a q t", q=4), g.o_in, omixl)
    for a in range(2):
        for q in range(4):
            k.allgather(g.o_out, g.o_in, GROUPS, out_ap=g.o_out[a, q].rearrange("r p t -> (r p) t"), in_ap=g.o_in[a, q])


def mix_out_lat(g, Wsrc, Wap, bias, kcmap=lambda a, i: a * 4 + i):
    k = g.k
    pid = nc_pid(g)
    r4 = pid % 4
    with k.scope() as es:
        om = k.sbs(es, "om", [128, 8, 512], F32R)
        for hf in range(2):
            for i in range(4):
                for a in range(2):
                    k.dma(om[:, kcmap(a, i), :], g.o_out[a, bass.ds(r4, 1), i, :, hf * 512:(hf + 1) * 512].rearrange("o p t -> p (o t)"), om, g.o_out)

            def evac_o(ps, mt, ti, m):
                xs = g.xl[:, mt, hf * 512:(hf + 1) * 512]
                if bias is not None:
                    k.op("act", lambda e: e.activation(out=ps[:, 0:512], in_=ps[:, 0:512], func=AF.Identity, bias=bias[:, mt:mt + 1]),
                         w=[ps], r=[ps, g.hyb])
                k.op("dve", lambda e: e.scalar_tensor_tensor(out=xs, in0=ps[:, 0:512], scalar=g.mv[:, 1, 2, mt:mt + 1],
                                                             in1=xs, op0=ALU.mult, op1=ALU.add), w=[g.xl], r=[ps, g.mv, g.xl])
            proj(g, Wsrc, Wap, 8, 1024, lambda kc, ti: (om[:, kc, :], om), [(0, 512)], evac_o)


def dbg_dump(g, idx, tile):
    k = g.k
    if getattr(g, "dbg", None) is None:
        return
    slots = {0: 8, 1: 9, 4: 10, 5: 11, 6: 12, 8: 13, 9: 14, 10: 15, 11: 24, 12: 25, 13: 26, 14: 27, 15: 28, 16: 29, 17: 30, 18: 31}
    if idx not in slots:
        return
    k.dma(g.sd_out[slots[idx]], tile[:, 0:128], g.sd_out, tile)


def head_pass(g, j, P, rows, tok0, L, B, hd, cwt, cwi, init, sout, omix, och, ooff):
    k = g.k
    nb = L // B
    ncb = B // C
    with k.scope() as es:
        oa = k.sbs(es, "oa", [128, L])
        ob = k.sbs(es, "ob", [128, L])
        raw = [k.sbs(es, f"raw{i}", [128, B + 2]) for i in range(3)]
        cv = [k.sbs(es, f"cv{i}", [128, B]) for i in range(3)]
        rb = [k.sbs(es, f"rb{i}", [128, B]) for i in range(3)]
        AB = k.sbs(es, "AB", [128, B])
        tmp = k.sbs(es, "hp_tmp", [128, B])
        S = [k.sbs(es, f"S{i}", [128, 128]) for i in range(2)]
        R = [k.sbs(es, f"R{i}", [128, 128]) for i in range(2)]
        sm = {n: k.sbs(es, n, [128, 128]) for n in ("tm", "cols", "Gbc", "Grow", "DT", "Dm", "t1", "AT", "XA", "XB", "XTA", "XTB",
                                                   "TT", "kd", "bv", "qd", "rv", "vn", "ATb", "vtb", "ksd", "qcd", "DmTb", "cdrow", "rc")}
        k.op("dve", lambda e: e.memset(AB[:], 0.0), w=[AB])
        if getattr(g, "dbg", None):
            for nm_, t_ in sm.items():
                k.op("dve", lambda e, t_=t_: e.memset(t_[:], 0.0), w=[t_])
        for d in range(2):
            lg = g.lgam[:, j, hd, d:d + 1]
            k.op("act", lambda e: e.activation(out=sm["DmTb"][0:64, 0:64], in_=T_(g, 7), func=AF.Exp, scale=lg[0:64, :]), w=[sm["DmTb"]], r=[g.tab, g.lgam])
            k.op("dve", lambda e: e.scalar_tensor_tensor(out=sm["DmTb"][0:64, 0:64], in0=sm["DmTb"][0:64, 0:64], scalar=128.0 ** -0.5,
                                                         in1=g.tab[0:64, 1 + d, 0:64], op0=ALU.mult, op1=ALU.mult), w=[sm["DmTb"]], r=[g.tab, sm["DmTb"]])
            k.op("act", lambda e: e.activation(out=sm["cdrow"][:, 0:64], in_=g.tab[:, 8 + d, 0:64], func=AF.Exp, scale=lg), w=[sm["cdrow"]], r=[g.tab, g.lgam])
            k.op("dve", lambda e: e.tensor_scalar(out=sm["cdrow"][:, 0:64], in0=sm["cdrow"][:, 0:64], scalar1=128.0 ** -0.5, scalar2=None, op0=ALU.mult),
                 w=[sm["cdrow"]], r=[sm["cdrow"]])
            k.op("act", lambda e: e.activation(out=sm["rc"][0:64, 0:1], in_=g.tab[0:64, 10, d:d + 1], func=AF.Exp, scale=lg[0:64, :]), w=[sm["rc"]], r=[g.tab, g.lgam])
            k.op("act", lambda e: e.activation(out=sm["rc"][:, 1:2], in_=g.tab[:, 11, 0:1], func=AF.Exp, scale=lg, bias=0.0), w=[sm["rc"]], r=[g.tab, g.lgam])
            k.op("dve", lambda e: e.tensor_tensor(out=sm["rc"][:, 1:2], in0=sm["rc"][:, 1:2], in1=sm["rc"][:, 1:2], op=ALU.mult), w=[sm["rc"]], r=[sm["rc"]])
            for _ in range(5):
                k.op("dve", lambda e: e.tensor_tensor(out=sm["rc"][:, 1:2], in0=sm["rc"][:, 1:2], in1=sm["rc"][:, 1:2], op=ALU.mult), w=[sm["rc"]], r=[sm["rc"]])
            si = 0
            if init is None:
                k.op("dve", lambda e: e.memset(S[0][:], 0.0), w=[S[0]])
                k.op("dve", lambda e: e.memset(R[0][:], 0.0), w=[R[0]])
            else:
                k.dma(S[0][:], init[0][j, d], S[0], init[0])
                k.dma(R[0][:], init[1][j, d], R[0], init[1])
            for bi in (range(nb) if d == 0 else reversed(range(nb))):
                t0 = bi * B
                for i, nm in enumerate(("qa", "ka", "va")):
                    lo = max(t0 - 1, 0)
                    hi = min(t0 + B + 1, L)
                    if lo > t0 - 1:
                        k.op("dve", lambda e, i=i: e.memset(raw[i][:, 0:1], 0.0), w=[raw[i]])
                    if hi < t0 + B + 1:
                        k.op("dve", lambda e, i=i: e.memset(raw[i][:, B + 1:B + 2], 0.0), w=[raw[i]])
                    k.dma(raw[i][:, lo - (t0 - 1):hi - (t0 - 1)], P[rows[nm]:rows[nm] + 128, tok0 + lo:tok0 + hi], raw[i], P)
                    w3 = cwt[:, cwi[i], :]
                    k.op("dve", lambda e, i=i, w3=w3: e.tensor_scalar(out=tmp[:], in0=raw[i][:, 0:B], scalar1=w3[:, 0:1], scalar2=None, op0=ALU.mult), w=[tmp], r=[raw[i]])
                    k.op("dve", lambda e, i=i, w3=w3: e.scalar_tensor_tensor(out=tmp[:], in0=raw[i][:, 1:B + 1], scalar=w3[:, 1:2], in1=tmp[:], op0=ALU.mult, op1=ALU.add), w=[tmp], r=[raw[i], tmp])
                    k.op("dve", lambda e, i=i, w3=w3: e.scalar_tensor_tensor(out=tmp[:], in0=raw[i][:, 2:B + 2], scalar=w3[:, 2:3], in1=tmp[:], op0=ALU.mult, op1=ALU.add), w=[tmp], r=[raw[i], tmp])
                    k.op("act", lambda e, i=i: e.activation(out=cv[i][:], in_=tmp[:], func=AF.Silu), w=[cv[i]], r=[tmp])
                    if i < 2:
                        k.op("act", lambda e, i=i: e.activation(out=tmp[:], in_=cv[i][:], func=AF.Square), w=[tmp], r=[cv[i]])
                        ps = k.ps()
                        k.op("pe", lambda e: e.matmul(ps[:, 0:B], lhsT=g.tab[:, 11, :], rhs=tmp[:], start=True, stop=True), w=[ps], r=[g.tab, tmp])
                        k.op("act", lambda e: e.activation(out=tmp[:], in_=ps[:, 0:B], func=AF.Sqrt, bias=EPS), w=[tmp], r=[ps])
                        k.op("dve", lambda e: e.reciprocal(out=tmp[:], in_=tmp[:]), w=[tmp], r=[tmp])
                        if i == 0:
                            k.op("dve", lambda e, i=i: e.scalar_tensor_tensor(out=cv[i][:], in0=cv[i][:], scalar=128.0 ** -0.5, in1=tmp[:], op0=ALU.mult, op1=ALU.mult), w=[cv[i]], r=[cv[i], tmp])
                        else:
                            k.op("dve", lambda e, i=i: e.tensor_tensor(out=cv[i][:], in0=cv[i][:], in1=tmp[:], op=ALU.mult), w=[cv[i]], r=[cv[i], tmp])
                for i, nm in enumerate(("qb", "kb", "vb")):
                    k.dma(rb[i][:], P[rows[nm]:rows[nm] + 128, tok0 + t0:tok0 + t0 + B], rb[i], P)
                import os
                for dd in (range(0) if os.environ.get("ABCUT2") == "1" else range(2)):
                    k.dma(AB[dd:dd + 1, :], P[rows["a"][dd]:rows["a"][dd] + 1, tok0 + t0:tok0 + t0 + B], AB, P)
                    k.dma(AB[32 + dd:33 + dd, :], P[rows["b"][dd]:rows["b"][dd] + 1, tok0 + t0:tok0 + t0 + B], AB, P)
                k.op("act", lambda e: e.activation(out=AB[0:2, :], in_=AB[0:2, :], func=AF.Exp, bias=g.abc[0:2, j, hd, 1:2]), w=[AB], r=[AB, g.abc])
                k.op("act", lambda e: e.activation(out=AB[0:2, :], in_=AB[0:2, :], func=AF.Ln, bias=1.0), w=[AB], r=[AB])
                k.op("dve", lambda e: e.tensor_scalar(out=AB[0:2, :], in0=AB[0:2, :], scalar1=g.nexp[0:2, j, hd, 0:1], scalar2=None, op0=ALU.mult), w=[AB], r=[AB, g.nexp])
                k.op("act", lambda e: e.activation(out=AB[32:34, :], in_=AB[32:34, :], func=AF.Sigmoid), w=[AB], r=[AB])
                import os
                ABCUT = int(os.environ.get("ABCUT", "0"))
                k.fence = True
                for ci in (range(ncb) if d == 0 else reversed(range(ncb))):
                    if ABCUT == 1:
                        continue
                    c0 = ci * C
                    sl = slice(c0, c0 + C)
                    tcol = t0 + c0
                    last = 63 if d == 0 else 0
                    S0, S1 = S[si % 2], S[(si + 1) % 2]
                    R0, R1 = R[si % 2], R[(si + 1) % 2]
                    si += 1
                    tm, cols, Gbc, Grow, DT, Dm, t1, AT = (sm[n] for n in ("tm", "cols", "Gbc", "Grow", "DT", "Dm", "t1", "AT"))
                    ps = k.ps()
                    k.op("pe", lambda e: e.matmul(ps[0:64, 0:34], lhsT=AB[0:34, sl], rhs=g.tab[0:34, 0, 0:34], start=True, stop=True), w=[ps], r=[AB, g.tab])
                    k.op("act", lambda e: e.activation(out=tm[0:64, 0:34], in_=ps[0:64, 0:34], func=AF.Copy), w=[tm], r=[ps])
                    ps = k.ps()
                    k.op("pe", lambda e: e.matmul(ps[0:64, 0:2], lhsT=T_(g, 1 + d), rhs=tm[0:64, 0:2], start=True, stop=True), w=[ps], r=[g.tab, tm])
                    k.op("pe", lambda e: e.matmul(ps[0:64, 2:4], lhsT=T_(g, 3 + d), rhs=tm[0:64, 0:2], start=True, stop=True), w=[ps], r=[g.tab, tm])
                    k.op("dve", lambda e: e.tensor_copy(out=cols[0:64, 0:1], in_=ps[0:64, d:d + 1]), w=[cols], r=[ps])
                    k.op("act", lambda e: e.activation(out=cols[0:64, 1:2], in_=ps[0:64, d:d + 1], func=AF.Exp), w=[cols], r=[ps])
                    k.op("act", lambda e: e.activation(out=cols[0:64, 3:4], in_=ps[0:64, 2 + d:3 + d], func=AF.Exp), w=[cols], r=[ps])
                    k.op("dve", lambda e: e.scalar_tensor_tensor(out=cols[0:64, 2:3], in0=cols[0:64, 1:2], scalar=-1.0, in1=tm[0:64, 32 + d:33 + d], op0=ALU.mult, op1=ALU.mult), w=[cols], r=[cols, tm])
                    k.op("dve", lambda e: e.tensor_tensor(out=Gbc[0:64, :], in0=g.tab[0:64, 11, :], in1=tm[0:64, d:d + 1].to_broadcast([64, 128]), op=ALU.mult), w=[Gbc], r=[g.tab, tm])
                    ps = k.ps()
                    k.op("pe", lambda e: e.matmul(ps[:, 0:64], lhsT=Gbc[0:64, :], rhs=T_(g, 1 + d), start=True, stop=True), w=[ps], r=[Gbc, g.tab])
                    k.op("act", lambda e: e.activation(out=Grow[:, 0:64], in_=ps[:, 0:64], func=AF.Exp), w=[Grow], r=[ps])
                    gcb = cols[0:64, 0:1].to_broadcast([64, 64])
                    k.op("dve", lambda e: e.tensor_tensor(out=DT[0:64, 0:64], in0=ps[0:64, 0:64], in1=gcb, op=ALU.subtract), w=[DT], r=[ps, cols])
                    k.op("dve", lambda e: e.tensor_scalar(out=DT[0:64, 0:64], in0=DT[0:64, 0:64], scalar1=0.0, scalar2=None, op0=ALU.min), w=[DT], r=[DT])
                    k.op("act", lambda e: e.activation(out=DT[0:64, 0:64], in_=DT[0:64, 0:64], func=AF.Exp), w=[DT], r=[DT])
                    k.op("dve", lambda e: e.tensor_tensor(out=Dm[0:64, 0:64], in0=ps[0:64, 0:64], in1=gcb, op=ALU.subtract), w=[Dm], r=[ps, cols])
                    k.op("dve", lambda e: e.tensor_scalar(out=Dm[0:64, 0:64], in0=Dm[0:64, 0:64], scalar1=0.0, scalar2=None, op0=ALU.max), w=[Dm], r=[Dm])
                    k.op("act", lambda e: e.activation(out=Dm[0:64, 0:64], in_=Dm[0:64, 0:64], func=AF.Exp, scale=-1.0), w=[Dm], r=[Dm])
                    qaT, kaT, vaT = cv[0][:, sl], cv[1][:, sl], cv[2][:, sl]
                    psk = k.ps()
                    k.op("pe", lambda e: e.matmul(psk[0:64, 0:64], lhsT=kaT, rhs=kaT, start=True, stop=True), w=[psk], r=[cv[1]])
                    k.op("pe", lambda e: e.matmul(psk[0:64, 64:128], lhsT=kaT, rhs=qaT, start=True, stop=True), w=[psk], r=[cv[1], cv[0]])
                    XA, XB, XTA, XTB, TTt = sm["XA"], sm["XB"], sm["XTA"], sm["XTB"], sm["TT"]
                    k.op("dve", lambda e: e.tensor_tensor(out=t1[0:64, 0:64], in0=psk[0:64, 0:64], in1=Dm[0:64, 0:64], op=ALU.mult), w=[t1], r=[psk, Dm])
                    k.op("dve", lambda e: e.tensor_tensor(out=t1[0:64, 0:64], in0=t1[0:64, 0:64], in1=tm[0:64, 32 + d:33 + d].to_broadcast([64, 64]), op=ALU.mult), w=[t1], r=[t1, tm])
                    k.op("dve", lambda e: e.tensor_tensor(out=XA[0:64, 0:64], in0=t1[0:64, 0:64], in1=T_(g, 5 + d), op=ALU.mult), w=[XA], r=[t1, g.tab])
                    k.op("dve", lambda e: e.tensor_tensor(out=t1[0:64, 64:128], in0=psk[0:64, 64:128], in1=DT[0:64, 0:64], op=ALU.mult), w=[t1], r=[psk, DT])
                    k.op("dve", lambda e: e.tensor_tensor(out=AT[0:64, 0:64], in0=t1[0:64, 64:128], in1=T_(g, 1 + d), op=ALU.mult), w=[AT], r=[t1, g.tab])
                    ps = k.ps()
                    k.op("pe", lambda e: e.matmul(ps[0:64, 0:64], lhsT=XA[0:64, 0:64], rhs=T_(g, 0), start=True, stop=True), w=[ps], r=[XA, g.tab])
                    k.op("act", lambda e: e.activation(out=XTA[0:64, 0:64], in_=ps[0:64, 0:64], func=AF.Copy), w=[XTA], r=[ps])
                    k.op("dve", lambda e: e.tensor_tensor(out=TTt[0:64, 0:64], in0=ps[0:64, 0:64], in1=T_(g, 0), op=ALU.add), w=[TTt], r=[ps, g.tab])
                    DBG = (sout == (0, 0) and d == 0 and ci == 0 and j == 0 and getattr(g, "dbg", None) is not None)
                    if DBG:
                        for ii, nm in enumerate(("tm", "cols", "Grow", "DT", "Dm", "XA", "AT", "XTA")):
                            dbg_dump(g, ii, sm[nm])
                        dbg_dump(g, 14, cv[0]); dbg_dump(g, 15, cv[1]); dbg_dump(g, 16, cv[2]); dbg_dump(g, 17, AB)
                    X, XT, Xn, XTn = XA, XTA, XB, XTB
                    for jj in range(1, 6):
                        ps = k.ps()
                        k.op("pe", lambda e, XT=XT, X=X: e.matmul(ps[0:64, 0:64], lhsT=XT[0:64, 0:64], rhs=X[0:64, 0:64], start=True, stop=True), w=[ps], r=[XT, X])
                        k.op("act", lambda e, Xn=Xn: e.activation(out=Xn[0:64, 0:64], in_=ps[0:64, 0:64], func=AF.Copy), w=[Xn], r=[ps])
                        if jj < 5:
                            ps2 = k.ps()
                            k.op("pe", lambda e, XT=XT, X=X: e.matmul(ps2[0:64, 0:64], lhsT=X[0:64, 0:64], rhs=XT[0:64, 0:64], start=True, stop=True), w=[ps2], r=[XT, X])
                            k.op("act", lambda e, XTn=XTn: e.activation(out=XTn[0:64, 0:64], in_=ps2[0:64, 0:64], func=AF.Copy), w=[XTn], r=[ps2])
                        ps3 = k.ps()
                        k.op("pe", lambda e, Xn=Xn: e.matmul(ps3[0:64, 0:64], lhsT=Xn[0:64, 0:64], rhs=TTt[0:64, 0:64], start=True, stop=True), w=[ps3], r=[Xn, TTt])
                        k.op("dve", lambda e: e.tensor_tensor(out=TTt[0:64, 0:64], in0=TTt[0:64, 0:64], in1=ps3[0:64, 0:64], op=ALU.add), w=[TTt], r=[TTt, ps3])
                        X, XT, Xn, XTn = Xn, XTn, X, XT
                    kd, bv, qd, rv, vn = sm["kd"], sm["bv"], sm["qd"], sm["rv"], sm["vn"]
                    ps = k.ps()
                    k.op("pe", lambda e: e.matmul(ps[0:64, 0:128], lhsT=kaT, rhs=g.tab[:, 0, :], start=True, stop=True), w=[ps], r=[cv[1], g.tab])
                    k.op("dve", lambda e: e.tensor_tensor(out=kd[0:64, :], in0=ps[0:64, 0:128], in1=cols[0:64, 3:4].to_broadcast([64, 128]), op=ALU.mult), w=[kd], r=[ps, cols])
                    ps = k.ps()
                    k.op("pe", lambda e: e.matmul(ps[0:64, 0:128], lhsT=vaT, rhs=g.tab[:, 0, :], start=True, stop=True), w=[ps], r=[cv[2], g.tab])
                    k.op("dve", lambda e: e.tensor_tensor(out=bv[0:64, :], in0=ps[0:64, 0:128], in1=tm[0:64, 32 + d:33 + d].to_broadcast([64, 128]), op=ALU.mult), w=[bv], r=[ps, tm])
                    k.op("dve", lambda e: e.tensor_tensor(out=qd[:, 0:64], in0=qaT, in1=Grow[:, 0:64], op=ALU.mult), w=[qd], r=[cv[0], Grow])
                    ps = k.ps()
                    k.op("pe", lambda e: e.matmul(ps[0:64, 0:128], lhsT=kaT, rhs=S0[:], start=True, stop=True), w=[ps], r=[cv[1], S0])
                    k.op("dve", lambda e: e.tensor_tensor(out=rv[0:64, :], in0=ps[0:64, 0:128], in1=cols[0:64, 2:3].to_broadcast([64, 128]), op=ALU.mult), w=[rv], r=[ps, cols])
                    k.op("dve", lambda e: e.tensor_tensor(out=rv[0:64, :], in0=rv[0:64, :], in1=bv[0:64, :], op=ALU.add), w=[rv], r=[rv, bv])
                    ps = k.ps()
                    k.op("pe", lambda e: e.matmul(ps[0:64, 0:128], lhsT=TTt[0:64, 0:64], rhs=rv[0:64, :], start=True, stop=True), w=[ps], r=[TTt, rv])
                    k.op("act", lambda e: e.activation(out=vn[0:64, :], in_=ps[0:64, 0:128], func=AF.Copy), w=[vn], r=[ps])
                    ps = k.ps()
                    k.op("pe", lambda e: e.matmul(ps[:, 0:64], lhsT=S0[:], rhs=qd[:, 0:64], start=True, stop=False), w=[ps], r=[S0, qd])
                    k.op("pe", lambda e: e.matmul(ps[:, 0:64], lhsT=vn[0:64, :], rhs=AT[0:64, 0:64], start=False, stop=True), w=[ps], r=[vn, AT])
                    if d == 0:
                        k.op("act", lambda e: e.activation(out=oa[:, tcol:tcol + C], in_=ps[:, 0:64], func=AF.Copy), w=[oa], r=[ps])
                    else:
                        k.op("dve", lambda e: e.tensor_tensor(out=oa[:, tcol:tcol + C], in0=oa[:, tcol:tcol + C], in1=ps[:, 0:64], op=ALU.add), w=[oa], r=[oa, ps])
                    ps = k.ps()
                    k.op("pe", lambda e: e.matmul(ps[:, 0:128], lhsT=kd[0:64, :], rhs=vn[0:64, :], start=True, stop=True), w=[ps], r=[kd, vn])
                    k.op("dve", lambda e: e.tensor_tensor(out=S1[:], in0=S0[:], in1=Grow[:, last:last + 1].to_broadcast([128, 128]), op=ALU.mult), w=[S1], r=[S0, Grow])
                    k.op("dve", lambda e: e.tensor_tensor(out=S1[:], in0=S1[:], in1=ps[:, 0:128], op=ALU.add), w=[S1], r=[S1, ps])
                    if DBG:
                        for ii, nm in enumerate(("TT", "kd", "bv", "qd", "rv", "vn")):
                            dbg_dump(g, 8 + ii, sm[nm])
                        dbg_dump(g, 18, S1)
                    qbT, kbT, vbT = rb[0][:, sl], rb[1][:, sl], rb[2][:, sl]
                    ATb, vtb, ksd, qcd = sm["ATb"], sm["vtb"], sm["ksd"], sm["qcd"]
                    ps = k.ps()
                    k.op("pe", lambda e: e.matmul(ps[0:64, 0:64], lhsT=kbT, rhs=qbT, start=True, stop=True), w=[ps], r=[rb[1], rb[0]])
                    k.op("dve", lambda e: e.tensor_tensor(out=ATb[0:64, 0:64], in0=ps[0:64, 0:64], in1=sm["DmTb"][0:64, 0:64], op=ALU.mult), w=[ATb], r=[ps, sm["DmTb"]])
                    ps = k.ps()
                    k.op("pe", lambda e: e.matmul(ps[0:64, 0:128], lhsT=vbT, rhs=g.tab[:, 0, :], start=True, stop=True), w=[ps], r=[rb[2], g.tab])
                    k.op("act", lambda e: e.activation(out=vtb[0:64, :], in_=ps[0:64, 0:128], func=AF.Copy), w=[vtb], r=[ps])
                    ps = k.ps()
                    k.op("pe", lambda e: e.matmul(ps[0:64, 0:128], lhsT=kbT, rhs=g.tab[:, 0, :], start=True, stop=True), w=[ps], r=[rb[1], g.tab])
                    k.op("dve", lambda e: e.tensor_scalar(out=ksd[0:64, :], in0=ps[0:64, 0:128], scalar1=sm["rc"][0:64, 0:1], scalar2=None, op0=ALU.mult), w=[ksd], r=[ps, sm["rc"]])
                    k.op("dve", lambda e: e.tensor_tensor(out=qcd[:, 0:64], in0=qbT, in1=sm["cdrow"][:, 0:64], op=ALU.mult), w=[qcd], r=[rb[0], sm["cdrow"]])
                    ps = k.ps()
                    k.op("pe", lambda e: e.matmul(ps[:, 0:64], lhsT=R0[:], rhs=qcd[:, 0:64], start=True, stop=False), w=[ps], r=[R0, qcd])
                    k.op("pe", lambda e: e.matmul(ps[:, 0:64], lhsT=vtb[0:64, :], rhs=ATb[0:64, 0:64], start=False, stop=True), w=[ps], r=[vtb, ATb])
                    if d == 0:
                        k.op("act", lambda e: e.activation(out=ob[:, tcol:tcol + C], in_=ps[:, 0:64], func=AF.Copy), w=[ob], r=[ps])
                    else:
                        k.op("dve", lambda e: e.tensor_tensor(out=ob[:, tcol:tcol + C], in0=ob[:, tcol:tcol + C], in1=ps[:, 0:64], op=ALU.add), w=[ob], r=[ob, ps])
                    ps = k.ps()
                    k.op("pe", lambda e: e.matmul(ps[:, 0:128], lhsT=ksd[0:64, :], rhs=vtb[0:64, :], start=True, stop=True), w=[ps], r=[ksd, vtb])
                    k.op("dve", lambda e: e.scalar_tensor_tensor(out=R1[:], in0=R0[:], scalar=sm["rc"][:, 1:2], in1=ps[:, 0:128], op0=ALU.mult, op1=ALU.add), w=[R1], r=[R0, sm["rc"], ps])
            k.fence = False
            import os
            if sout is not None and os.environ.get("ABCUT3") != "1":
                s_, h_ = sout
                k.dma(g.sd_out[((s_ * 2 + j) * 2 + d) * 4 + h_], S[si % 2][:], g.sd_out, S[si % 2])
                k.dma(g.sr_out[((s_ * 2 + j) * 2 + d) * 4 + h_], R[si % 2][:], g.sr_out, R[si % 2])
        zt = raw[0]
        for t0 in range(0, L, B):
            tl = B
            k.op("act", lambda e: e.activation(out=tmp[:], in_=oa[:, t0:t0 + tl], func=AF.Square), w=[tmp], r=[oa])
            ps = k.ps()
            k.op("pe", lambda e: e.matmul(ps[:, 0:tl], lhsT=g.cstf[:, 2, :], rhs=tmp[:], start=True, stop=True), w=[ps], r=[g.cstf, tmp])
            k.op("act", lambda e: e.activation(out=tmp[:], in_=ps[:, 0:tl], func=AF.Sqrt, bias=EPS), w=[tmp], r=[ps])
            k.op("dve", lambda e: e.reciprocal(out=tmp[:], in_=tmp[:]), w=[tmp], r=[tmp])
            k.op("dve", lambda e: e.tensor_tensor(out=tmp[:], in0=tmp[:], in1=oa[:, t0:t0 + tl], op=ALU.mult), w=[tmp], r=[tmp, oa])
            k.dma(zt[:, 0:tl], P[rows["z"]:rows["z"] + 128, tok0 + t0:tok0 + t0 + tl], zt, P)
            k.op("act", lambda e: e.activation(out=zt[:, 0:tl], in_=zt[:, 0:tl], func=AF.Silu), w=[zt], r=[zt])
            k.op("dve", lambda e: e.scalar_tensor_tensor(out=omix[:, och[0], ooff + t0:ooff + t0 + tl], in0=tmp[:], scalar=g.nab[:, j, 0:1], in1=zt[:, 0:tl],
                                                         op0=ALU.mult, op1=ALU.mult), w=[omix], r=[tmp, g.nab, zt])
            ps = k.ps()
            k.op("pe", lambda e: e.matmul(ps[:, 0:tl], lhsT=g.cstf[:, 2, :], rhs=ob[:, t0:t0 + tl], start=True, stop=True), w=[ps], r=[g.cstf, ob])
            cen = cv[0]
            k.op("dve", lambda e: e.tensor_tensor(out=cen[:, 0:tl], in0=ob[:, t0:t0 + tl], in1=ps[:, 0:tl], op=ALU.subtract), w=[cen], r=[ob, ps])
            k.op("act", lambda e: e.activation(out=tmp[:], in_=cen[:, 0:tl], func=AF.Square), w=[tmp], r=[cen])
            ps = k.ps()
            k.op("pe", lambda e: e.matmul(ps[:, 0:tl], lhsT=g.cstf[:, 2, :], rhs=tmp[:], start=True, stop=True), w=[ps], r=[g.cstf, tmp])
            k.op("act", lambda e: e.activation(out=tmp[:], in_=ps[:, 0:tl], func=AF.Sqrt, bias=EPS), w=[tmp], r=[ps])
            k.op("dve", lambda e: e.reciprocal(out=tmp[:], in_=tmp[:]), w=[tmp], r=[tmp])
            k.op("dve", lambda e: e.tensor_tensor(out=tmp[:], in0=tmp[:], in1=cen[:, 0:tl], op=ALU.mult), w=[tmp], r=[tmp, cen])
            k.dma(zt[:, 0:tl], P[rows["gb"]:rows["gb"] + 128, tok0 + t0:tok0 + t0 + tl], zt, P)
            k.op("act", lambda e: e.activation(out=zt[:, 0:tl], in_=zt[:, 0:tl], func=AF.Silu), w=[zt], r=[zt])
            k.op("dve", lambda e: e.scalar_tensor_tensor(out=omix[:, och[1], ooff + t0:ooff + t0 + tl], in0=tmp[:], scalar=g.nab[:, j, 1:2], in1=zt[:, 0:tl],
                                                         op0=ALU.mult, op1=ALU.mult), w=[omix], r=[tmp, g.nab, zt])


def hy_setup(g, din):
    k = g.k
    g.hy_w_in = din("hy_w_in", [2, 1024, 3072], F32R)
    g.hy_w_in_l = din("hy_w_in_l", [2, 1024, 768], F32R)
    g.hy_w_out = din("hy_w_out", [2, 1024, 1024], F32R)
    g.hy_pc_d = din("hy_pc", [128, 2, 24, 5])
    g.hy_pl_d = din("hy_pl", [128, 2, 6, 5])
    g.hy_fb_d = din("hy_fb", [128, 2, 10])
    g.hy_bo_d = din("hy_bo", [128, 2, 8])
    g.hy_w1_d = din("hy_w1", [33, 2, 64])
    g.hy_w2_d = din("hy_w2", [64, 2, 64])
    g.hy_fp_d = din("hy_fp", [64, 2, 4])
    g.hy_w3c = din("hy_w3c", [64, 2, 2048])
    g.hy_w3l = din("hy_w3l", [64, 2, 512])
    g.hy_zc = din("hy_zc", [33, 256])
    g.hy_zl = din("hy_zl", [33, 4096])
    g.hy_tnc_d = din("hy_tnc", [128, 2, 2])
    g.hy_tnl_d = din("hy_tnl", [128, 32, 2])
    g.hy_dlc = din("hy_dlc", [128, 1024])
    g.hy_dll = din("hy_dll", [128, 256])
    g.hy_wfc_d = din("hy_wfc", [128, 3])
    g.hy_wfl_d = din("hy_wfl", [128, 33])
    g.TCc = din("hy_TCc", [384, 384], F32R)
    g.TSc = din("hy_TSc", [384, 384], F32R)
    g.TCl = din("hy_TCl", [4224, 4224], F32R)
    g.TSl = din("hy_TSl", [4224, 4224], F32R)
    g.DDc = k.dram("DDc", [256, 4096], F32R)
    g.DDl = k.dram("DDl", [4096, 768], F32R)
    g.SPc = k.dram("SPc", [2, 384, 4096], F32)
    g.SPl = k.dram("SPl", [2, 4224, 768], F32)
    g.YSc = k.dram("YSc", [2, 384, 2048], F32R)
    g.YSl = k.dram("YSl", [2, 4224, 256], F32R)
    g.hy_pc = k.sb("hy_pc", [128, 2, 24, 5])
    g.hy_pl = k.sb("hy_pl", [128, 2, 6, 5])
    g.hy_fb = k.sb("hy_fb", [128, 2, 10])
    g.hyb = k.sb("hy_bo", [128, 2, 8])
    g.hy_w1 = k.sb("hy_w1", [33, 2, 64])
    g.hy_w2 = k.sb("hy_w2", [64, 2, 64])
    g.hy_fp = k.sb("hy_fp", [64, 2, 4])
    g.hy_fbb = k.sb("hy_fbb", [64, 2, 2])
    g.hy_tnc = k.sb("hy_tnc", [128, 2, 2])
    g.hy_tnl = k.sb("hy_tnl", [128, 32, 2])
    g.hy_wfc = k.sb("hy_wfc", [128, 3])
    g.hy_wfl = k.sb("hy_wfl", [128, 33])
    for (t, s) in ((g.hy_pc, g.hy_pc_d), (g.hy_pl, g.hy_pl_d), (g.hy_fb, g.hy_fb_d), (g.hyb, g.hy_bo_d), (g.hy_w1, g.hy_w1_d),
                   (g.hy_w2, g.hy_w2_d), (g.hy_fp, g.hy_fp_d), (g.hy_tnc, g.hy_tnc_d), (g.hy_tnl, g.hy_tnl_d),
                   (g.hy_wfc, g.hy_wfc_d), (g.hy_wfl, g.hy_wfl_d)):
        k.dma(t[:], s[:], t, s)

    def post():
        k.op("dve", lambda e: e.tensor_tensor(out=g.hy_fbb[:, :, 0:1], in0=g.hy_fp[:, :, 0:1], in1=g.hy_fp[:, :, 1:2], op=ALU.mult), w=[g.hy_fbb], r=[g.hy_fp])
        k.op("dve", lambda e: e.tensor_tensor(out=g.hy_fbb[:, :, 1:2], in0=g.hy_fp[:, :, 2:3], in1=g.hy_fp[:, :, 3:4], op=ALU.mult), w=[g.hy_fbb], r=[g.hy_fp])
    g.post_setup.append(post)


def hy_conv(g, P, row, tok0, L, t0, B, prm, raw, tmp, out):
    k = g.k
    lo = max(t0 - 1, 0)
    hi = min(t0 + B + 1, L)
    if lo > t0 - 1:
        k.op("dve", lambda e: e.memset(raw[:, 0:1], 0.0), w=[raw])
    if hi < t0 + B + 1:
        k.op("dve", lambda e: e.memset(raw[:, B + 1:B + 2], 0.0), w=[raw])
    k.dma(raw[:, lo - (t0 - 1):hi - (t0 - 1)], P[row:row + 128, tok0 + lo:tok0 + hi], raw, P)
    k.op("dve", lambda e: e.tensor_scalar(out=tmp[:, 0:B], in0=raw[:, 0:B], scalar1=prm[:, 1:2], scalar2=None, op0=ALU.mult), w=[tmp], r=[raw])
    k.op("dve", lambda e: e.scalar_tensor_tensor(out=tmp[:, 0:B], in0=raw[:, 1:B + 1], scalar=prm[:, 2:3], in1=tmp[:, 0:B], op0=ALU.mult, op1=ALU.add), w=[tmp], r=[raw, tmp])
    k.op("dve", lambda e: e.scalar_tensor_tensor(out=tmp[:, 0:B], in0=raw[:, 2:B + 2], scalar=prm[:, 3:4], in1=tmp[:, 0:B], op0=ALU.mult, op1=ALU.add), w=[tmp], r=[raw, tmp])
    k.op("dve", lambda e: e.tensor_scalar(out=out[:, 0:B], in0=tmp[:, 0:B], scalar1=prm[:, 4:5], scalar2=None, op0=ALU.add), w=[out], r=[tmp])


def hy_layer(g, l):
    k = g.k
    j = l // 2
    with k.scope() as es:
        h1 = k.sbs(es, "h1c", [128, 8, TC], F32R)
        stg = [k.sbs(es, f"stg{i}", [128, 512]) for i in range(2)]
        normmod(g, g.xc, TC, 0, 0, h1)
        h1l = k.sbs(es, "h1l", [128, 8, TL], F32R)
        normmod(g, g.xl, TL, 1, 0, h1l)
        h1_exchange(g, h1l)
        cnt = [0]

        def mk_evac(P, tcol, prm):
            def evac(ps, mt, ti, m):
                s = stg[cnt[0] % 2]
                cnt[0] += 1
                k.op("act", lambda e: e.activation(out=s[:], in_=ps[:], func=AF.Identity, bias=prm[:, j, mt, 0:1]), w=[s], r=[ps, g.hy_pc, g.hy_pl])
                k.dma(P[mt * 128:(mt + 1) * 128, tcol:tcol + 512], s[:], P, s)
            return evac

        proj(g, g.hy_w_in, g.hy_w_in[j], 8, 3072, lambda kc, ti: (h1[:, kc, :], h1), [(0, 512)], mk_evac(g.Pc, 0, g.hy_pc))
        hb = h1
        for tb in range(8):
            k.dma(hb[:], g.h1_out[:, tb // 2, :, (tb % 2) * 512:(tb % 2) * 512 + 512].rearrange("k p t -> p k t"), hb, g.h1_out)
            proj(g, g.hy_w_in_l, g.hy_w_in_l[j], 8, 768, lambda kc, ti: (hb[:, kc, :], hb), [(0, 512)], mk_evac(g.Pl, tb * 512, g.hy_pl), cbw=384)
    with k.scope() as es:
        omix = k.sbs(es, "omix", [128, 8, TC], F32R)
        cfg = dict(L=256, nseq=2, nch=8, P=g.Pc, prm=g.hy_pc, fbo=0, zT=g.hy_zc, tn=g.hy_tnc, dl=g.hy_dlc, wf=g.hy_wfc,
                   TC=g.TCc, TS=g.TSc, DD=g.DDc, SP=g.SPc, YS=g.YSc, w3=g.hy_w3c, omix=omix, oseq=256)
        hy_core(g, j, cfg)

        def evac_o(ps, mt, ti, m):
            k.op("act", lambda e: e.activation(out=ps[:, 0:512], in_=ps[:, 0:512], func=AF.Identity, bias=g.hyb[:, j, mt:mt + 1]), w=[ps], r=[ps, g.hyb])
            k.op("dve", lambda e: e.scalar_tensor_tensor(out=g.xc[:, mt, :], in0=ps[:, 0:512], scalar=g.mv[:, 0, 2, mt:mt + 1],
                                                         in1=g.xc[:, mt, :], op0=ALU.mult, op1=ALU.add), w=[g.xc], r=[ps, g.mv, g.xc])
        proj(g, g.hy_w_out, g.hy_w_out[j], 8, 1024, lambda kc, ti: (omix[:, kc, :], omix), [(0, 512)], evac_o)
    with k.scope() as es:
        omixl = k.sbs(es, "omixl", [128, 2, 4096], F32R)
        cfg = dict(L=4096, nseq=1, nch=2, P=g.Pl, prm=g.hy_pl, fbo=8, zT=g.hy_zl, tn=g.hy_tnl, dl=g.hy_dll, wf=g.hy_wfl,
                   TC=g.TCl, TS=g.TSl, DD=g.DDl, SP=g.SPl, YS=g.YSl, w3=g.hy_w3l, omix=omixl, oseq=4096)
        hy_core(g, j, cfg)
        o_exchange(g, omixl)
    mix_out_lat(g, g.hy_w_out, g.hy_w_out[j], g.hyb[:, j, :], kcmap=lambda a, i: 2 * i + a)


def hy_core(g, j, c):
    k = g.k
    L, nseq, nch, P, DD, SP, YS = c["L"], c["nseq"], c["nch"], c["P"], c["DD"], c["SP"], c["YS"]
    ntc = L // 128
    nft = ntc + 1
    W = nch * 128
    ndata = nseq * W
    fcol = ndata
    ncols = ndata + 2 * W
    prm = c["prm"]
    with k.scope() as es:
        zt = k.sbs(es, "zt", [33, L])
        hid1 = k.sbs(es, "hid1", [64, L])
        hid2 = k.sbs(es, "hid2", [64, L])
        msk = k.sbs(es, "msk", [64, 512])
        w3t = k.sbs(es, "w3t", [64, 2 * W])
        dlt = k.sbs(es, "dlt", [128, W])
        win = k.sbs(es, "win", [128, W])
        flt = k.sbs(es, "flt", [128, 2 * W])
        cmb = k.sbs(es, "cmb", [128, 2 * W], F32R)
        k.dma(zt[:], c["zT"][:], zt, c["zT"])
        k.dma(w3t[:], c["w3"][:, j, :], w3t, c["w3"])
        k.dma(dlt[:], c["dl"][:], dlt, c["dl"])
        for (src, srcb, dst, wt, kk, fi) in ((zt, zt, hid1, g.hy_w1, 33, 0), (hid1, hid1, hid2, g.hy_w2, 64, 1)):
            for b0 in range(0, L, 512):
                bl = min(512, L - b0)
                ps = k.ps()
                k.op("pe", lambda e: e.matmul(ps[0:64, 0:bl], lhsT=wt[0:kk, j, :], rhs=src[0:kk, b0:b0 + bl], start=True, stop=True), w=[ps], r=[wt, srcb])
                d_ = dst[:, b0:b0 + bl]
                k.op("act", lambda e: e.activation(out=d_, in_=ps[0:64, 0:bl], func=AF.Identity, scale=g.hy_fp[:, j, 2 * fi:2 * fi + 1],
                                                   bias=g.hy_fbb[:, j, fi:fi + 1]), w=[dst], r=[ps, g.hy_fp, g.hy_fbb])
                for _ in range(2):
                    k.op("dve", lambda e: e.tensor_scalar(out=msk[:, 0:bl], in0=d_, scalar1=math.pi, scalar2=None, op0=ALU.is_gt), w=[msk], r=[dst])
                    k.op("dve", lambda e: e.scalar_tensor_tensor(out=d_, in0=msk[:, 0:bl], scalar=-2.0 * math.pi, in1=d_, op0=ALU.mult, op1=ALU.add), w=[dst], r=[msk, dst])
                    k.op("dve", lambda e: e.tensor_scalar(out=msk[:, 0:bl], in0=d_, scalar1=-math.pi, scalar2=None, op0=ALU.is_lt), w=[msk], r=[dst])
                    k.op("dve", lambda e: e.scalar_tensor_tensor(out=d_, in0=msk[:, 0:bl], scalar=2.0 * math.pi, in1=d_, op0=ALU.mult, op1=ALU.add), w=[dst], r=[msk, dst])
                k.op("act", lambda e: e.activation(out=d_, in_=d_, func=AF.Sin), w=[dst], r=[dst])
        for tc in range(ntc):
            for c0 in range(0, 2 * W, 512):
                ps = k.ps()
                k.op("pe", lambda e: e.matmul(ps[:, 0:512], lhsT=hid2[:, tc * 128:(tc + 1) * 128], rhs=w3t[:, c0:c0 + 512], start=True, stop=True), w=[ps], r=[hid2, w3t])
                k.op("act", lambda e: e.activation(out=flt[:, c0:c0 + 512], in_=ps[:, 0:512], func=AF.Copy), w=[flt], r=[ps])
            k.op("act", lambda e: e.activation(out=win[:], in_=dlt[:], func=AF.Exp, scale=c["tn"][:, tc, 0:1]), w=[win], r=[dlt, c["tn"]])
            k.op("dve", lambda e: e.tensor_tensor(out=flt[:, 0:W], in0=flt[:, 0:W], in1=win[:], op=ALU.mult), w=[flt], r=[flt, win])
            k.op("dve", lambda e: e.scalar_tensor_tensor(out=flt[:, W:2 * W], in0=flt[:, W:2 * W], scalar=c["tn"][:, tc, 1:2], in1=win[:], op0=ALU.mult, op1=ALU.mult),
                 w=[flt], r=[flt, win, c["tn"]])
            k.op("dve", lambda e: e.tensor_tensor(out=cmb[:, 0:W], in0=flt[:, 0:W], in1=flt[:, W:2 * W], op=ALU.add), w=[cmb], r=[flt])
            k.op("dve", lambda e: e.tensor_tensor(out=cmb[:, W:2 * W], in0=flt[:, W:2 * W], in1=flt[:, 0:W], op=ALU.subtract), w=[cmb], r=[flt])
            k.dma(DD[tc * 128:(tc + 1) * 128, fcol:fcol + 2 * W], cmb[:], DD, cmb)
    B = 256
    with k.scope() as es:
        raw = k.sbs(es, "hraw", [128, B + 2])
        tmp = k.sbs(es, "htmp", [128, B])
        x1c = k.sbs(es, "x1c", [128, B])
        vc = k.sbs(es, "vc", [128, B])
        tmr = [k.sbs(es, f"tmr{i}", [128, W], F32R) for i in range(2)]
        for s in range(nseq):
            for t0 in range(0, L, B):
                for ct in range(nch):
                    hy_conv(g, P, (nch + ct) * 128, s * L, L, t0, B, prm[:, j, nch + ct, :], raw, tmp, x1c)
                    hy_conv(g, P, (2 * nch + ct) * 128, s * L, L, t0, B, prm[:, j, 2 * nch + ct, :], raw, tmp, vc)
                    k.op("dve", lambda e: e.tensor_tensor(out=vc[:], in0=vc[:], in1=x1c[:], op=ALU.mult), w=[vc], r=[vc, x1c])
                    for sb in range(2):
                        ps = k.ps()
                        k.op("pe", lambda e: e.matmul(ps[:, 0:128], lhsT=vc[:, sb * 128:(sb + 1) * 128], rhs=g.tab[:, 0, :], start=True, stop=True), w=[ps], r=[vc, g.tab])
                        k.op("act", lambda e: e.activation(out=tmr[sb][:, ct * 128:(ct + 1) * 128], in_=ps[:, 0:128], func=AF.Copy), w=[tmr[sb]], r=[ps])
                for sb in range(2):
                    k.dma(DD[t0 + sb * 128:t0 + (sb + 1) * 128, s * W:(s + 1) * W], tmr[sb][:], DD, tmr[sb])
    with k.scope() as es:
        CG = 256
        ddg = k.sbs(es, "ddg", [128, ntc, CG], F32R)
        tt = k.sbs(es, "tt", [128, ntc, 128], F32R)
        st = [k.sbs(es, f"fst{i}", [128, 512]) for i in range(2)]
        si = 0
        for c0 in range(0, ncols, CG):
            cw = min(CG, ncols - c0)
            k.dma(ddg[:, :, 0:cw], DD[:, c0:c0 + cw].rearrange("(t p) c -> p t c", p=128), ddg, DD)
            for ft in range(nft):
                for Ti, T in enumerate((c["TC"], c["TS"])):
                    k.dma(tt[:], T[0:L, ft * 128:(ft + 1) * 128].rearrange("(t p) f -> p t f", p=128), tt, T)
                    ps = k.ps()
                    for tc in range(ntc):
                        k.op("pe", lambda e, tc=tc: e.matmul(ps[:, 0:cw], lhsT=tt[:, tc, :], rhs=ddg[:, tc, 0:cw], start=(tc == 0), stop=(tc == ntc - 1)), w=[ps], r=[tt, ddg])
                    s_ = st[si % 2]
                    si += 1
                    k.op("act", lambda e: e.activation(out=s_[:, 0:cw], in_=ps[:, 0:cw], func=AF.Copy), w=[s_], r=[ps])
                    k.dma(SP[Ti, ft * 128:(ft + 1) * 128, c0:c0 + cw], s_[:, 0:cw], SP, s_)
    with k.scope() as es:
        uc = k.sbs(es, "uc", [128, W])
        us = k.sbs(es, "us", [128, W])
        kr = k.sbs(es, "kr", [128, W])
        ki = k.sbs(es, "ki", [128, W])
        t1 = k.sbs(es, "yt1", [128, W])
        ya = k.sbs(es, "ya", [128, W], F32R)
        yb = k.sbs(es, "yb", [128, W], F32R)
        for ft in range(nft):
            rs = slice(ft * 128, (ft + 1) * 128)
            k.dma(kr[:], SP[0, rs, fcol:fcol + W], kr, SP)
            k.dma(ki[:], SP[1, rs, fcol + W:fcol + 2 * W], ki, SP)
            for s in range(nseq):
                k.dma(uc[:], SP[0, rs, s * W:(s + 1) * W], uc, SP)
                k.dma(us[:], SP[1, rs, s * W:(s + 1) * W], us, SP)
                wfc = c["wf"][:, ft:ft + 1]
                k.op("dve", lambda e: e.tensor_tensor(out=t1[:], in0=uc[:], in1=kr[:], op=ALU.mult), w=[t1], r=[uc, kr])
                k.op("dve", lambda e: e.tensor_tensor(out=ya[:], in0=us[:], in1=ki[:], op=ALU.mult), w=[ya], r=[us, ki])
                k.op("dve", lambda e: e.tensor_tensor(out=t1[:], in0=t1[:], in1=ya[:].bitcast(F32), op=ALU.add), w=[t1], r=[t1, ya])
                k.op("dve", lambda e: e.tensor_scalar(out=ya[:], in0=t1[:], scalar1=wfc, scalar2=None, op0=ALU.mult), w=[ya], r=[t1, c["wf"]])
                k.op("dve", lambda e: e.tensor_tensor(out=t1[:], in0=us[:], in1=kr[:], op=ALU.mult), w=[t1], r=[us, kr])
                k.op("dve", lambda e: e.tensor_tensor(out=yb[:], in0=uc[:], in1=ki[:], op=ALU.mult), w=[yb], r=[uc, ki])
                k.op("dve", lambda e: e.tensor_tensor(out=t1[:], in0=t1[:], in1=yb[:].bitcast(F32), op=ALU.subtract), w=[t1], r=[t1, yb])
                k.op("dve", lambda e: e.tensor_scalar(out=yb[:], in0=t1[:], scalar1=wfc, scalar2=None, op0=ALU.mult), w=[yb], r=[t1, c["wf"]])
                k.dma(YS[0, rs, s * W:(s + 1) * W], ya[:], YS, ya)
                k.dma(YS[1, rs, s * W:(s + 1) * W], yb[:], YS, yb)
    TB = min(512, L)
    with k.scope() as es:
        YA = k.sbs(es, "YA", [128, ndata], F32R)
        YB = k.sbs(es, "YB", [128, ndata], F32R)
        tcC = k.sbs(es, "tcC", [128, TB], F32R)
        tcS = k.sbs(es, "tcS", [128, TB], F32R)
        raw = k.sbs(es, "iraw", [128, TB + 2])
        tmp = k.sbs(es, "itmp", [128, TB])
        x0c = k.sbs(es, "ix0", [128, TB])
        x1c = k.sbs(es, "ix1", [128, TB])
        vc = k.sbs(es, "ivc", [128, TB])
        units = [(s, ct) for s in range(nseq) for ct in range(nch)]
        for t0 in range(0, L, TB):
            for u0 in range(0, len(units), 4):
                grp = units[u0:u0 + 4]
                pss = [k.ps() for _ in grp]
                for fc in range(nft):
                    k.dma(tcC[:], c["TC"][fc * 128:(fc + 1) * 128, t0:t0 + TB], tcC, c["TC"])
                    k.dma(tcS[:], c["TS"][fc * 128:(fc + 1) * 128, t0:t0 + TB], tcS, c["TS"])
                    k.dma(YA[:], YS[0, fc * 128:(fc + 1) * 128, :], YA, YS)
                    k.dma(YB[:], YS[1, fc * 128:(fc + 1) * 128, :], YB, YS)
                    for (s, ct), ps in zip(grp, pss):
                        col = s * W + ct * 128
                        k.op("pe", lambda e, ps=ps, col=col: e.matmul(ps[:, 0:TB], lhsT=YA[:, col:col + 128], rhs=tcC[:], start=(fc == 0), stop=False), w=[ps], r=[YA, tcC])
                        k.op("pe", lambda e, ps=ps, col=col: e.matmul(ps[:, 0:TB], lhsT=YB[:, col:col + 128], rhs=tcS[:], start=False, stop=(fc == nft - 1)), w=[ps], r=[YB, tcS])
                for (s, ct), ps in zip(grp, pss):
                    hy_conv(g, P, ct * 128, s * L, L, t0, TB, prm[:, j, ct, :], raw, tmp, x0c)
                    hy_conv(g, P, (nch + ct) * 128, s * L, L, t0, TB, prm[:, j, nch + ct, :], raw, tmp, x1c)
                    hy_conv(g, P, (2 * nch + ct) * 128, s * L, L, t0, TB, prm[:, j, 2 * nch + ct, :], raw, tmp, vc)
                    k.op("dve", lambda e: e.tensor_tensor(out=vc[:], in0=vc[:], in1=x1c[:], op=ALU.mult), w=[vc], r=[vc, x1c])
                    k.op("dve", lambda e, ps=ps, ct=ct: e.scalar_tensor_tensor(out=vc[:], in0=vc[:], scalar=g.hy_fb[:, j, c["fbo"] + ct:c["fbo"] + ct + 1], in1=ps[:, 0:TB],
                                                                               op0=ALU.mult, op1=ALU.add), w=[vc], r=[vc, g.hy_fb, ps])
                    oo = c["omix"][:, ct, s * c["oseq"] + t0:s * c["oseq"] + t0 + TB]
                    k.op("dve", lambda e, oo=oo: e.tensor_tensor(out=oo, in0=vc[:], in1=x0c[:], op=ALU.mult), w=[c["omix"]], r=[vc, x0c])


def _hy_tables(L):
    Lp = L + 128
    a = np.arange(Lp, dtype=np.float64)
    ang = 2.0 * np.pi * np.outer(a, a) / (2.0 * L)
    return np.cos(ang).astype(np.float32), np.sin(ang).astype(np.float32)


def _hy_z(L):
    t = np.linspace(0.0, 1.0, L, dtype=np.float32)[:, None]
    wpos = (2.0 * math.pi * np.arange(L, dtype=np.float32)[:, None] / L).astype(np.float32)
    bands = np.linspace(1e-4, 16 - 1, 16, dtype=np.float32)[None, :]
    z = np.concatenate([t, np.cos(bands * wpos), -np.sin(bands * wpos)], axis=-1).astype(np.float32)
    return np.ascontiguousarray(z.T), t[:, 0]


def _hy_host(inp, m, core):
    f32 = np.float32
    if core is None:
        m["hy_w_in"] = np.ascontiguousarray(inp["hy_w_in"], f32)
        m["hy_w_out"] = np.ascontiguousarray(inp["hy_w_out"], f32)
        par = np.stack([inp["hy_b_in"], inp["hy_conv_w"][:, 0], inp["hy_conv_w"][:, 1], inp["hy_conv_w"][:, 2], inp["hy_conv_b"]], 1)
        m["hy_pc"] = np.ascontiguousarray(np.moveaxis(fm(par), 2, 3))
        m["hy_bo"] = fm(inp["hy_b_out"])
        m["hy_w1"] = np.ascontiguousarray(np.moveaxis(np.asarray(inp["hy_f_w1"], f32), 0, 1))
        m["hy_w2"] = np.ascontiguousarray(np.moveaxis(np.asarray(inp["hy_f_w2"], f32), 0, 1))
        fp = np.stack([inp["hy_f_freq1"], inp["hy_f_b1"], inp["hy_f_freq2"], inp["hy_f_b2"]], -1)
        m["hy_fp"] = np.ascontiguousarray(np.moveaxis(np.asarray(fp, f32), 0, 1))
        m["hy_w3c"] = np.ascontiguousarray(np.moveaxis(np.asarray(inp["hy_f_w3"], f32), 0, 1))
        min_decay = math.log(1e-2) / 1.5
        max_decay = math.log(1e-2) / 0.3
        deltas = np.abs(np.linspace(min_decay, max_decay, 1024, dtype=f32))
        m["_deltas"] = deltas
        m["hy_dlc"] = np.ascontiguousarray(np.broadcast_to(deltas[None, :], (128, 1024)), f32)
        for nm, L in (("c", 256), ("l", 4096)):
            zT, t = _hy_z(L)
            m["hy_z" + nm] = zT
            ntc = L // 128
            tn = np.zeros((128, ntc, 2), f32)
            tn[:, :, 0] = -t.reshape(ntc, 128).T
            tn[:, :, 1] = 1.0
            tn[0, 0, 1] = 0.0
            m["hy_tn" + nm] = tn
            nft = ntc + 1
            f = np.arange(nft * 128)
            wf = np.where(f <= L, 2.0, 0.0)
            wf[0] = 1.0
            wf[L] = 1.0
            wf = (wf / (2.0 * L)).astype(f32)
            m["hy_wf" + nm] = np.ascontiguousarray(wf.reshape(nft, 128).T)
            Ct, St = _hy_tables(L)
            m["hy_TC" + nm] = Ct
            m["hy_TS" + nm] = St
        return
    c, gi, r = core
    sl = slice(256 * r, 256 * (r + 1))
    cols = np.r_[np.arange(256 * r, 256 * r + 256), 1024 + np.arange(256 * r, 256 * r + 256), 2048 + np.arange(256 * r, 256 * r + 256)]
    m["hy_w_in_l"] = np.ascontiguousarray(np.asarray(inp["hy_w_in"], f32)[:, :, cols])
    tiles = [2 * r, 2 * r + 1, 8 + 2 * r, 8 + 2 * r + 1, 16 + 2 * r, 16 + 2 * r + 1]
    m["hy_pl"] = np.ascontiguousarray(m["hy_pc"][:, :, tiles, :])
    fb = fm(inp["hy_f_bias"])
    m["hy_fb"] = np.ascontiguousarray(np.concatenate([fb, fb[:, :, 2 * r:2 * r + 2]], -1))
    w3 = np.asarray(inp["hy_f_w3"], f32)
    w3l = np.concatenate([w3[:, :, sl], w3[:, :, 1024 + 256 * r:1024 + 256 * (r + 1)]], -1)
    m["hy_w3l"] = np.ascontiguousarray(np.moveaxis(w3l, 0, 1))
    m["hy_dll"] = np.ascontiguousarray(np.broadcast_to(m["_deltas"][None, sl], (128, 256)), f32)


HY_HOST = _hy_host


MIXERS = 2


def kernel(**inputs):
    inp = {n: np.asarray(v) for n, v in inputs.items()}
    kbld = build(nlayers=4, do_mix=MIXERS, raw=False)
    maps = prep(inp)
    used = {b.name for b in kbld.bufs if getattr(b, "kind", None) == "ExternalInput"}
    maps = [{n: v for n, v in m.items() if n in used} for m in maps]
    res = run_bass_kernel_spmd(kbld.nc, maps, core_ids=list(range(8)))
    yp, ys = assemble(res.results)
    sd, sr = assemble_states(res.results)
    return yp, ys, sd, sr
```

```python
import numpy as np
import contextlib
import concourse.bass as bass
import concourse.mybir as mybir
from concourse.bass_utils import run_bass_kernel_spmd

F32 = mybir.dt.float32
F32R = mybir.dt.float32r
BF16 = mybir.dt.bfloat16
AF = mybir.ActivationFunctionType
ALU = mybir.AluOpType


import os as _os
SERIAL = _os.environ.get("SERIAL", "1") == "1"
FENCE = _os.environ.get("FENCE", "0") == "1"


class Buf:
    def __init__(self, kb, name, t):
        self.kb = kb
        self.name = name
        self.t = t
        self.w = {}
        self.is_dram = False
        self.r = {}
        self.dsem = {}
        self.used = False

    def __getitem__(self, idx):
        return self.t[idx]


class KB:
    def __init__(self):
        self.nc = bass.Bass("TRN2", target_bir_lowering=False)
        nc = self.nc
        self.eng = {"pe": nc.tensor, "act": nc.scalar, "dve": nc.vector, "pool": nc.gpsimd, "sp": nc.sync}
        self.sems = {}
        self.cnt = {}
        for e in self.eng:
            self.sems[e] = nc.alloc_semaphore("sem_" + e)
            self.cnt[e] = 0
        self.sems["cc"] = nc.alloc_semaphore("sem_cc")
        self.cnt["cc"] = 0
        self.seen = {e: {} for e in self.eng}
        self.nbuf = 0
        self.bufs = []
        self.free_dsems = {}
        self.psums = []
        self.pi = 0
        self.dq = 0
        self.group = None
        self.serial = SERIAL
        self.fence = False
        self.ftile = None

    def sb(self, name, shape, dt=F32):
        self.nbuf += 1
        b = Buf(self, name, self.nc.alloc_sbuf_tensor(f"{name}_{self.nbuf}", list(shape), dt))
        self.bufs.append(b)
        return b

    def sbs(self, es, name, shape, dt=F32):
        self.nbuf += 1
        t = es.enter_context(self.nc.sbuf_tensor(f"{name}_{self.nbuf}", list(shape), dt))
        b = Buf(self, name, t)
        self.bufs.append(b)
        return b

    def dram(self, name, shape, dt=F32, kind=None):
        if kind is None:
            t = self.nc.dram_tensor(name, list(shape), dt)
        else:
            t = self.nc.dram_tensor(name, list(shape), dt, kind=kind)
        b = Buf(self, name, t)
        b.is_dram = True
        b.kind = kind
        self.bufs.append(b)
        return b

    def init_psum(self):
        self.ftile = self.nc.alloc_sbuf_tensor("fence_t", [128, 8], F32)
        for i in range(8):
            t = self.nc.alloc_psum_tensor(f"ps{i}", [128, 512], F32)
            b = Buf(self, f"ps{i}", t)
            self.psums.append(b)

    def ps(self):
        b = self.psums[self.pi % 8]
        self.pi += 1
        return b

    def _wait(self, e, key, val):
        if val <= 0:
            return
        if self.seen[e].get(key, 0) >= val:
            return
        self.seen[e][key] = val
        self.eng[e].wait_ge(self.sems[key], val)

    def _deps(self, e, r, w, skipkey=None):
        for b in r:
            for k, v in b.w.items():
                if not (k == e and e == "pe"):
                    self._wait(e, k, v)
        for b in w:
            for k, v in b.w.items():
                if k == skipkey:
                    continue
                if not (k == e and e == "pe"):
                    self._wait(e, k, v)
            for k, v in b.r.items():
                if k == e:
                    continue
                self._wait(e, k, v)

    def op(self, e, fn, w=(), r=()):
        for b in r:
            b.used = True
        self._deps(e, r, w)
        if self.serial:
            for k_, v_ in list(self.cnt.items()):
                if k_ != e:
                    self._wait(e, k_, v_)
        ins = fn(self.eng[e])
        self.cnt[e] += 1
        ins.then_inc(self.sems[e], 1)
        if self.fence and FENCE and e in ("act", "dve"):
            if e == "dve":
                ins2 = self.eng[e].memset(self.ftile[0:1, 0:1], 0.0)
            else:
                ins2 = self.eng[e].activation(out=self.ftile[0:1, 2:3], in_=self.ftile[0:1, 4:5], func=AF.Copy)
            self.cnt[e] += 1
            ins2.then_inc(self.sems[e], 1)
        v = self.cnt[e]
        for b in r:
            b.r[e] = v
        for b in w:
            b.w = {e: v}
            b.r = {}
        return ins

    def _dsem(self, b, q):
        if q not in b.dsem:
            fl = self.free_dsems.setdefault(q, [])
            if fl:
                key = fl.pop()
            else:
                key = f"d{len(self.sems)}"
                self.sems[key] = self.nc.alloc_semaphore(key)
                self.cnt[key] = 0
            b.dsem[q] = key
        return b.dsem[q]

    @contextlib.contextmanager
    def scope(self):
        n0 = len(self.bufs)
        with contextlib.ExitStack() as es:
            yield es
            self.barrier()
            for b in self.bufs[n0:]:
                for q, key in b.dsem.items():
                    self.free_dsems.setdefault(q, []).append(key)
                b.dsem = {}
            del self.bufs[n0:]

    def dma(self, out_ap, in_ap, w, r, q=None, **kw):
        r.used = True
        if q is None:
            q = "pool" if (out_ap.dtype == F32R or in_ap.dtype == F32R) else "sp"
        sb_side = r if (w.is_dram and not r.is_dram) else w
        if self.group is not None and q == "sp":
            if self.group[0] is None:
                gk = f"d{len(self.sems)}"
                self.sems[gk] = self.nc.alloc_semaphore(gk)
                self.cnt[gk] = 0
                self.group[0] = gk
            key = self.group[0]
            self.group[1].append((w, r))
        else:
            key = self._dsem(sb_side, q)
        self._deps(q, [r], [w], skipkey=key)
        self.eng[q].dma_start(out=out_ap, in_=in_ap, **kw).then_inc(self.sems[key], 16)
        self.cnt[key] += 16
        r.r[key] = self.cnt[key]
        if w.is_dram:
            w.w[key] = self.cnt[key]
        else:
            w.w = {key: self.cnt[key]}
        w.r = {}

    def group_begin(self):
        self.group = [None, []]

    def group_end(self):
        key, bl = self.group
        self.group = None
        for (w, r) in bl:
            if w.is_dram:
                w.w[key] = self.cnt[key]
            else:
                w.w = {key: self.cnt[key]}
            r.r[key] = self.cnt[key]

    def allgather(self, ob, ib, groups, out_ap=None, in_ap=None):
        e = "pool"
        self._deps(e, [ib], [ob])
        if in_ap is None:
            in_ap = ib.t.ap()
        if out_ap is None:
            out_ap = ob.t.ap()
        self.eng[e].collective_compute("AllGather", ALU.bypass, replica_groups=groups,
                                       ins=[in_ap.opt()], outs=[out_ap.opt()]).then_inc(self.sems["cc"])
        self.cnt["cc"] += 1
        ib.r["cc"] = self.cnt["cc"]
        ob.w = dict(ob.w)
        ob.w["cc"] = self.cnt["cc"]
        ob.r = {}

    def barrier(self):
        tot = dict(self.cnt)
        for e in self.eng:
            for k, v in tot.items():
                if k != e:
                    self._wait(e, k, v)

    def finish(self):
        self.barrier()


import math
import os
RELAX2 = os.environ.get("RELAX2", "1") == "1"

D = 1024
KC = 8
TC = 512
TL = 1024
TT = TC + TL
DFF = 2816
NF = 22
EPS = 1e-6
GROUPS = [[0, 1, 2, 3], [4, 5, 6, 7]]


class Ctx:
    pass


def build(nlayers=4, do_mix=True, raw=False):
    k = KB()
    nc = k.nc
    k.init_psum()
    g = Ctx()
    g.k = k

    def din(name, shape, dt=F32):
        return k.dram(name, shape, dt, kind="ExternalInput")

    g.xT = din("xT", [8, 128, TT])
    g.cond = din("cond", [128, 16])
    g.mod_w = din("mod_w", [4, 1024, 6144], F32R)
    g.mod_b = din("mod_b", [128, 4, 48])
    g.nrm = din("nrm", [128, 9, 8])
    g.ffn_wg = din("ffn_w_gate", [4, 1024, DFF], F32R)
    g.ffn_wu = din("ffn_w_up", [4, 1024, DFF], F32R)
    g.ffn_wd = din("ffn_w_down", [4, DFF, 1024], F32R)
    g.ffn_cw = din("ffn_cw", [128, 4, NF, 9])
    g.ffn_cb = din("ffn_cb", [128, 4, NF])
    g.cmask = din("cmask", [128, 2])
    g.consts = din("consts", [128, 4, 128])
    g.yT = k.dram("yT", [8, 128, TT], F32, kind="ExternalOutput")
    g.halo_in = k.dram("halo_in", [128, 8, 128], F32R)
    g.halo_out = k.dram("halo_out", [4, 128, 8, 128], F32R)
    import os
    if os.environ.get("BIGSCR"):
        g.big = k.dram("bigscr", [int(os.environ["BIGSCR"]), 1024, 256], F32)
        g.bigt = k.sb("bigt", [128, 256])
        k.dma(g.bigt[:], g.big[0, 0:128, :], g.bigt, g.big)

    g.xc = k.sb("xc", [128, KC, TC])
    g.xl = k.sb("xl", [128, KC, TL])
    g.wb = [k.sb(f"wb{i}", [128, 5632], F32R) for i in range(2)]
    g.wi = 0
    g.cst = k.sb("cst", [128, 4, 128], F32R)
    g.cstf = k.sb("cstf", [128, 4, 128])
    g.sT = k.sb("sT", [128, 8, 2], F32R)
    g.cnd = k.sb("cnd", [128, 16])
    g.modT = k.sb("modT", [128, 48, 2])
    g.modb = k.sb("modb", [128, 4, 48])
    g.nrmt = k.sb("nrmt", [128, 9, 8])
    g.mv = k.sb("mv", [128, 2, 6, 8])
    g.cw = k.sb("cw", [128, 4, NF, 9])
    g.cb = k.sb("cb", [128, 4, NF])
    g.cm = k.sb("cm", [128, 2])

    g.post_setup = []
    k.group_begin()
    if do_mix:
        ab_setup(g, din)
        if do_mix > 1:
            hy_setup(g, din)
    for i in range(8):
        k.dma(g.xc[:, i, :], g.xT[i, :, 0:TC], g.xc, g.xT)
        k.dma(g.xl[:, i, :], g.xT[i, :, TC:TT], g.xl, g.xT)
    k.dma(g.cstf[:], g.consts[:], g.cstf, g.consts)
    k.dma(g.cnd[:], g.cond[:], g.cnd, g.cond)
    k.dma(g.modb[:], g.mod_b[:], g.modb, g.mod_b)
    k.dma(g.nrmt[:], g.nrm[:], g.nrmt, g.nrm)
    k.dma(g.cw[:], g.ffn_cw[:], g.cw, g.ffn_cw)
    k.dma(g.cb[:], g.ffn_cb[:], g.cb, g.ffn_cb)
    k.dma(g.cm[:], g.cmask[:], g.cm, g.cmask)
    k.group_end()
    for f in g.post_setup:
        f()
    k.op("dve", lambda e: e.tensor_copy(out=g.cst[:], in_=g.cstf[:]), w=[g.cst], r=[g.cstf])
    k.op("act", lambda e: e.activation(out=g.sT[:].rearrange("p k j -> p j k"),
                                       in_=g.cnd[:].rearrange("p (j k) -> p j k", j=2), func=AF.Silu),
         w=[g.sT], r=[g.cnd])

    ser = k.serial
    relax = os.environ.get("RELAX", "1") == "1"
    for l in range(nlayers):
        k.serial = ser and not relax
        mod_layer(g, l)
        k.serial = ser
        if do_mix:
            if l % 2 == 0:
                ab_layer(g, l)
            elif do_mix > 1:
                hy_layer(g, l)
        k.serial = ser and not relax
        ffn_layer(g, l)
    final_out(g, raw)
    k.serial = ser
    k.finish()
    return k


def wload(g, Wap, K, cw, src):
    k = g.k
    wb = g.wb[g.wi % 2]
    g.wi += 1
    cwp = ((cw + 127) // 128) * 128
    view = wb[:, 0:K * cwp].rearrange("p (k n) -> p k n", k=K)
    k.dma(view[:, :, 0:cw], Wap.rearrange("(k p) n -> p k n", p=128), wb, src)
    return wb, view


def proj(g, Wsrc, Wap2d, K, ncols, xin, tblocks, evac, cbw=512):
    k = g.k
    for c0 in range(0, ncols, cbw):
        cw = min(cbw, ncols - c0)
        wb, view = wload(g, Wap2d[:, c0:c0 + cw], K, cw, Wsrc)
        nmt = (cw + 127) // 128
        for mt in range(nmt):
            m = min(128, cw - mt * 128)
            for ti, (t0, tl) in enumerate(tblocks):
                ps = k.ps()
                for kc in range(K):
                    ap, buf = xin(kc, ti)
                    k.op("pe", lambda e, kc=kc, ap=ap: e.matmul(ps[:, 0:tl], lhsT=view[:, kc, mt * 128:(mt + 1) * 128],
                                                                 rhs=ap, start=(kc == 0), stop=(kc == K - 1)),
                         w=[ps], r=[wb, buf])
                evac(ps, c0 // 128 + mt, ti, m)


def mod_layer(g, l):
    k = g.k
    ps = k.ps()
    for cb in range(12):
        wb, view = wload(g, g.mod_w[l, :, cb * 512:(cb + 1) * 512], 8, 512, g.mod_w)
        for mt in range(4):
            j = cb * 4 + mt
            for kc in range(8):
                k.op("pe", lambda e, kc=kc, j=j, mt=mt: e.matmul(ps[:, 2 * j:2 * j + 2], lhsT=view[:, kc, mt * 128:(mt + 1) * 128],
                                                                 rhs=g.sT[:, kc, :], start=(kc == 0), stop=(kc == 7)),
                     w=[ps], r=[wb, g.sT])
    for w_ in range(2):
        k.op("dve", lambda e, w_=w_: e.tensor_tensor(out=g.modT[:, :, w_], in0=ps[:, 0:96].rearrange("p (j w) -> p j w", w=2)[:, :, w_],
                                                     in1=g.modb[:, l, :], op=ALU.add), w=[g.modT], r=[ps, g.modb])
    for w_ in range(2):
        for half in range(2):
            nidx = l if half == 0 else 4 + l
            sh = g.modT[:, (3 * half + 0) * 8:(3 * half + 1) * 8, w_]
            sc = g.modT[:, (3 * half + 1) * 8:(3 * half + 2) * 8, w_]
            gt = g.modT[:, (3 * half + 2) * 8:(3 * half + 3) * 8, w_]
            k.op("dve", lambda e, sc=sc, nidx=nidx, half=half, w_=w_: e.scalar_tensor_tensor(
                out=g.mv[:, w_, 3 * half + 0, :], in0=sc, scalar=1.0, in1=g.nrmt[:, nidx, :], op0=ALU.add, op1=ALU.mult),
                 w=[g.mv], r=[g.modT, g.nrmt])
            k.op("dve", lambda e, sh=sh, half=half, w_=w_: e.tensor_copy(out=g.mv[:, w_, 3 * half + 1, :], in_=sh), w=[g.mv], r=[g.modT])
            k.op("dve", lambda e, gt=gt, half=half, w_=w_: e.tensor_copy(out=g.mv[:, w_, 3 * half + 2, :], in_=gt), w=[g.mv], r=[g.modT])


def normmod(g, x, T, which, half, out, tsl=None, ooff=0):
    k = g.k
    with k.scope() as es:
        sq = k.sbs(es, "sq", [128, 8, 512], F32R)
        rstd = k.sbs(es, "rstd", [128, 512])
        tmp = k.sbs(es, "tmp", [128, 512])
        for t0 in range(0, T, 512):
            tl = min(512, T - t0)
            for kc in range(8):
                k.op("act", lambda e, kc=kc: e.activation(out=sq[:, kc, 0:tl], in_=x[:, kc, t0:t0 + tl], func=AF.Square),
                     w=[sq], r=[x])
            ps = k.ps()
            for kc in range(8):
                k.op("pe", lambda e, kc=kc: e.matmul(ps[:, 0:tl], lhsT=g.cst[:, 1, :], rhs=sq[:, kc, 0:tl],
                                                     start=(kc == 0), stop=(kc == 7)), w=[ps], r=[g.cst, sq])
            k.op("act", lambda e: e.activation(out=rstd[:, 0:tl], in_=ps[:, 0:tl], func=AF.Sqrt, bias=EPS), w=[rstd], r=[ps])
            k.op("dve", lambda e: e.reciprocal(out=rstd[:, 0:tl], in_=rstd[:, 0:tl]), w=[rstd], r=[rstd])
            for kc in range(8):
                k.op("dve", lambda e, kc=kc: e.tensor_tensor(out=tmp[:, 0:tl], in0=x[:, kc, t0:t0 + tl], in1=rstd[:, 0:tl], op=ALU.mult),
                     w=[tmp], r=[x, rstd])
                k.op("act", lambda e, kc=kc: e.activation(out=out[:, kc, ooff + t0:ooff + t0 + tl], in_=tmp[:, 0:tl], func=AF.Identity,
                                                          scale=g.mv[:, which, 3 * half + 0, kc:kc + 1],
                                                          bias=g.mv[:, which, 3 * half + 1, kc:kc + 1]),
                     w=[out], r=[tmp, g.mv])


def ffn_layer(g, l):
    k = g.k
    with k.scope() as es:
        h2 = k.sbs(es, "h2", [128, 8, TC], F32R)
        aT = k.sbs(es, "aT", [128, NF, TC], F32R)
        gb = k.sbs(es, "gb", [128, TC])
        acc = k.sbs(es, "acc", [128, TC])
        normmod(g, g.xc, TC, 0, 1, h2)
        ffn_core(g, l, h2, aT, gb, acc, [(0, TC)], grid=False, which=0, x=g.xc, xoff=0)
    with k.scope() as es:
        h2 = k.sbs(es, "h2l", [128, 8, 64 + TL + 64], F32R)
        ed = k.sbs(es, "ed", [128, 8, 128], F32R)
        hp = k.sbs(es, "hp", [128, 8, 128], F32R)
        normmod(g, g.xl, TL, 1, 1, h2, ooff=64)
        k.op("dve", lambda e: e.tensor_copy(out=ed[:, :, 0:64], in_=h2[:, :, 64:128].bitcast(F32)), w=[ed], r=[h2])
        k.op("dve", lambda e: e.tensor_copy(out=ed[:, :, 64:128], in_=h2[:, :, 64 + TL - 64:64 + TL].bitcast(F32)), w=[ed], r=[h2])
        k.dma(g.halo_in[:], ed[:], g.halo_in, ed)
        k.allgather(g.halo_out, g.halo_in, GROUPS)
        pid = nc_pid(g)
        rp = (pid + 3) % 4
        rn = (pid + 1) % 4
        k.dma(hp[:, :, 0:64], g.halo_out[bass.ds(rp, 1), :, :, 64:128].rearrange("o p k t -> p (o k) t"), hp, g.halo_out)
        k.dma(hp[:, :, 64:128], g.halo_out[bass.ds(rn, 1), :, :, 0:64].rearrange("o p k t -> p (o k) t"), hp, g.halo_out)
        k.op("dve", lambda e: e.tensor_scalar(out=h2[:, :, 0:64], in0=hp[:, :, 0:64].bitcast(F32), scalar1=g.cm[:, 0:1], scalar2=None, op0=ALU.mult),
             w=[h2], r=[hp, g.cm])
        k.op("dve", lambda e: e.tensor_scalar(out=h2[:, :, 64 + TL:128 + TL], in0=hp[:, :, 64:128].bitcast(F32), scalar1=g.cm[:, 1:2], scalar2=None, op0=ALU.mult),
             w=[h2], r=[hp, g.cm])
        aT = k.sbs(es, "aTl", [128, NF, 512], F32R)
        gb = k.sbs(es, "gbl", [128, 640])
        acc = k.sbs(es, "accl", [128, 512])
        for hf in range(2):
            ffn_core(g, l, h2, aT, gb, acc, [(hf * 512, 640)], grid=True, which=1, x=g.xl, xoff=hf * 512)


def nc_pid(g):
    if not hasattr(g, "pid"):
        g.pid = g.k.nc.gpsimd.partition_id()
    return g.pid


def ffn_core(g, l, h2, aT, gb, acc, tblk, grid, which, x, xoff):
    k = g.k
    t0, tl = tblk[0]
    nout = 512

    def xin(kc, ti):
        return h2[:, kc, t0:t0 + tl], h2

    blocks = [(0, 512)] if tl == 512 else [(0, 512), (512, tl - 512)]

    def xin2(kc, ti):
        b0, bl = blocks[ti]
        return h2[:, kc, t0 + b0:t0 + b0 + bl], h2

    def evac_gate(ps, mt, ti, m):
        b0, bl = blocks[ti]
        k.op("act", lambda e: e.activation(out=gb[:, b0:b0 + bl], in_=ps[:, 0:bl], func=AF.Copy), w=[gb], r=[ps])
        if ti != len(blocks) - 1:
            return
        cwt = g.cw[:, l, mt, :]
        if not grid:
            gv = gb[:, 0:512].rearrange("p (s t) -> p s t", s=2)
            av = acc[:, 0:512].rearrange("p (s t) -> p s t", s=2)
            k.op("dve", lambda e: e.tensor_scalar(out=acc[:, 0:512], in0=gb[:, 0:512], scalar1=cwt[:, 4:5], scalar2=None, op0=ALU.mult), w=[acc], r=[gb, g.cw])
            k.op("dve", lambda e: e.scalar_tensor_tensor(out=av[:, :, 1:256], in0=gv[:, :, 0:255], scalar=cwt[:, 3:4], in1=av[:, :, 1:256],
                                                         op0=ALU.mult, op1=ALU.add), w=[acc], r=[gb, g.cw, acc])
            k.op("dve", lambda e: e.scalar_tensor_tensor(out=av[:, :, 0:255], in0=gv[:, :, 1:256], scalar=cwt[:, 5:6], in1=av[:, :, 0:255],
                                                         op0=ALU.mult, op1=ALU.add), w=[acc], r=[gb, g.cw, acc])
        else:
            gv = gb[:, 0:640].rearrange("p (r c) -> p r c", c=64)
            av = acc[:, 0:512].rearrange("p (r c) -> p r c", c=64)
            first = True
            eng = "dve"
            for i in range(3):
                for j in (1, 0, 2):
                    tap = cwt[:, 3 * i + j:3 * i + j + 1]
                    if j == 1:
                        o_, i_ = av[:, :, :], gv[:, i:i + 8, :]
                    elif j == 0:
                        o_, i_ = av[:, :, 1:64], gv[:, i:i + 8, 0:63]
                    else:
                        o_, i_ = av[:, :, 0:63], gv[:, i:i + 8, 1:64]
                    if first:
                        k.op(eng, lambda e, o_=o_, i_=i_, tap=tap: e.tensor_scalar(out=o_, in0=i_, scalar1=tap, scalar2=None, op0=ALU.mult),
                             w=[acc], r=[gb, g.cw])
                        first = False
                    else:
                        k.op(eng, lambda e, o_=o_, i_=i_, tap=tap: e.scalar_tensor_tensor(out=o_, in0=i_, scalar=tap, in1=o_, op0=ALU.mult, op1=ALU.add),
                             w=[acc], r=[gb, g.cw, acc])
        k.op("act", lambda e: e.activation(out=aT[:, mt, :], in_=acc[:, 0:512], func=AF.Silu, bias=g.cb[:, l, mt:mt + 1]),
             w=[aT], r=[acc, g.cb])

    proj(g, g.ffn_wg, g.ffn_wg[l], 8, DFF, xin2, blocks, evac_gate)

    uoff = t0 + (64 if grid else 0)

    def xin_up(kc, ti):
        return h2[:, kc, uoff:uoff + 512], h2

    def evac_up(ps, mt, ti, m):
        k.op("dve", lambda e: e.tensor_tensor(out=aT[:, mt, :], in0=aT[:, mt, :].bitcast(F32), in1=ps[:, 0:512], op=ALU.mult), w=[aT], r=[aT, ps])

    proj(g, g.ffn_wu, g.ffn_wu[l], 8, DFF, xin_up, [(0, 512)], evac_up)

    def xin_dn(kc, ti):
        return aT[:, kc, :], aT

    def evac_dn(ps, mt, ti, m):
        k.op("dve", lambda e: e.scalar_tensor_tensor(out=x[:, mt, xoff:xoff + 512], in0=ps[:, 0:512], scalar=g.mv[:, which, 5, mt:mt + 1],
                                                     in1=x[:, mt, xoff:xoff + 512], op0=ALU.mult, op1=ALU.add), w=[x], r=[ps, g.mv, x])

    proj(g, g.ffn_wd, g.ffn_wd[l], NF, 1024, xin_dn, [(0, 512)], evac_dn, cbw=256)


def final_out(g, raw):
    k = g.k
    with k.scope() as es:
        g.mvf = None
        for (x, T, off) in ((g.xc, TC, 0), (g.xl, TL, TC)):
            o = k.sbs(es, "fo", [128, 8, T])
            if raw:
                k.op("dve", lambda e, o=o, x=x: e.tensor_copy(out=o[:], in_=x[:]), w=[o], r=[x])
            else:
                fin_norm(g, x, T, o, es)
            for kc in range(8):
                k.dma(g.yT[kc, :, off:off + T], o[:, kc, :], g.yT, o)


def fin_norm(g, x, T, out, es):
    k = g.k
    sq = k.sbs(es, "sqf", [128, 8, 512], F32R)
    rstd = k.sbs(es, "rstdf", [128, 512])
    for t0 in range(0, T, 512):
        tl = 512
        for kc in range(8):
            k.op("act", lambda e, kc=kc: e.activation(out=sq[:, kc, 0:tl], in_=x[:, kc, t0:t0 + tl], func=AF.Square), w=[sq], r=[x])
        ps = k.ps()
        for kc in range(8):
            k.op("pe", lambda e, kc=kc: e.matmul(ps[:, 0:tl], lhsT=g.cst[:, 1, :], rhs=sq[:, kc, 0:tl], start=(kc == 0), stop=(kc == 7)),
                 w=[ps], r=[g.cst, sq])
        k.op("act", lambda e: e.activation(out=rstd[:, 0:tl], in_=ps[:, 0:tl], func=AF.Sqrt, bias=EPS), w=[rstd], r=[ps])
        k.op("dve", lambda e: e.reciprocal(out=rstd[:, 0:tl], in_=rstd[:, 0:tl]), w=[rstd], r=[rstd])
        for kc in range(8):
            k.op("dve", lambda e, kc=kc: e.scalar_tensor_tensor(out=out[:, kc, t0:t0 + tl], in0=x[:, kc, t0:t0 + tl], scalar=g.nrmt[:, 8, kc:kc + 1],
                                                                in1=rstd[:, 0:tl], op0=ALU.mult, op1=ALU.mult), w=[out], r=[x, g.nrmt, rstd])


def fm(v):
    v = np.asarray(v, np.float32)
    sh = v.shape
    n = sh[-1] // 128
    v = v.reshape(sh[:-1] + (n, 128))
    return np.ascontiguousarray(np.moveaxis(v, -1, 0))


def prep(inp):
    shared = {}
    shared["mod_w"] = np.ascontiguousarray(inp["mod_w"], np.float32)
    shared["mod_b"] = np.ascontiguousarray(fm(inp["mod_b"]))
    nrm = np.concatenate([inp["norm1"], inp["norm2"], inp["final_norm"][None]], 0)
    shared["nrm"] = fm(nrm)
    shared["ffn_w_gate"] = np.ascontiguousarray(inp["ffn_w_gate"], np.float32)
    shared["ffn_w_up"] = np.ascontiguousarray(inp["ffn_w_up"], np.float32)
    shared["ffn_w_down"] = np.ascontiguousarray(inp["ffn_w_down"], np.float32)
    cw = np.asarray(inp["ffn_conv"], np.float32).reshape(4, 9, DFF)
    shared["ffn_cw"] = np.ascontiguousarray(np.moveaxis(fm(cw), 2, 3))
    shared["ffn_cb"] = fm(inp["ffn_conv_b"])
    consts = np.zeros((128, 4, 128), np.float32)
    consts[:, 0, :] = np.eye(128)
    consts[:, 1, :] = 1.0 / 1024
    consts[:, 2, :] = 1.0 / 128
    consts[:, 3, :] = 1.0
    shared["consts"] = consts
    shared["ab_w_in"] = np.ascontiguousarray(inp["ab_w_in"], np.float32)
    shared["ab_w_out"] = np.ascontiguousarray(inp["ab_w_out"], np.float32)
    abc = np.asarray(inp["ab_conv"], np.float32)
    cwc = np.moveaxis(fm(abc), 2, 3)
    shared["ab_cw_c"] = np.ascontiguousarray(cwc)
    shared["ab_nab"] = np.ascontiguousarray(np.stack([inp["ab_norm_a"].T, inp["ab_norm_b"].T], -1), np.float32)
    tab = np.zeros((128, 12, 128), np.float32)
    tab[:, 0, :] = np.eye(128)
    ii = np.arange(64)
    s_, c_ = ii[:, None], ii[None, :]
    tab[:64, 1, :64] = (s_ <= c_); tab[:64, 2, :64] = (s_ >= c_)
    tab[:64, 3, :64] = (s_ > c_); tab[:64, 4, :64] = (s_ < c_)
    tab[:64, 5, :64] = -1.0 * (c_ < s_)
    tab[:64, 6, :64] = -1.0 * (c_ > s_)
    tab[:64, 7, :64] = np.abs(s_ - c_)
    tab[:, 8, :64] = (ii + 1)[None, :]
    tab[:, 9, :64] = (64 - ii)[None, :]
    tab[:64, 10, 0] = 63 - ii
    tab[:64, 10, 1] = ii
    tab[:, 11, :] = 1.0
    shared["abtab"] = tab
    if "hy_w_in" in inp and HY_HOST is not None:
        HY_HOST(inp, shared, None)
    maps = []
    for c in range(8):
        gi, r = c // 4, c % 4
        m = dict(shared)
        wi = np.asarray(inp["ab_w_in"], np.float32)
        cols = []
        for base in (0, 512, 1024, 1536, 2064, 2576, 3088, 3600):
            cols += list(range(base + 128 * r, base + 128 * r + 128))
        cols += [2048 + r, 2052 + r, 2056 + r, 2060 + r]
        m["ab_w_in_l"] = np.ascontiguousarray(wi[:, :, cols])
        m["ab_cw_l"] = np.ascontiguousarray(cwc[:, :, [r, 4 + r, 8 + r], :])
        t5 = np.zeros((128, 2, 5, 2), np.float32)
        rt = np.zeros((128, 2, 5, 2), np.float32)
        for hd in range(5):
            hh = hd if hd < 4 else r
            for d in range(2):
                t5[d, :, hd, 0] = inp["ab_a_log"][:, d, hh]
                t5[d, :, hd, 1] = inp["ab_dt_bias"][:, d, hh]
                rt[:, :, hd, d] = inp["ab_ret_decay"][:, d, hh][None, :]
        m["ab_abc"] = t5
        m["ab_ret"] = rt
        m["sdl_in"] = np.ascontiguousarray(inp["state_delta"][gi, :, :, r], np.float32)
        m["srl_in"] = np.ascontiguousarray(inp["state_ret"][gi, :, :, r], np.float32)
        if "hy_w_in" in inp and HY_HOST is not None:
            HY_HOST(inp, m, (c, gi, r))
            m.pop("_deltas", None)
        xt = np.concatenate([inp["x_prompt"][2 * c].T, inp["x_prompt"][2 * c + 1].T,
                             inp["x_sample"][gi, 1024 * r:1024 * (r + 1)].T], axis=1)
        m["xT"] = np.ascontiguousarray(xt.reshape(8, 128, TT), np.float32)
        cond = np.stack([inp["c_ctx"], inp["c"][gi]], 0)
        m["cond"] = np.ascontiguousarray(fm(cond).reshape(128, 16))
        cm = np.ones((128, 2), np.float32)
        if r == 0:
            cm[:, 0] = 0
        if r == 3:
            cm[:, 1] = 0
        m["cmask"] = cm
        maps.append(m)
    return maps


HY_HOST = None


def assemble_states(res):
    sd = np.zeros((16, 2, 2, 4, 128, 128), np.float32)
    sr = np.zeros((16, 2, 2, 4, 128, 128), np.float32)
    for c in range(8):
        sd[2 * c:2 * c + 2] = res[c]["sd_out"].reshape(2, 2, 2, 4, 128, 128)
        sr[2 * c:2 * c + 2] = res[c]["sr_out"].reshape(2, 2, 2, 4, 128, 128)
    return sd, sr


def assemble(res):
    yp = np.zeros((16, 256, 1024), np.float32)
    ys = np.zeros((2, 4096, 1024), np.float32)
    for c in range(8):
        gi, r = c // 4, c % 4
        yT = res[c]["yT"].reshape(1024, TT)
        yp[2 * c] = yT[:, 0:256].T
        yp[2 * c + 1] = yT[:, 256:512].T
        ys[gi, 1024 * r:1024 * (r + 1)] = yT[:, 512:].T
    return yp, ys


C = 64


def ab_setup(g, din):
    k = g.k
    g.ab_w_in = din("ab_w_in", [2, 1024, 4112], F32R)
    import os
    if os.environ.get("SKIPL") == "1":
        din2 = lambda n, sh, dt=F32: k.dram(n, sh, dt)
    else:
        din2 = din
    g.ab_w_in_l = din2("ab_w_in_l", [2, 1024, 1028], F32R)
    g.ab_w_out = din("ab_w_out", [2, 1024, 1024], F32R)
    g.ab_cw_c = din("ab_cw_c", [128, 2, 12, 3])
    g.ab_cw_l = din("ab_cw_l", [128, 2, 3, 3])
    g.ab_abc = din("ab_abc", [128, 2, 5, 2])
    g.ab_ret = din("ab_ret", [128, 2, 5, 2])
    g.ab_nab = din("ab_nab", [128, 2, 2])
    g.sdl_in = din2("sdl_in", [2, 2, 128, 128])
    g.srl_in = din2("srl_in", [2, 2, 128, 128])
    g.abtab = din("abtab", [128, 12, 128])
    g.sd_out = k.dram("sd_out", [32, 128, 128], F32, kind="ExternalOutput")
    g.sr_out = k.dram("sr_out", [32, 128, 128], F32, kind="ExternalOutput")
    import os
    g.dbg = True if os.environ.get("DBG") == "1" else None
    g.Pc = k.dram("Pc", [4224, TC], F32)
    g.Pl = k.dram("Pl", [1152, 4096], F32)
    g.h1_in = k.dram("h1_in", [8, 128, TL], F32R)
    g.h1_out = k.dram("h1_out", [8, 4, 128, TL], F32R)
    g.o_in = k.dram("o_in", [2, 4, 128, 1024], F32R)
    g.o_out = k.dram("o_out", [2, 4, 4, 128, 1024], F32R)
    g.cwc = k.sb("cwc", [128, 2, 12, 3])
    g.cwl = k.sb("cwl", [128, 2, 3, 3])
    g.abc = k.sb("abc", [128, 2, 5, 2])
    g.nexp = k.sb("nexp", [128, 2, 5, 1])
    g.ret = k.sb("ret", [128, 2, 5, 2])
    g.lgam = k.sb("lgam", [128, 2, 5, 2])
    g.nab = k.sb("nab", [128, 2, 2])
    g.tab = k.sb("tab", [128, 12, 128])
    for (t, s) in ((g.cwc, g.ab_cw_c), (g.cwl, g.ab_cw_l), (g.abc, g.ab_abc), (g.ret, g.ab_ret), (g.nab, g.ab_nab), (g.tab, g.abtab)):
        k.dma(t[:], s[:], t, s)
    g.post_setup.append(lambda: ab_setup2(g))


def ab_setup2(g):
    k = g.k
    k.op("act", lambda e: e.activation(out=g.nexp[:], in_=g.abc[:, :, :, 0:1], func=AF.Exp), w=[g.nexp], r=[g.abc])
    k.op("dve", lambda e: e.tensor_scalar(out=g.nexp[:], in0=g.nexp[:], scalar1=-1.0, scalar2=None, op0=ALU.mult), w=[g.nexp], r=[g.nexp])
    k.op("act", lambda e: e.activation(out=g.lgam[:], in_=g.ret[:], func=AF.Exp), w=[g.lgam], r=[g.ret])
    k.op("dve", lambda e: e.tensor_scalar(out=g.lgam[:], in0=g.lgam[:], scalar1=-1.0, scalar2=None, op0=ALU.mult), w=[g.lgam], r=[g.lgam])


def T_(g, i, p=64, f=64):
    return g.tab[0:p, i, 0:f]


def ab_layer(g, l):
    k = g.k
    j = l // 2
    g.ser_on = k.serial
    if RELAX2:
        k.serial = False
    with k.scope() as es:
        h1 = k.sbs(es, "h1c", [128, 8, TC], F32R)
        stg = [k.sbs(es, f"stg{i}", [128, 512]) for i in range(2)]
        normmod(g, g.xc, TC, 0, 0, h1)
        h1l = k.sbs(es, "h1l", [128, 8, TL], F32R)
        normmod(g, g.xl, TL, 1, 0, h1l)
        h1_exchange(g, h1l)
        cnt = [0]

        def mk_evac(P, tcol):
            def evac(ps, mt, ti, m):
                s = stg[cnt[0] % 2]
                cnt[0] += 1
                k.op("act", lambda e: e.activation(out=s[:], in_=ps[:], func=AF.Copy), w=[s], r=[ps])
                k.dma(P[mt * 128:(mt + 1) * 128, tcol:tcol + 512], s[:], P, s)
            return evac

        proj(g, g.ab_w_in, g.ab_w_in[j], 8, 4112, lambda kc, ti: (h1[:, kc, :], h1), [(0, 512)], mk_evac(g.Pc, 0))
        hb = h1
        import os
        SKIPL = os.environ.get("SKIPL") == "1"
        for tb in (range(0) if SKIPL else range(8)):
            k.dma(hb[:], g.h1_out[:, tb // 2, :, (tb % 2) * 512:(tb % 2) * 512 + 512].rearrange("k p t -> p k t"), hb, g.h1_out)
            proj(g, g.ab_w_in_l, g.ab_w_in_l[j], 8, 1028, lambda kc, ti: (hb[:, kc, :], hb), [(0, 512)], mk_evac(g.Pl, tb * 512), cbw=640)
    import os
    if os.environ.get("ABV") == "1":
        return
    with k.scope() as es:
        omix = k.sbs(es, "omix", [128, 8, TC], F32R)
        for s in range(2):
            for h in range(4):
                rows = dict(qa=128 * h, ka=512 + 128 * h, va=1024 + 128 * h, z=1536 + 128 * h,
                            a=[2048 + h, 2052 + h], b=[2056 + h, 2060 + h],
                            qb=2064 + 128 * h, kb=2576 + 128 * h, vb=3088 + 128 * h, gb=3600 + 128 * h)
                head_pass(g, j, g.Pc, rows, s * 256, 256, 256, h, g.cwc[:, j, :, :], [h, 4 + h, 8 + h], None,
                          (s, h), omix, (h, 4 + h), s * 256)

        def evac_o(ps, mt, ti, m):
            k.op("dve", lambda e: e.scalar_tensor_tensor(out=g.xc[:, mt, :], in0=ps[:, 0:512], scalar=g.mv[:, 0, 2, mt:mt + 1],
                                                         in1=g.xc[:, mt, :], op0=ALU.mult, op1=ALU.add), w=[g.xc], r=[ps, g.mv, g.xc])
        proj(g, g.ab_w_out, g.ab_w_out[j], 8, 1024, lambda kc, ti: (omix[:, kc, :], omix), [(0, 512)], evac_o)
    import os
    if os.environ.get("SKIPL") == "1":
        return
    with k.scope() as es:
        omixl = k.sbs(es, "omixl", [128, 2, 4096], F32R)
        rows = dict(qa=0, ka=128, va=256, z=384, qb=512, kb=640, vb=768, gb=896, a=[1024, 1025], b=[1026, 1027])
        head_pass(g, j, g.Pl, rows, 0, 4096, 256, 4, g.cwl[:, j, :, :], [0, 1, 2], (g.sdl_in, g.srl_in), None, omixl, (0, 1), 0)
        o_exchange(g, omixl)
    mix_out_lat(g, g.ab_w_out, g.ab_w_out[j], None)
    k.serial = g.ser_on


def h1_exchange(g, h1l):
    k = g.k
    k.dma(g.h1_in[:].rearrange("k p t -> p k t"), h1l[:], g.h1_in, h1l)
    for kc in range(8):
        k.allgather(g.h1_out, g.h1_in, GROUPS, out_ap=g.h1_out[kc].rearrange("r p t -> (r p) t"), in_ap=g.h1_in[kc])


def o_exchange(g, omixl):
    k = g.k
    k.dma(g.o_in[:].rearrange("a q p t -> p a q t"), omixl[:].rearrange("p a (q t) -> p a q t", q=4), g.o_in, omixl)
    for a in range(2):
        for q in range(4):
            k.allgather(g.o_out, g.o_in, GROUPS, out_ap=g.o_out[a, q].rearrange("r p t -> (r p) t"), in_ap=g.o_in[a, q])


def mix_out_lat(g, Wsrc, Wap, bias, kcmap=lambda a, i: a * 4 + i):
    k = g.k
    pid = nc_pid(g)
    r4 = pid % 4
    with k.scope() as es:
        om = k.sbs(es, "om", [128, 8, 512], F32R)
        for hf in range(2):
            for i in range(4):
                for a in range(2):
                    k.dma(om[:, kcmap(a, i), :], g.o_out[a, bass.ds(r4, 1), i, :, hf * 512:(hf + 1) * 512].rearrange("o p t -> p (o t)"), om, g.o_out)

            def evac_o(ps, mt, ti, m):
                xs = g.xl[:, mt, hf * 512:(hf + 1) * 512]
                if bias is not None:
                    k.op("act", lambda e: e.activation(out=ps[:, 0:512], in_=ps[:, 0:512], func=AF.Identity, bias=bias[:, mt:mt + 1]),
                         w=[ps], r=[ps, g.hyb])
                k.op("dve", lambda e: e.scalar_tensor_tensor(out=xs, in0=ps[:, 0:512], scalar=g.mv[:, 1, 2, mt:mt + 1],
                                                             in1=xs, op0=ALU.mult, op1=ALU.add), w=[g.xl], r=[ps, g.mv, g.xl])
            proj(g, Wsrc, Wap, 8, 1024, lambda kc, ti: (om[:, kc, :], om), [(0, 512)], evac_o)


def dbg_dump(g, idx, tile):
    k = g.k
    if getattr(g, "dbg", None) is None:
        return
    slots = {0: 8, 1: 9, 4: 10, 5: 11, 6: 12, 8: 13, 9: 14, 10: 15, 11: 24, 12: 25, 13: 26, 14: 27, 15: 28, 16: 29, 17: 30, 18: 31}
    if idx not in slots:
        return
    k.dma(g.sd_out[slots[idx]], tile[:, 0:128], g.sd_out, tile)


def head_pass(g, j, P, rows, tok0, L, B, hd, cwt, cwi, init, sout, omix, och, ooff):
    k = g.k
    nb = L // B
    ncb = B // C
    with k.scope() as es:
        oa = k.sbs(es, "oa", [128, L])
        ob = k.sbs(es, "ob", [128, L])
        raw = [k.sbs(es, f"raw{i}", [128, B + 2]) for i in range(3)]
        cv = [k.sbs(es, f"cv{i}", [128, B]) for i in range(3)]
        rb = [k.sbs(es, f"rb{i}", [128, B]) for i in range(3)]
        AB = k.sbs(es, "AB", [128, B])
        tmp = k.sbs(es, "hp_tmp", [128, B])
        S = [k.sbs(es, f"S{i}", [128, 128]) for i in range(2)]
        R = [k.sbs(es, f"R{i}", [128, 128]) for i in range(2)]
        sm = {n: k.sbs(es, n, [128, 128]) for n in ("tm", "cols", "Gbc", "Grow", "DT", "Dm", "t1", "AT", "XA", "XB", "XTA", "XTB",
                                                   "TT", "kd", "bv", "qd", "rv", "vn", "ATb", "vtb", "ksd", "qcd", "DmTb", "cdrow", "rc")}
        k.op("dve", lambda e: e.memset(AB[:], 0.0), w=[AB])
        if getattr(g, "dbg", None):
            for nm_, t_ in sm.items():
                k.op("dve", lambda e, t_=t_: e.memset(t_[:], 0.0), w=[t_])
        for d in range(2):
            lg = g.lgam[:, j, hd, d:d + 1]
            k.op("act", lambda e: e.activation(out=sm["DmTb"][0:64, 0:64], in_=T_(g, 7), func=AF.Exp, scale=lg[0:64, :]), w=[sm["DmTb"]], r=[g.tab, g.lgam])
            k.op("dve", lambda e: e.scalar_tensor_tensor(out=sm["DmTb"][0:64, 0:64], in0=sm["DmTb"][0:64, 0:64], scalar=128.0 ** -0.5,
                                                         in1=g.tab[0:64, 1 + d, 0:64], op0=ALU.mult, op1=ALU.mult), w=[sm["DmTb"]], r=[g.tab, sm["DmTb"]])
            k.op("act", lambda e: e.activation(out=sm["cdrow"][:, 0:64], in_=g.tab[:, 8 + d, 0:64], func=AF.Exp, scale=lg), w=[sm["cdrow"]], r=[g.tab, g.lgam])
            k.op("dve", lambda e: e.tensor_scalar(out=sm["cdrow"][:, 0:64], in0=sm["cdrow"][:, 0:64], scalar1=128.0 ** -0.5, scalar2=None, op0=ALU.mult),
                 w=[sm["cdrow"]], r=[sm["cdrow"]])
            k.op("act", lambda e: e.activation(out=sm["rc"][0:64, 0:1], in_=g.tab[0:64, 10, d:d + 1], func=AF.Exp, scale=lg[0:64, :]), w=[sm["rc"]], r=[g.tab, g.lgam])
            k.op("act", lambda e: e.activation(out=sm["rc"][:, 1:2], in_=g.tab[:, 11, 0:1], func=AF.Exp, scale=lg, bias=0.0), w=[sm["rc"]], r=[g.tab, g.lgam])
            k.op("dve", lambda e: e.tensor_tensor(out=sm["rc"][:, 1:2], in0=sm["rc"][:, 1:2], in1=sm["rc"][:, 1:2], op=ALU.mult), w=[sm["rc"]], r=[sm["rc"]])
            for _ in range(5):
                k.op("dve", lambda e: e.tensor_tensor(out=sm["rc"][:, 1:2], in0=sm["rc"][:, 1:2], in1=sm["rc"][:, 1:2], op=ALU.mult), w=[sm["rc"]], r=[sm["rc"]])
            si = 0
            if init is None:
                k.op("dve", lambda e: e.memset(S[0][:], 0.0), w=[S[0]])
                k.op("dve", lambda e: e.memset(R[0][:], 0.0), w=[R[0]])
            else:
                k.dma(S[0][:], init[0][j, d], S[0], init[0])
                k.dma(R[0][:], init[1][j, d], R[0], init[1])
            for bi in (range(nb) if d == 0 else reversed(range(nb))):
                t0 = bi * B
                for i, nm in enumerate(("qa", "ka", "va")):
                    lo = max(t0 - 1, 0)
                    hi = min(t0 + B + 1, L)
                    if lo > t0 - 1:
                        k.op("dve", lambda e, i=i: e.memset(raw[i][:, 0:1], 0.0), w=[raw[i]])
                    if hi < t0 + B + 1:
                        k.op("dve", lambda e, i=i: e.memset(raw[i][:, B + 1:B + 2], 0.0), w=[raw[i]])
                    k.dma(raw[i][:, lo - (t0 - 1):hi - (t0 - 1)], P[rows[nm]:rows[nm] + 128, tok0 + lo:tok0 + hi], raw[i], P)
                    w3 = cwt[:, cwi[i], :]
                    k.op("dve", lambda e, i=i, w3=w3: e.tensor_scalar(out=tmp[:], in0=raw[i][:, 0:B], scalar1=w3[:, 0:1], scalar2=None, op0=ALU.mult), w=[tmp], r=[raw[i]])
                    k.op("dve", lambda e, i=i, w3=w3: e.scalar_tensor_tensor(out=tmp[:], in0=raw[i][:, 1:B + 1], scalar=w3[:, 1:2], in1=tmp[:], op0=ALU.mult, op1=ALU.add), w=[tmp], r=[raw[i], tmp])
                    k.op("dve", lambda e, i=i, w3=w3: e.scalar_tensor_tensor(out=tmp[:], in0=raw[i][:, 2:B + 2], scalar=w3[:, 2:3], in1=tmp[:], op0=ALU.mult, op1=ALU.add), w=[tmp], r=[raw[i], tmp])
                    k.op("act", lambda e, i=i: e.activation(out=cv[i][:], in_=tmp[:], func=AF.Silu), w=[cv[i]], r=[tmp])
                    if i < 2:
                        k.op("act", lambda e, i=i: e.activation(out=tmp[:], in_=cv[i][:], func=AF.Square), w=[tmp], r=[cv[i]])
                        ps = k.ps()
                        k.op("pe", lambda e: e.matmul(ps[:, 0:B], lhsT=g.tab[:, 11, :], rhs=tmp[:], start=True, stop=True), w=[ps], r=[g.tab, tmp])
                        k.op("act", lambda e: e.activation(out=tmp[:], in_=ps[:, 0:B], func=AF.Sqrt, bias=EPS), w=[tmp], r=[ps])
                        k.op("dve", lambda e: e.reciprocal(out=tmp[:], in_=tmp[:]), w=[tmp], r=[tmp])
                        if i == 0:
                            k.op("dve", lambda e, i=i: e.scalar_tensor_tensor(out=cv[i][:], in0=cv[i][:], scalar=128.0 ** -0.5, in1=tmp[:], op0=ALU.mult, op1=ALU.mult), w=[cv[i]], r=[cv[i], tmp])
                        else:
                            k.op("dve", lambda e, i=i: e.tensor_tensor(out=cv[i][:], in0=cv[i][:], in1=tmp[:], op=ALU.mult), w=[cv[i]], r=[cv[i], tmp])
                for i, nm in enumerate(("qb", "kb", "vb")):
                    k.dma(rb[i][:], P[rows[nm]:rows[nm] + 128, tok0 + t0:tok0 + t0 + B], rb[i], P)
                import os
                k.serial = g.ser_on
                for dd in (range(0) if os.environ.get("ABCUT2") == "1" else range(2)):
                    k.dma(AB[dd:dd + 1, :], P[rows["a"][dd]:rows["a"][dd] + 1, tok0 + t0:tok0 + t0 + B], AB, P)
                    k.dma(AB[32 + dd:33 + dd, :], P[rows["b"][dd]:rows["b"][dd] + 1, tok0 + t0:tok0 + t0 + B], AB, P)
                k.op("act", lambda e: e.activation(out=AB[0:2, :], in_=AB[0:2, :], func=AF.Exp, bias=g.abc[0:2, j, hd, 1:2]), w=[AB], r=[AB, g.abc])
                k.op("act", lambda e: e.activation(out=AB[0:2, :], in_=AB[0:2, :], func=AF.Ln, bias=1.0), w=[AB], r=[AB])
                k.op("dve", lambda e: e.tensor_scalar(out=AB[0:2, :], in0=AB[0:2, :], scalar1=g.nexp[0:2, j, hd, 0:1], scalar2=None, op0=ALU.mult), w=[AB], r=[AB, g.nexp])
                k.op("act", lambda e: e.activation(out=AB[32:34, :], in_=AB[32:34, :], func=AF.Sigmoid), w=[AB], r=[AB])
                import os
                ABCUT = int(os.environ.get("ABCUT", "0"))
                k.fence = True
                for ci in (range(ncb) if d == 0 else reversed(range(ncb))):
                    if ABCUT == 1:
                        continue
                    k.serial = g.ser_on
                    c0 = ci * C
                    sl = slice(c0, c0 + C)
                    tcol = t0 + c0
                    last = 63 if d == 0 else 0
                    S0, S1 = S[si % 2], S[(si + 1) % 2]
                    R0, R1 = R[si % 2], R[(si + 1) % 2]
                    si += 1
                    tm, cols, Gbc, Grow, DT, Dm, t1, AT = (sm[n] for n in ("tm", "cols", "Gbc", "Grow", "DT", "Dm", "t1", "AT"))
                    ps = k.ps()
                    k.op("pe", lambda e: e.matmul(ps[0:64, 0:34], lhsT=AB[0:34, sl], rhs=g.tab[0:34, 0, 0:34], start=True, stop=True), w=[ps], r=[AB, g.tab])
                    k.op("act", lambda e: e.activation(out=tm[0:64, 0:34], in_=ps[0:64, 0:34], func=AF.Copy), w=[tm], r=[ps])
                    ps = k.ps()
                    k.op("pe", lambda e: e.matmul(ps[0:64, 0:2], lhsT=T_(g, 1 + d), rhs=tm[0:64, 0:2], start=True, stop=True), w=[ps], r=[g.tab, tm])
                    k.op("pe", lambda e: e.matmul(ps[0:64, 2:4], lhsT=T_(g, 3 + d), rhs=tm[0:64, 0:2], start=True, stop=True), w=[ps], r=[g.tab, tm])
                    k.op("dve", lambda e: e.tensor_copy(out=cols[0:64, 0:1], in_=ps[0:64, d:d + 1]), w=[cols], r=[ps])
                    k.op("act", lambda e: e.activation(out=cols[0:64, 1:2], in_=ps[0:64, d:d + 1], func=AF.Exp), w=[cols], r=[ps])
                    k.op("act", lambda e: e.activation(out=cols[0:64, 3:4], in_=ps[0:64, 2 + d:3 + d], func=AF.Exp), w=[cols], r=[ps])
                    k.op("dve", lambda e: e.scalar_tensor_tensor(out=cols[0:64, 2:3], in0=cols[0:64, 1:2], scalar=-1.0, in1=tm[0:64, 32 + d:33 + d], op0=ALU.mult, op1=ALU.mult), w=[cols], r=[cols, tm])
                    k.op("dve", lambda e: e.tensor_tensor(out=Gbc[0:64, :], in0=g.tab[0:64, 11, :], in1=tm[0:64, d:d + 1].to_broadcast([64, 128]), op=ALU.mult), w=[Gbc], r=[g.tab, tm])
                    ps = k.ps()
                    k.op("pe", lambda e: e.matmul(ps[:, 0:64], lhsT=Gbc[0:64, :], rhs=T_(g, 1 + d), start=True, stop=True), w=[ps], r=[Gbc, g.tab])
                    k.op("act", lambda e: e.activation(out=Grow[:, 0:64], in_=ps[:, 0:64], func=AF.Exp), w=[Grow], r=[ps])
                    gcb = cols[0:64, 0:1].to_broadcast([64, 64])
                    k.op("dve", lambda e: e.tensor_tensor(out=DT[0:64, 0:64], in0=ps[0:64, 0:64], in1=gcb, op=ALU.subtract), w=[DT], r=[ps, cols])
                    k.op("dve", lambda e: e.tensor_scalar(out=DT[0:64, 0:64], in0=DT[0:64, 0:64], scalar1=0.0, scalar2=None, op0=ALU.min), w=[DT], r=[DT])
                    k.op("act", lambda e: e.activation(out=DT[0:64, 0:64], in_=DT[0:64, 0:64], func=AF.Exp), w=[DT], r=[DT])
                    k.op("dve", lambda e: e.tensor_tensor(out=Dm[0:64, 0:64], in0=ps[0:64, 0:64], in1=gcb, op=ALU.subtract), w=[Dm], r=[ps, cols])
                    k.op("dve", lambda e: e.tensor_scalar(out=Dm[0:64, 0:64], in0=Dm[0:64, 0:64], scalar1=0.0, scalar2=None, op0=ALU.max), w=[Dm], r=[Dm])
                    k.op("act", lambda e: e.activation(out=Dm[0:64, 0:64], in_=Dm[0:64, 0:64], func=AF.Exp, scale=-1.0), w=[Dm], r=[Dm])
                    qaT, kaT, vaT = cv[0][:, sl], cv[1][:, sl], cv[2][:, sl]
                    psk = k.ps()
                    k.op("pe", lambda e: e.matmul(psk[0:64, 0:64], lhsT=kaT, rhs=kaT, start=True, stop=True), w=[psk], r=[cv[1]])
                    k.op("pe", lambda e: e.matmul(psk[0:64, 64:128], lhsT=kaT, rhs=qaT, start=True, stop=True), w=[psk], r=[cv[1], cv[0]])
                    XA, XB, XTA, XTB, TTt = sm["XA"], sm["XB"], sm["XTA"], sm["XTB"], sm["TT"]
                    k.op("dve", lambda e: e.tensor_tensor(out=t1[0:64, 0:64], in0=psk[0:64, 0:64], in1=Dm[0:64, 0:64], op=ALU.mult), w=[t1], r=[psk, Dm])
                    k.op("dve", lambda e: e.tensor_tensor(out=t1[0:64, 0:64], in0=t1[0:64, 0:64], in1=tm[0:64, 32 + d:33 + d].to_broadcast([64, 64]), op=ALU.mult), w=[t1], r=[t1, tm])
                    k.op("dve", lambda e: e.tensor_tensor(out=XA[0:64, 0:64], in0=t1[0:64, 0:64], in1=T_(g, 5 + d), op=ALU.mult), w=[XA], r=[t1, g.tab])
                    k.op("dve", lambda e: e.tensor_tensor(out=t1[0:64, 64:128], in0=psk[0:64, 64:128], in1=DT[0:64, 0:64], op=ALU.mult), w=[t1], r=[psk, DT])
                    k.op("dve", lambda e: e.tensor_tensor(out=AT[0:64, 0:64], in0=t1[0:64, 64:128], in1=T_(g, 1 + d), op=ALU.mult), w=[AT], r=[t1, g.tab])
                    ps = k.ps()
                    k.op("pe", lambda e: e.matmul(ps[0:64, 0:64], lhsT=XA[0:64, 0:64], rhs=T_(g, 0), start=True, stop=True), w=[ps], r=[XA, g.tab])
                    k.op("act", lambda e: e.activation(out=XTA[0:64, 0:64], in_=ps[0:64, 0:64], func=AF.Copy), w=[XTA], r=[ps])
                    k.op("dve", lambda e: e.tensor_tensor(out=TTt[0:64, 0:64], in0=ps[0:64, 0:64], in1=T_(g, 0), op=ALU.add), w=[TTt], r=[ps, g.tab])
                    DBG = (sout == (0, 0) and d == 0 and ci == 0 and j == 0 and getattr(g, "dbg", None) is not None)
                    if DBG:
                        for ii, nm in enumerate(("tm", "cols", "Grow", "DT", "Dm", "XA", "AT", "XTA")):
                            dbg_dump(g, ii, sm[nm])
                        dbg_dump(g, 14, cv[0]); dbg_dump(g, 15, cv[1]); dbg_dump(g, 16, cv[2]); dbg_dump(g, 17, AB)
                    X, XT, Xn, XTn = XA, XTA, XB, XTB
                    for jj in range(1, 6):
                        ps = k.ps()
                        k.op("pe", lambda e, XT=XT, X=X: e.matmul(ps[0:64, 0:64], lhsT=XT[0:64, 0:64], rhs=X[0:64, 0:64], start=True, stop=True), w=[ps], r=[XT, X])
                        k.op("act", lambda e, Xn=Xn: e.activation(out=Xn[0:64, 0:64], in_=ps[0:64, 0:64], func=AF.Copy), w=[Xn], r=[ps])
                        if jj < 5:
                            ps2 = k.ps()
                            k.op("pe", lambda e, XT=XT, X=X: e.matmul(ps2[0:64, 0:64], lhsT=X[0:64, 0:64], rhs=XT[0:64, 0:64], start=True, stop=True), w=[ps2], r=[XT, X])
                            k.op("act", lambda e, XTn=XTn: e.activation(out=XTn[0:64, 0:64], in_=ps2[0:64, 0:64], func=AF.Copy), w=[XTn], r=[ps2])
                        ps3 = k.ps()
                        k.op("pe", lambda e, Xn=Xn: e.matmul(ps3[0:64, 0:64], lhsT=Xn[0:64, 0:64], rhs=TTt[0:64, 0:64], start=True, stop=True), w=[ps3], r=[Xn, TTt])
                        k.op("dve", lambda e: e.tensor_tensor(out=TTt[0:64, 0:64], in0=TTt[0:64, 0:64], in1=ps3[0:64, 0:64], op=ALU.add), w=[TTt], r=[TTt, ps3])
                        X, XT, Xn, XTn = Xn, XTn, X, XT
                    kd, bv, qd, rv, vn = sm["kd"], sm["bv"], sm["qd"], sm["rv"], sm["vn"]
                    ps = k.ps()
                    k.op("pe", lambda e: e.matmul(ps[0:64, 0:128], lhsT=kaT, rhs=g.tab[:, 0, :], start=True, stop=True), w=[ps], r=[cv[1], g.tab])
                    k.op("dve", lambda e: e.tensor_tensor(out=kd[0:64, :], in0=ps[0:64, 0:128], in1=cols[0:64, 3:4].to_broadcast([64, 128]), op=ALU.mult), w=[kd], r=[ps, cols])
                    ps = k.ps()
                    k.op("pe", lambda e: e.matmul(ps[0:64, 0:128], lhsT=vaT, rhs=g.tab[:, 0, :], start=True, stop=True), w=[ps], r=[cv[2], g.tab])
                    k.op("dve", lambda e: e.tensor_tensor(out=bv[0:64, :], in0=ps[0:64, 0:128], in1=tm[0:64, 32 + d:33 + d].to_broadcast([64, 128]), op=ALU.mult), w=[bv], r=[ps, tm])
                    k.op("dve", lambda e: e.tensor_tensor(out=qd[:, 0:64], in0=qaT, in1=Grow[:, 0:64], op=ALU.mult), w=[qd], r=[cv[0], Grow])
                    ps = k.ps()
                    k.op("pe", lambda e: e.matmul(ps[0:64, 0:128], lhsT=kaT, rhs=S0[:], start=True, stop=True), w=[ps], r=[cv[1], S0])
                    k.op("dve", lambda e: e.tensor_tensor(out=rv[0:64, :], in0=ps[0:64, 0:128], in1=cols[0:64, 2:3].to_broadcast([64, 128]), op=ALU.mult), w=[rv], r=[ps, cols])
                    k.op("dve", lambda e: e.tensor_tensor(out=rv[0:64, :], in0=rv[0:64, :], in1=bv[0:64, :], op=ALU.add), w=[rv], r=[rv, bv])
                    ps = k.ps()
                    k.op("pe", lambda e: e.matmul(ps[0:64, 0:128], lhsT=TTt[0:64, 0:64], rhs=rv[0:64, :], start=True, stop=True), w=[ps], r=[TTt, rv])
                    k.op("act", lambda e: e.activation(out=vn[0:64, :], in_=ps[0:64, 0:128], func=AF.Copy), w=[vn], r=[ps])
                    ps = k.ps()
                    k.op("pe", lambda e: e.matmul(ps[:, 0:64], lhsT=S0[:], rhs=qd[:, 0:64], start=True, stop=False), w=[ps], r=[S0, qd])
                    k.op("pe", lambda e: e.matmul(ps[:, 0:64], lhsT=vn[0:64, :], rhs=AT[0:64, 0:64], start=False, stop=True), w=[ps], r=[vn, AT])
                    if d == 0:
                        k.op("act", lambda e: e.activation(out=oa[:, tcol:tcol + C], in_=ps[:, 0:64], func=AF.Copy), w=[oa], r=[ps])
                    else:
                        k.op("dve", lambda e: e.tensor_tensor(out=oa[:, tcol:tcol + C], in0=oa[:, tcol:tcol + C], in1=ps[:, 0:64], op=ALU.add), w=[oa], r=[oa, ps])
                    ps = k.ps()
                    k.op("pe", lambda e: e.matmul(ps[:, 0:128], lhsT=kd[0:64, :], rhs=vn[0:64, :], start=True, stop=True), w=[ps], r=[kd, vn])
                    k.op("dve", lambda e: e.tensor_tensor(out=S1[:], in0=S0[:], in1=Grow[:, last:last + 1].to_broadcast([128, 128]), op=ALU.mult), w=[S1], r=[S0, Grow])
                    k.op("dve", lambda e: e.tensor_tensor(out=S1[:], in0=S1[:], in1=ps[:, 0:128], op=ALU.add), w=[S1], r=[S1, ps])
                    if DBG:
                        for ii, nm in enumerate(("TT", "kd", "bv", "qd", "rv", "vn")):
                            dbg_dump(g, 8 + ii, sm[nm])
                        dbg_dump(g, 18, S1)
                    if RELAX2:
                        k.serial = False
                    qbT, kbT, vbT = rb[0][:, sl], rb[1][:, sl], rb[2][:, sl]
                    ATb, vtb, ksd, qcd = sm["ATb"], sm["vtb"], sm["ksd"], sm["qcd"]
                    ps = k.ps()
                    k.op("pe", lambda e: e.matmul(ps[0:64, 0:64], lhsT=kbT, rhs=qbT, start=True, stop=True), w=[ps], r=[rb[1], rb[0]])
                    k.op("dve", lambda e: e.tensor_tensor(out=ATb[0:64, 0:64], in0=ps[0:64, 0:64], in1=sm["DmTb"][0:64, 0:64], op=ALU.mult), w=[ATb], r=[ps, sm["DmTb"]])
                    ps = k.ps()
                    k.op("pe", lambda e: e.matmul(ps[0:64, 0:128], lhsT=vbT, rhs=g.tab[:, 0, :], start=True, stop=True), w=[ps], r=[rb[2], g.tab])
                    k.op("act", lambda e: e.activation(out=vtb[0:64, :], in_=ps[0:64, 0:128], func=AF.Copy), w=[vtb], r=[ps])
                    ps = k.ps()
                    k.op("pe", lambda e: e.matmul(ps[0:64, 0:128], lhsT=kbT, rhs=g.tab[:, 0, :], start=True, stop=True), w=[ps], r=[rb[1], g.tab])
                    k.op("dve", lambda e: e.tensor_scalar(out=ksd[0:64, :], in0=ps[0:64, 0:128], scalar1=sm["rc"][0:64, 0:1], scalar2=None, op0=ALU.mult), w=[ksd], r=[ps, sm["rc"]])
                    k.op("dve", lambda e: e.tensor_tensor(out=qcd[:, 0:64], in0=qbT, in1=sm["cdrow"][:, 0:64], op=ALU.mult), w=[qcd], r=[rb[0], sm["cdrow"]])
                    ps = k.ps()
                    k.op("pe", lambda e: e.matmul(ps[:, 0:64], lhsT=R0[:], rhs=qcd[:, 0:64], start=True, stop=False), w=[ps], r=[R0, qcd])
                    k.op("pe", lambda e: e.matmul(ps[:, 0:64], lhsT=vtb[0:64, :], rhs=ATb[0:64, 0:64], start=False, stop=True), w=[ps], r=[vtb, ATb])
                    if d == 0:
                        k.op("act", lambda e: e.activation(out=ob[:, tcol:tcol + C], in_=ps[:, 0:64], func=AF.Copy), w=[ob], r=[ps])
                    else:
                        k.op("dve", lambda e: e.tensor_tensor(out=ob[:, tcol:tcol + C], in0=ob[:, tcol:tcol + C], in1=ps[:, 0:64], op=ALU.add), w=[ob], r=[ob, ps])
                    ps = k.ps()
                    k.op("pe", lambda e: e.matmul(ps[:, 0:128], lhsT=ksd[0:64, :], rhs=vtb[0:64, :], start=True, stop=True), w=[ps], r=[ksd, vtb])
                    k.op("dve", lambda e: e.scalar_tensor_tensor(out=R1[:], in0=R0[:], scalar=sm["rc"][:, 1:2], in1=ps[:, 0:128], op0=ALU.mult, op1=ALU.add), w=[R1], r=[R0, sm["rc"], ps])
            k.fence = False
            if RELAX2:
                k.serial = False
            import os
            if sout is not None and os.environ.get("ABCUT3") != "1":
                s_, h_ = sout
                k.dma(g.sd_out[((s_ * 2 + j) * 2 + d) * 4 + h_], S[si % 2][:], g.sd_out, S[si % 2])
                k.dma(g.sr_out[((s_ * 2 + j) * 2 + d) * 4 + h_], R[si % 2][:], g.sr_out, R[si % 2])
        zt = raw[0]
        for t0 in range(0, L, B):
            tl = B
            k.op("act", lambda e: e.activation(out=tmp[:], in_=oa[:, t0:t0 + tl], func=AF.Square), w=[tmp], r=[oa])
            ps = k.ps()
            k.op("pe", lambda e: e.matmul(ps[:, 0:tl], lhsT=g.cstf[:, 2, :], rhs=tmp[:], start=True, stop=True), w=[ps], r=[g.cstf, tmp])
            k.op("act", lambda e: e.activation(out=tmp[:], in_=ps[:, 0:tl], func=AF.Sqrt, bias=EPS), w=[tmp], r=[ps])
            k.op("dve", lambda e: e.reciprocal(out=tmp[:], in_=tmp[:]), w=[tmp], r=[tmp])
            k.op("dve", lambda e: e.tensor_tensor(out=tmp[:], in0=tmp[:], in1=oa[:, t0:t0 + tl], op=ALU.mult), w=[tmp], r=[tmp, oa])
            k.dma(zt[:, 0:tl], P[rows["z"]:rows["z"] + 128, tok0 + t0:tok0 + t0 + tl], zt, P)
            k.op("act", lambda e: e.activation(out=zt[:, 0:tl], in_=zt[:, 0:tl], func=AF.Silu), w=[zt], r=[zt])
            k.op("dve", lambda e: e.scalar_tensor_tensor(out=omix[:, och[0], ooff + t0:ooff + t0 + tl], in0=tmp[:], scalar=g.nab[:, j, 0:1], in1=zt[:, 0:tl],
                                                         op0=ALU.mult, op1=ALU.mult), w=[omix], r=[tmp, g.nab, zt])
            ps = k.ps()
            k.op("pe", lambda e: e.matmul(ps[:, 0:tl], lhsT=g.cstf[:, 2, :], rhs=ob[:, t0:t0 + tl], start=True, stop=True), w=[ps], r=[g.cstf, ob])
            cen = cv[0]
            k.op("dve", lambda e: e.tensor_tensor(out=cen[:, 0:tl], in0=ob[:, t0:t0 + tl], in1=ps[:, 0:tl], op=ALU.subtract), w=[cen], r=[ob, ps])
            k.op("act", lambda e: e.activation(out=tmp[:], in_=cen[:, 0:tl], func=AF.Square), w=[tmp], r=[cen])
            ps = k.ps()
            k.op("pe", lambda e: e.matmul(ps[:, 0:tl], lhsT=g.cstf[:, 2, :], rhs=tmp[:], start=True, stop=True), w=[ps], r=[g.cstf, tmp])
            k.op("act", lambda e: e.activation(out=tmp[:], in_=ps[:, 0:tl], func=AF.Sqrt, bias=EPS), w=[tmp], r=[ps])
            k.op("dve", lambda e: e.reciprocal(out=tmp[:], in_=tmp[:]), w=[tmp], r=[tmp])
            k.op("dve", lambda e: e.tensor_tensor(out=tmp[:], in0=tmp[:], in1=cen[:, 0:tl], op=ALU.mult), w=[tmp], r=[tmp, cen])
            k.dma(zt[:, 0:tl], P[rows["gb"]:rows["gb"] + 128, tok0 + t0:tok0 + t0 + tl], zt, P)
            k.op("act", lambda e: e.activation(out=zt[:, 0:tl], in_=zt[:, 0:tl], func=AF.Silu), w=[zt], r=[zt])
            k.op("dve", lambda e: e.scalar_tensor_tensor(out=omix[:, och[1], ooff + t0:ooff + t0 + tl], in0=tmp[:], scalar=g.nab[:, j, 1:2], in1=zt[:, 0:tl],
                                                         op0=ALU.mult, op1=ALU.mult), w=[omix], r=[tmp, g.nab, zt])


def hy_setup(g, din):
    k = g.k
    g.hy_w_in = din("hy_w_in", [2, 1024, 3072], F32R)
    g.hy_w_in_l = din("hy_w_in_l", [2, 1024, 768], F32R)
    g.hy_w_out = din("hy_w_out", [2, 1024, 1024], F32R)
    g.hy_pc_d = din("hy_pc", [128, 2, 24, 5])
    g.hy_pl_d = din("hy_pl", [128, 2, 6, 5])
    g.hy_fb_d = din("hy_fb", [128, 2, 10])
    g.hy_bo_d = din("hy_bo", [128, 2, 8])
    g.hy_w1_d = din("hy_w1", [33, 2, 64])
    g.hy_w2_d = din("hy_w2", [64, 2, 64])
    g.hy_fp_d = din("hy_fp", [64, 2, 4])
    g.hy_w3c = din("hy_w3c", [64, 2, 2048])
    g.hy_w3l = din("hy_w3l", [64, 2, 512])
    g.hy_zc = din("hy_zc", [33, 256])
    g.hy_zl = din("hy_zl", [33, 4096])
    g.hy_tnc_d = din("hy_tnc", [128, 2, 2])
    g.hy_tnl_d = din("hy_tnl", [128, 32, 2])
    g.hy_dlc = din("hy_dlc", [128, 1024])
    g.hy_dll = din("hy_dll", [128, 256])
    g.hy_wfc_d = din("hy_wfc", [128, 3])
    g.hy_wfl_d = din("hy_wfl", [128, 33])
    g.TCc = din("hy_TCc", [384, 384], F32R)
    g.TSc = din("hy_TSc", [384, 384], F32R)
    g.TCl = din("hy_TCl", [4224, 4224], F32R)
    g.TSl = din("hy_TSl", [4224, 4224], F32R)
    g.DDc = k.dram("DDc", [256, 4096], F32R)
    g.DDl = k.dram("DDl", [4096, 768], F32R)
    g.SPc = k.dram("SPc", [2, 384, 4096], F32)
    g.SPl = k.dram("SPl", [2, 4224, 768], F32)
    g.YSc = k.dram("YSc", [2, 384, 2048], F32R)
    g.YSl = k.dram("YSl", [2, 4224, 256], F32R)
    g.hy_pc = k.sb("hy_pc", [128, 2, 24, 5])
    g.hy_pl = k.sb("hy_pl", [128, 2, 6, 5])
    g.hy_fb = k.sb("hy_fb", [128, 2, 10])
    g.hyb = k.sb("hy_bo", [128, 2, 8])
    g.hy_w1 = k.sb("hy_w1", [33, 2, 64])
    g.hy_w2 = k.sb("hy_w2", [64, 2, 64])
    g.hy_fp = k.sb("hy_fp", [64, 2, 4])
    g.hy_fbb = k.sb("hy_fbb", [64, 2, 2])
    g.hy_tnc = k.sb("hy_tnc", [128, 2, 2])
    g.hy_tnl = k.sb("hy_tnl", [128, 32, 2])
    g.hy_wfc = k.sb("hy_wfc", [128, 3])
    g.hy_wfl = k.sb("hy_wfl", [128, 33])
    for (t, s) in ((g.hy_pc, g.hy_pc_d), (g.hy_pl, g.hy_pl_d), (g.hy_fb, g.hy_fb_d), (g.hyb, g.hy_bo_d), (g.hy_w1, g.hy_w1_d),
                   (g.hy_w2, g.hy_w2_d), (g.hy_fp, g.hy_fp_d), (g.hy_tnc, g.hy_tnc_d), (g.hy_tnl, g.hy_tnl_d),
                   (g.hy_wfc, g.hy_wfc_d), (g.hy_wfl, g.hy_wfl_d)):
        k.dma(t[:], s[:], t, s)

    def post():
        k.op("dve", lambda e: e.tensor_tensor(out=g.hy_fbb[:, :, 0:1], in0=g.hy_fp[:, :, 0:1], in1=g.hy_fp[:, :, 1:2], op=ALU.mult), w=[g.hy_fbb], r=[g.hy_fp])
        k.op("dve", lambda e: e.tensor_tensor(out=g.hy_fbb[:, :, 1:2], in0=g.hy_fp[:, :, 2:3], in1=g.hy_fp[:, :, 3:4], op=ALU.mult), w=[g.hy_fbb], r=[g.hy_fp])
    g.post_setup.append(post)


def hy_conv(g, P, row, tok0, L, t0, B, prm, raw, tmp, out):
    k = g.k
    lo = max(t0 - 1, 0)
    hi = min(t0 + B + 1, L)
    if lo > t0 - 1:
        k.op("dve", lambda e: e.memset(raw[:, 0:1], 0.0), w=[raw])
    if hi < t0 + B + 1:
        k.op("dve", lambda e: e.memset(raw[:, B + 1:B + 2], 0.0), w=[raw])
    k.dma(raw[:, lo - (t0 - 1):hi - (t0 - 1)], P[row:row + 128, tok0 + lo:tok0 + hi], raw, P)
    k.op("dve", lambda e: e.tensor_scalar(out=tmp[:, 0:B], in0=raw[:, 0:B], scalar1=prm[:, 1:2], scalar2=None, op0=ALU.mult), w=[tmp], r=[raw])
    k.op("dve", lambda e: e.scalar_tensor_tensor(out=tmp[:, 0:B], in0=raw[:, 1:B + 1], scalar=prm[:, 2:3], in1=tmp[:, 0:B], op0=ALU.mult, op1=ALU.add), w=[tmp], r=[raw, tmp])
    k.op("dve", lambda e: e.scalar_tensor_tensor(out=tmp[:, 0:B], in0=raw[:, 2:B + 2], scalar=prm[:, 3:4], in1=tmp[:, 0:B], op0=ALU.mult, op1=ALU.add), w=[tmp], r=[raw, tmp])
    k.op("dve", lambda e: e.tensor_scalar(out=out[:, 0:B], in0=tmp[:, 0:B], scalar1=prm[:, 4:5], scalar2=None, op0=ALU.add), w=[out], r=[tmp])


def hy_layer(g, l):
    k = g.k
    j = l // 2
    g.ser_on = k.serial
    if RELAX2:
        k.serial = False
    with k.scope() as es:
        h1 = k.sbs(es, "h1c", [128, 8, TC], F32R)
        stg = [k.sbs(es, f"stg{i}", [128, 512]) for i in range(2)]
        normmod(g, g.xc, TC, 0, 0, h1)
        h1l = k.sbs(es, "h1l", [128, 8, TL], F32R)
        normmod(g, g.xl, TL, 1, 0, h1l)
        h1_exchange(g, h1l)
        cnt = [0]

        def mk_evac(P, tcol, prm):
            def evac(ps, mt, ti, m):
                s = stg[cnt[0] % 2]
                cnt[0] += 1
                k.op("act", lambda e: e.activation(out=s[:], in_=ps[:], func=AF.Identity, bias=prm[:, j, mt, 0:1]), w=[s], r=[ps, g.hy_pc, g.hy_pl])
                k.dma(P[mt * 128:(mt + 1) * 128, tcol:tcol + 512], s[:], P, s)
            return evac

        proj(g, g.hy_w_in, g.hy_w_in[j], 8, 3072, lambda kc, ti: (h1[:, kc, :], h1), [(0, 512)], mk_evac(g.Pc, 0, g.hy_pc))
        hb = h1
        for tb in range(8):
            k.dma(hb[:], g.h1_out[:, tb // 2, :, (tb % 2) * 512:(tb % 2) * 512 + 512].rearrange("k p t -> p k t"), hb, g.h1_out)
            proj(g, g.hy_w_in_l, g.hy_w_in_l[j], 8, 768, lambda kc, ti: (hb[:, kc, :], hb), [(0, 512)], mk_evac(g.Pl, tb * 512, g.hy_pl), cbw=384)
    with k.scope() as es:
        omix = k.sbs(es, "omix", [128, 8, TC], F32R)
        cfg = dict(L=256, nseq=2, nch=8, P=g.Pc, prm=g.hy_pc, fbo=0, zT=g.hy_zc, tn=g.hy_tnc, dl=g.hy_dlc, wf=g.hy_wfc,
                   TC=g.TCc, TS=g.TSc, DD=g.DDc, SP=g.SPc, YS=g.YSc, w3=g.hy_w3c, omix=omix, oseq=256)
        hy_core(g, j, cfg)

        def evac_o(ps, mt, ti, m):
            k.op("act", lambda e: e.activation(out=ps[:, 0:512], in_=ps[:, 0:512], func=AF.Identity, bias=g.hyb[:, j, mt:mt + 1]), w=[ps], r=[ps, g.hyb])
            k.op("dve", lambda e: e.scalar_tensor_tensor(out=g.xc[:, mt, :], in0=ps[:, 0:512], scalar=g.mv[:, 0, 2, mt:mt + 1],
                                                         in1=g.xc[:, mt, :], op0=ALU.mult, op1=ALU.add), w=[g.xc], r=[ps, g.mv, g.xc])
        proj(g, g.hy_w_out, g.hy_w_out[j], 8, 1024, lambda kc, ti: (omix[:, kc, :], omix), [(0, 512)], evac_o)
    with k.scope() as es:
        cfg = dict(L=4096, nseq=1, nch=2, P=g.Pl, prm=g.hy_pl, fbo=8, zT=g.hy_zl, tn=g.hy_tnl, dl=g.hy_dll, wf=g.hy_wfl,
                   TC=g.TCl, TS=g.TSl, DD=g.DDl, SP=g.SPl, YS=g.YSl, w3=g.hy_w3l, omix=None, oseq=4096,
                   mk_omix=lambda: k.sbs(es, "omixl", [128, 2, 4096], F32R))
        hy_core(g, j, cfg)
        o_exchange(g, cfg["omix"])
    mix_out_lat(g, g.hy_w_out, g.hy_w_out[j], g.hyb[:, j, :], kcmap=lambda a, i: 2 * i + a)
    k.serial = g.ser_on


def hy_core(g, j, c):
    k = g.k
    L, nseq, nch, P, DD, SP, YS = c["L"], c["nseq"], c["nch"], c["P"], c["DD"], c["SP"], c["YS"]
    ntc = L // 128
    nft = ntc + 1
    W = nch * 128
    ndata = nseq * W
    dcol = W
    c2col = W + ndata
    ncols = ndata + 2 * W
    prm = c["prm"]
    k.serial = g.ser_on
    with k.scope() as es:
        zt = k.sbs(es, "zt", [33, L])
        hid1 = k.sbs(es, "hid1", [64, L])
        hid2 = k.sbs(es, "hid2", [64, L])
        msk = k.sbs(es, "msk", [64, 512])
        w3t = k.sbs(es, "w3t", [64, 2 * W])
        dlt = k.sbs(es, "dlt", [128, W])
        win = k.sbs(es, "win", [128, W])
        flt = k.sbs(es, "flt", [128, 2 * W])
        cmb = k.sbs(es, "cmb", [128, 2 * W], F32R)
        k.dma(zt[:], c["zT"][:], zt, c["zT"])
        k.dma(w3t[:], c["w3"][:, j, :], w3t, c["w3"])
        k.dma(dlt[:], c["dl"][:], dlt, c["dl"])
        for (src, srcb, dst, wt, kk, fi) in ((zt, zt, hid1, g.hy_w1, 33, 0), (hid1, hid1, hid2, g.hy_w2, 64, 1)):
            for b0 in range(0, L, 512):
                bl = min(512, L - b0)
                ps = k.ps()
                k.op("pe", lambda e: e.matmul(ps[0:64, 0:bl], lhsT=wt[0:kk, j, :], rhs=src[0:kk, b0:b0 + bl], start=True, stop=True), w=[ps], r=[wt, srcb])
                d_ = dst[:, b0:b0 + bl]
                k.op("act", lambda e: e.activation(out=d_, in_=ps[0:64, 0:bl], func=AF.Identity, scale=g.hy_fp[:, j, 2 * fi:2 * fi + 1],
                                                   bias=g.hy_fbb[:, j, fi:fi + 1]), w=[dst], r=[ps, g.hy_fp, g.hy_fbb])
                for _ in range(2):
                    k.op("dve", lambda e: e.tensor_scalar(out=msk[:, 0:bl], in0=d_, scalar1=math.pi, scalar2=None, op0=ALU.is_gt), w=[msk], r=[dst])
                    k.op("dve", lambda e: e.scalar_tensor_tensor(out=d_, in0=msk[:, 0:bl], scalar=-2.0 * math.pi, in1=d_, op0=ALU.mult, op1=ALU.add), w=[dst], r=[msk, dst])
                    k.op("dve", lambda e: e.tensor_scalar(out=msk[:, 0:bl], in0=d_, scalar1=-math.pi, scalar2=None, op0=ALU.is_lt), w=[msk], r=[dst])
                    k.op("dve", lambda e: e.scalar_tensor_tensor(out=d_, in0=msk[:, 0:bl], scalar=2.0 * math.pi, in1=d_, op0=ALU.mult, op1=ALU.add), w=[dst], r=[msk, dst])
                k.op("act", lambda e: e.activation(out=d_, in_=d_, func=AF.Sin), w=[dst], r=[dst])
        for tc in range(ntc):
            for c0 in range(0, 2 * W, 512):
                ps = k.ps()
                k.op("pe", lambda e: e.matmul(ps[:, 0:512], lhsT=hid2[:, tc * 128:(tc + 1) * 128], rhs=w3t[:, c0:c0 + 512], start=True, stop=True), w=[ps], r=[hid2, w3t])
                k.op("act", lambda e: e.activation(out=flt[:, c0:c0 + 512], in_=ps[:, 0:512], func=AF.Copy), w=[flt], r=[ps])
            k.op("act", lambda e: e.activation(out=win[:], in_=dlt[:], func=AF.Exp, scale=c["tn"][:, tc, 0:1]), w=[win], r=[dlt, c["tn"]])
            k.op("dve", lambda e: e.tensor_tensor(out=flt[:, 0:W], in0=flt[:, 0:W], in1=win[:], op=ALU.mult), w=[flt], r=[flt, win])
            k.op("dve", lambda e: e.scalar_tensor_tensor(out=flt[:, W:2 * W], in0=flt[:, W:2 * W], scalar=c["tn"][:, tc, 1:2], in1=win[:], op0=ALU.mult, op1=ALU.mult),
                 w=[flt], r=[flt, win, c["tn"]])
            k.op("dve", lambda e: e.tensor_tensor(out=cmb[:, 0:W], in0=flt[:, 0:W], in1=flt[:, W:2 * W], op=ALU.add), w=[cmb], r=[flt])
            k.op("dve", lambda e: e.tensor_tensor(out=cmb[:, W:2 * W], in0=flt[:, W:2 * W], in1=flt[:, 0:W], op=ALU.subtract), w=[cmb], r=[flt])
            k.dma(DD[tc * 128:(tc + 1) * 128, 0:W], cmb[:, 0:W], DD, cmb)
            k.dma(DD[tc * 128:(tc + 1) * 128, c2col:c2col + W], cmb[:, W:2 * W], DD, cmb)
    if RELAX2:
        k.serial = False
    B = 256
    with k.scope() as es:
        raw = k.sbs(es, "hraw", [128, B + 2])
        tmp = k.sbs(es, "htmp", [128, B])
        x1c = k.sbs(es, "x1c", [128, B])
        vc = k.sbs(es, "vc", [128, B])
        tmr = [k.sbs(es, f"tmr{i}", [128, W], F32R) for i in range(2)]
        for s in range(nseq):
            for t0 in range(0, L, B):
                for ct in range(nch):
                    hy_conv(g, P, (nch + ct) * 128, s * L, L, t0, B, prm[:, j, nch + ct, :], raw, tmp, x1c)
                    hy_conv(g, P, (2 * nch + ct) * 128, s * L, L, t0, B, prm[:, j, 2 * nch + ct, :], raw, tmp, vc)
                    k.op("dve", lambda e: e.tensor_tensor(out=vc[:], in0=vc[:], in1=x1c[:], op=ALU.mult), w=[vc], r=[vc, x1c])
                    for sb in range(2):
                        ps = k.ps()
                        k.op("pe", lambda e: e.matmul(ps[:, 0:128], lhsT=vc[:, sb * 128:(sb + 1) * 128], rhs=g.tab[:, 0, :], start=True, stop=True), w=[ps], r=[vc, g.tab])
                        k.op("act", lambda e: e.activation(out=tmr[sb][:, ct * 128:(ct + 1) * 128], in_=ps[:, 0:128], func=AF.Copy), w=[tmr[sb]], r=[ps])
                for sb in range(2):
                    k.dma(DD[t0 + sb * 128:t0 + (sb + 1) * 128, dcol + s * W:dcol + (s + 1) * W], tmr[sb][:], DD, tmr[sb])
    with k.scope() as es:
        CG = 512
        ddg = k.sbs(es, "ddg", [128, ntc, CG], F32R)
        tt = k.sbs(es, "tt", [128, ntc, 128], F32R)
        st = [k.sbs(es, f"fst{i}", [128, 512]) for i in range(2)]
        si = 0
        for Ti, (T, lo, hi) in enumerate(((c["TC"], 0, W + ndata), (c["TS"], W, ncols))):
            for c0 in range(lo, hi, CG):
                cw = min(CG, hi - c0)
                k.dma(ddg[:, :, 0:cw], DD[:, c0:c0 + cw].rearrange("(t p) c -> p t c", p=128), ddg, DD)
                for ft in range(nft):
                    k.dma(tt[:], T[0:L, ft * 128:(ft + 1) * 128].rearrange("(t p) f -> p t f", p=128), tt, T)
                    ps = k.ps()
                    for tc in range(ntc):
                        k.op("pe", lambda e, tc=tc: e.matmul(ps[:, 0:cw], lhsT=tt[:, tc, :], rhs=ddg[:, tc, 0:cw], start=(tc == 0), stop=(tc == ntc - 1)), w=[ps], r=[tt, ddg])
                    s_ = st[si % 2]
                    si += 1
                    k.op("act", lambda e: e.activation(out=s_[:, 0:cw], in_=ps[:, 0:cw], func=AF.Copy), w=[s_], r=[ps])
                    k.dma(SP[Ti, ft * 128:(ft + 1) * 128, c0:c0 + cw], s_[:, 0:cw], SP, s_)
    with k.scope() as es:
        uc = k.sbs(es, "uc", [128, W])
        us = k.sbs(es, "us", [128, W])
        kr = k.sbs(es, "kr", [128, W])
        ki = k.sbs(es, "ki", [128, W])
        t1 = k.sbs(es, "yt1", [128, W])
        ya = k.sbs(es, "ya", [128, W], F32R)
        yb = k.sbs(es, "yb", [128, W], F32R)
        for ft in range(nft):
            rs = slice(ft * 128, (ft + 1) * 128)
            k.dma(kr[:], SP[0, rs, 0:W], kr, SP)
            k.dma(ki[:], SP[1, rs, c2col:c2col + W], ki, SP)
            for s in range(nseq):
                k.dma(uc[:], SP[0, rs, dcol + s * W:dcol + (s + 1) * W], uc, SP)
                k.dma(us[:], SP[1, rs, dcol + s * W:dcol + (s + 1) * W], us, SP)
                wfc = c["wf"][:, ft:ft + 1]
                k.op("dve", lambda e: e.tensor_tensor(out=t1[:], in0=uc[:], in1=kr[:], op=ALU.mult), w=[t1], r=[uc, kr])
                k.op("dve", lambda e: e.tensor_tensor(out=ya[:], in0=us[:], in1=ki[:], op=ALU.mult), w=[ya], r=[us, ki])
                k.op("dve", lambda e: e.tensor_tensor(out=t1[:], in0=t1[:], in1=ya[:].bitcast(F32), op=ALU.add), w=[t1], r=[t1, ya])
                k.op("dve", lambda e: e.tensor_scalar(out=ya[:], in0=t1[:], scalar1=wfc, scalar2=None, op0=ALU.mult), w=[ya], r=[t1, c["wf"]])
                k.op("dve", lambda e: e.tensor_tensor(out=t1[:], in0=us[:], in1=kr[:], op=ALU.mult), w=[t1], r=[us, kr])
                k.op("dve", lambda e: e.tensor_tensor(out=yb[:], in0=uc[:], in1=ki[:], op=ALU.mult), w=[yb], r=[uc, ki])
                k.op("dve", lambda e: e.tensor_tensor(out=t1[:], in0=t1[:], in1=yb[:].bitcast(F32), op=ALU.subtract), w=[t1], r=[t1, yb])
                k.op("dve", lambda e: e.tensor_scalar(out=yb[:], in0=t1[:], scalar1=wfc, scalar2=None, op0=ALU.mult), w=[yb], r=[t1, c["wf"]])
                k.dma(YS[0, rs, s * W:(s + 1) * W], ya[:], YS, ya)
                k.dma(YS[1, rs, s * W:(s + 1) * W], yb[:], YS, yb)
    if c.get("omix") is None:
        c["omix"] = c["mk_omix"]()
    TB = min(512, L)
    with k.scope() as es:
        YAs = [k.sbs(es, f"YA{i}", [128, ndata], F32R) for i in range(2)]
        YBs = [k.sbs(es, f"YB{i}", [128, ndata], F32R) for i in range(2)]
        tcCs = [k.sbs(es, f"tcC{i}", [128, TB], F32R) for i in range(2)]
        tcSs = [k.sbs(es, f"tcS{i}", [128, TB], F32R) for i in range(2)]
        raw = k.sbs(es, "iraw", [128, TB + 2])
        tmp = k.sbs(es, "itmp", [128, TB])
        x0c = k.sbs(es, "ix0", [128, TB])
        x1c = k.sbs(es, "ix1", [128, TB])
        vc = k.sbs(es, "ivc", [128, TB])
        units = [(s, ct) for s in range(nseq) for ct in range(nch)]
        for t0 in range(0, L, TB):
            for u0 in range(0, len(units), 4):
                grp = units[u0:u0 + 4]
                pss = [k.ps() for _ in grp]
                for fc in range(nft):
                    YA, YB, tcC, tcS = YAs[fc % 2], YBs[fc % 2], tcCs[fc % 2], tcSs[fc % 2]
                    k.dma(tcC[:], c["TC"][fc * 128:(fc + 1) * 128, t0:t0 + TB], tcC, c["TC"])
                    k.dma(tcS[:], c["TS"][fc * 128:(fc + 1) * 128, t0:t0 + TB], tcS, c["TS"])
                    k.dma(YA[:], YS[0, fc * 128:(fc + 1) * 128, :], YA, YS)
                    k.dma(YB[:], YS[1, fc * 128:(fc + 1) * 128, :], YB, YS)
                    for (s, ct), ps in zip(grp, pss):
                        col = s * W + ct * 128
                        k.op("pe", lambda e, ps=ps, col=col: e.matmul(ps[:, 0:TB], lhsT=YA[:, col:col + 128], rhs=tcC[:], start=(fc == 0), stop=False), w=[ps], r=[YA, tcC])
                        k.op("pe", lambda e, ps=ps, col=col: e.matmul(ps[:, 0:TB], lhsT=YB[:, col:col + 128], rhs=tcS[:], start=False, stop=(fc == nft - 1)), w=[ps], r=[YB, tcS])
                for (s, ct), ps in zip(grp, pss):
                    hy_conv(g, P, ct * 128, s * L, L, t0, TB, prm[:, j, ct, :], raw, tmp, x0c)
                    hy_conv(g, P, (nch + ct) * 128, s * L, L, t0, TB, prm[:, j, nch + ct, :], raw, tmp, x1c)
                    hy_conv(g, P, (2 * nch + ct) * 128, s * L, L, t0, TB, prm[:, j, 2 * nch + ct, :], raw, tmp, vc)
                    k.op("dve", lambda e: e.tensor_tensor(out=vc[:], in0=vc[:], in1=x1c[:], op=ALU.mult), w=[vc], r=[vc, x1c])
                    k.op("dve", lambda e, ps=ps, ct=ct: e.scalar_tensor_tensor(out=vc[:], in0=vc[:], scalar=g.hy_fb[:, j, c["fbo"] + ct:c["fbo"] + ct + 1], in1=ps[:, 0:TB],
                                                                               op0=ALU.mult, op1=ALU.add), w=[vc], r=[vc, g.hy_fb, ps])
                    oo = c["omix"][:, ct, s * c["oseq"] + t0:s * c["oseq"] + t0 + TB]
                    k.op("dve", lambda e, oo=oo: e.tensor_tensor(out=oo, in0=vc[:], in1=x0c[:], op=ALU.mult), w=[c["omix"]], r=[vc, x0c])


def _hy_tables(L):
    Lp = L + 128
    a = np.arange(Lp, dtype=np.float64)
    ang = 2.0 * np.pi * np.outer(a, a) / (2.0 * L)
    return np.cos(ang).astype(np.float32), np.sin(ang).astype(np.float32)


def _hy_z(L):
    t = np.linspace(0.0, 1.0, L, dtype=np.float32)[:, None]
    wpos = (2.0 * math.pi * np.arange(L, dtype=np.float32)[:, None] / L).astype(np.float32)
    bands = np.linspace(1e-4, 16 - 1, 16, dtype=np.float32)[None, :]
    z = np.concatenate([t, np.cos(bands * wpos), -np.sin(bands * wpos)], axis=-1).astype(np.float32)
    return np.ascontiguousarray(z.T), t[:, 0]


def _hy_host(inp, m, core):
    f32 = np.float32
    if core is None:
        m["hy_w_in"] = np.ascontiguousarray(inp["hy_w_in"], f32)
        m["hy_w_out"] = np.ascontiguousarray(inp["hy_w_out"], f32)
        par = np.stack([inp["hy_b_in"], inp["hy_conv_w"][:, 0], inp["hy_conv_w"][:, 1], inp["hy_conv_w"][:, 2], inp["hy_conv_b"]], 1)
        m["hy_pc"] = np.ascontiguousarray(np.moveaxis(fm(par), 2, 3))
        m["hy_bo"] = fm(inp["hy_b_out"])
        m["hy_w1"] = np.ascontiguousarray(np.moveaxis(np.asarray(inp["hy_f_w1"], f32), 0, 1))
        m["hy_w2"] = np.ascontiguousarray(np.moveaxis(np.asarray(inp["hy_f_w2"], f32), 0, 1))
        fp = np.stack([inp["hy_f_freq1"], inp["hy_f_b1"], inp["hy_f_freq2"], inp["hy_f_b2"]], -1)
        m["hy_fp"] = np.ascontiguousarray(np.moveaxis(np.asarray(fp, f32), 0, 1))
        m["hy_w3c"] = np.ascontiguousarray(np.moveaxis(np.asarray(inp["hy_f_w3"], f32), 0, 1))
        min_decay = math.log(1e-2) / 1.5
        max_decay = math.log(1e-2) / 0.3
        deltas = np.abs(np.linspace(min_decay, max_decay, 1024, dtype=f32))
        m["_deltas"] = deltas
        m["hy_dlc"] = np.ascontiguousarray(np.broadcast_to(deltas[None, :], (128, 1024)), f32)
        for nm, L in (("c", 256), ("l", 4096)):
            zT, t = _hy_z(L)
            m["hy_z" + nm] = zT
            ntc = L // 128
            tn = np.zeros((128, ntc, 2), f32)
            tn[:, :, 0] = -t.reshape(ntc, 128).T
            tn[:, :, 1] = 1.0
            tn[0, 0, 1] = 0.0
            m["hy_tn" + nm] = tn
            nft = ntc + 1
            f = np.arange(nft * 128)
            wf = np.where(f <= L, 2.0, 0.0)
            wf[0] = 1.0
            wf[L] = 1.0
            wf = (wf / (2.0 * L)).astype(f32)
            m["hy_wf" + nm] = np.ascontiguousarray(wf.reshape(nft, 128).T)
            Ct, St = _hy_tables(L)
            m["hy_TC" + nm] = Ct
            m["hy_TS" + nm] = St
        return
    c, gi, r = core
    sl = slice(256 * r, 256 * (r + 1))
    cols = np.r_[np.arange(256 * r, 256 * r + 256), 1024 + np.arange(256 * r, 256 * r + 256), 2048 + np.arange(256 * r, 256 * r + 256)]
    m["hy_w_in_l"] = np.ascontiguousarray(np.asarray(inp["hy_w_in"], f32)[:, :, cols])
    tiles = [2 * r, 2 * r + 1, 8 + 2 * r, 8 + 2 * r + 1, 16 + 2 * r, 16 + 2 * r + 1]
    m["hy_pl"] = np.ascontiguousarray(m["hy_pc"][:, :, tiles, :])
    fb = fm(inp["hy_f_bias"])
    m["hy_fb"] = np.ascontiguousarray(np.concatenate([fb, fb[:, :, 2 * r:2 * r + 2]], -1))
    w3 = np.asarray(inp["hy_f_w3"], f32)
    w3l = np.concatenate([w3[:, :, sl], w3[:, :, 1024 + 256 * r:1024 + 256 * (r + 1)]], -1)
    m["hy_w3l"] = np.ascontiguousarray(np.moveaxis(w3l, 0, 1))
    m["hy_dll"] = np.ascontiguousarray(np.broadcast_to(m["_deltas"][None, sl], (128, 256)), f32)


HY_HOST = _hy_host


MIXERS = 2


def kernel(**inputs):
    inp = {n: np.asarray(v) for n, v in inputs.items()}
    kbld = build(nlayers=4, do_mix=MIXERS, raw=False)
    maps = prep(inp)
    used = {b.name for b in kbld.bufs if getattr(b, "kind", None) == "ExternalInput"}
    maps = [{n: v for n, v in m.items() if n in used} for m in maps]
    res = run_bass_kernel_spmd(kbld.nc, maps, core_ids=list(range(8)))
    yp, ys = assemble(res.results)
    sd, sr = assemble_states(res.results)
    return yp, ys, sd, sr
```
